# Optimizing a Trainium2 kernel written in Bass

```python
import math
import jax, jax.numpy as jnp
from jax import lax
import numpy as np

D_MODEL = 1024
BATCH = 4
SEQ = 4096
DEPTH = 1
DEC_BATCH = 32
DEC_SEQ = 1
PAST_LEN = 8192
PAGE_SIZE = 128

DIFF_HEADS = 8
DIFF_DH = 64
DIFF_DV = 2 * DIFF_DH
DIFF_WIDTH = DIFF_HEADS * DIFF_DV
Q_BLOCK = 128
MLSTM_HEADS = 4
MLSTM_DK = 128
MLSTM_DV = 256
MLSTM_WIDTH = MLSTM_HEADS * MLSTM_DV
MLSTM_CHUNK = 64
PEER_HEADS = 8
PEER_NKEYS = 128
PEER_EXPERTS = PEER_NKEYS * PEER_NKEYS
PEER_DKEY = 256
PEER_HALF = PEER_DKEY // 2
PEER_TOPK = 16
PEER_BLOCK = 128

RMS_EPS = 1e-6
NEG_INF = -1e30

IN_SIZES = (DIFF_HEADS * 2 * DIFF_DH, DIFF_HEADS * 2 * DIFF_DH, DIFF_WIDTH,
            MLSTM_HEADS * MLSTM_DK, MLSTM_HEADS * MLSTM_DK, MLSTM_WIDTH,
            MLSTM_WIDTH, MLSTM_HEADS, MLSTM_HEADS,
            2 * D_MODEL)
IN_SPLITS = tuple(int(s) for s in np.cumsum(IN_SIZES)[:-1])
IN_COLS = int(sum(IN_SIZES))

kernel_name = "diffattn_mlstm_peer_hybrid_step"


def rmsnorm(x, g):
    xf = x.astype(jnp.float32)
    y = xf * lax.rsqrt(jnp.mean(xf * xf, axis=-1, keepdims=True) + RMS_EPS)
    return (y * g.astype(jnp.float32)).astype(x.dtype)


def alibi_slopes():
    return jnp.exp2(-8.0 * jnp.arange(1, DIFF_HEADS + 1, dtype=jnp.float32) / DIFF_HEADS)


def diff_lambda(lq1, lk1, lq2, lk2, lam_init):
    f = jnp.float32
    return (jnp.exp(jnp.sum(lq1.astype(f) * lk1.astype(f)))
            - jnp.exp(jnp.sum(lq2.astype(f) * lk2.astype(f))) + lam_init)


def diff_attend(q, k, v, q_pos, k_pos, lam):
    s = jnp.einsum('bqhcd,bkhcd->bhcqk', q, k).astype(jnp.float32) * (DIFF_DH ** -0.5)
    dist = (q_pos[:, None] - k_pos[None, :]).astype(jnp.float32)
    bias = -alibi_slopes()[:, None, None] * dist
    causal = q_pos[:, None] >= k_pos[None, :]
    s = jnp.where(causal, s + bias[None, :, None], NEG_INF)
    p = jax.nn.softmax(s, axis=-1)
    a = p[:, :, 0] - lam * p[:, :, 1]
    return jnp.einsum('bhqk,bkhd->bqhd', a.astype(v.dtype), v)


def mlstm_chunk(carry, inp):
    C0, n0, m0 = carry
    q, k, v, ig, lf = inp
    L = q.shape[2]
    F = jnp.cumsum(lf, axis=-1)
    causal = jnp.tril(jnp.ones((L, L), dtype=bool))
    logw = jnp.where(causal, F[..., :, None] - F[..., None, :] + ig[..., None, :], NEG_INF)
    inter = m0[..., None] + F
    m = jnp.maximum(inter, jnp.max(logw, axis=-1))
    w = jnp.where(causal, jnp.exp(logw - m[..., None]), 0.0)
    a_inter = jnp.exp(inter - m)
    qk = jnp.einsum('bhtd,bhsd->bhts', q, k) * w
    num = (a_inter[..., None] * jnp.einsum('bhtd,bhde->bhte', q, C0)
           + jnp.einsum('bhts,bhse->bhte', qk, v))
    den = a_inter * jnp.einsum('bhtd,bhd->bht', q, n0) + jnp.sum(qk, axis=-1)
    h = num / jnp.maximum(jnp.abs(den), jnp.exp(-m))[..., None]
    m_end = m[..., -1]
    w_end = jnp.exp(F[..., -1:] - F + ig - m_end[..., None])
    decay = jnp.exp(inter[..., -1] - m_end)
    C = decay[..., None, None] * C0 + jnp.einsum('bhs,bhsd,bhse->bhde', w_end, k, v)
    n = decay[..., None] * n0 + jnp.einsum('bhs,bhsd->bhd', w_end, k)
    return (C, n, m_end), h


def mlstm_scan(q, k, v, ig, lf, C0, n0, m0):
    B, T = q.shape[:2]
    L = MLSTM_CHUNK if T % MLSTM_CHUNK == 0 else T
    nc = T // L

    def chunks(a):
        a = a.reshape((B, nc, L) + a.shape[2:])
        return jnp.moveaxis(jnp.moveaxis(a, 1, 0), 3, 2)

    (C, n, m), h = lax.scan(mlstm_chunk, (C0, n0, m0),
                            (chunks(q), chunks(k), chunks(v), chunks(ig), chunks(lf)))
    h = jnp.moveaxis(jnp.moveaxis(h, 0, 1), 2, 3).reshape(B, T, MLSTM_HEADS, MLSTM_DV)
    return h, C, n, m


def mixer_inputs(x, norm_g, w_in, q_norm_g, k_norm_g, b_i, b_f):
    B, T, _ = x.shape
    z = rmsnorm(x, norm_g) @ w_in
    dq, dk, dv, mq, mk, mv, mo, mi, mf, gates = jnp.split(z, IN_SPLITS, axis=-1)
    dq = rmsnorm(dq.reshape(B, T, DIFF_HEADS, 2, DIFF_DH), q_norm_g)
    dk = rmsnorm(dk.reshape(B, T, DIFF_HEADS, 2, DIFF_DH), k_norm_g)
    dv = dv.reshape(B, T, DIFF_HEADS, DIFF_DV)
    mq = mq.reshape(B, T, MLSTM_HEADS, MLSTM_DK).astype(jnp.float32)
    mk = mk.reshape(B, T, MLSTM_HEADS, MLSTM_DK).astype(jnp.float32) * (MLSTM_DK ** -0.5)
    mv = mv.reshape(B, T, MLSTM_HEADS, MLSTM_DV).astype(jnp.float32)
    ig = (mi + b_i).astype(jnp.float32)
    lf = jax.nn.log_sigmoid((mf + b_f).astype(jnp.float32))
    return dq, dk, dv, mq, mk, mv, jax.nn.sigmoid(mo), ig, lf, gates


def merge_branches(x, o_diff, h_mlstm, o_gate, gates, subln_g, mnorm_g, w_a, w_b, w_out, lam_init):
    B, T, _ = x.shape
    a = (rmsnorm(o_diff, subln_g) * (1.0 - lam_init)).reshape(B, T, DIFF_WIDTH)
    hb = rmsnorm(h_mlstm.astype(x.dtype), mnorm_g).reshape(B, T, MLSTM_WIDTH) * o_gate
    g_a, g_b = jnp.split(jax.nn.sigmoid(gates), 2, axis=-1)
    y = (g_a * (a @ w_a) + g_b * (hb @ w_b)) @ w_out
    return x + y


def peer(xn, wq, subkeys, u, v):
    n = xn.shape[0]
    blk = min(PEER_BLOCK, n)
    n_pad = -(-n // blk) * blk
    xp = jnp.pad(xn, ((0, n_pad - n), (0, 0))).reshape(n_pad // blk, blk, D_MODEL)

    def one(xb):
        q = (xb @ wq).reshape(blk, PEER_HEADS, 2, PEER_HALF)
        s = jnp.einsum('nhcd,ckd->nhck', q, subkeys).astype(jnp.float32)
        s1, i1 = lax.top_k(s[:, :, 0], PEER_TOPK)
        s2, i2 = lax.top_k(s[:, :, 1], PEER_TOPK)
        cand = (s1[..., :, None] + s2[..., None, :]).reshape(blk, PEER_HEADS, PEER_TOPK * PEER_TOPK)
        cidx = (i1[..., :, None] * PEER_NKEYS + i2[..., None, :]).reshape(blk, PEER_HEADS, PEER_TOPK * PEER_TOPK)
        top_s, pos = lax.top_k(cand, PEER_TOPK)
        eidx = jnp.take_along_axis(cidx, pos, axis=-1)
        g = jax.nn.softmax(top_s, axis=-1)
        ue = u[eidx]
        ve = v[eidx]
        act = jax.nn.gelu(jnp.einsum('nd,nhkd->nhk', xb, ue).astype(jnp.float32))
        return jnp.einsum('nhk,nhkd->nd', (g * act).astype(xb.dtype), ve)

    return lax.map(one, xp).reshape(n_pad, D_MODEL)[:n]


def channel_mix(x, norm_g, wq, subkeys, u, v):
    out = peer(rmsnorm(x, norm_g).reshape(-1, D_MODEL), wq, subkeys, u, v)
    return x + out.reshape(x.shape)


def setup_inputs(seed: int = 0) -> dict:
    key = jax.random.key(seed)
    ks = jax.random.split(key, 32)
    f32 = jnp.float32
    n_pages = PAST_LEN // PAGE_SIZE
    n_pool = (DEC_BATCH * n_pages * 5) // 4

    def nrm(k, shape, scale):
        return jax.random.normal(k, shape, f32) * scale

    def gain(k, shape):
        return 1.0 + 0.02 * jax.random.normal(k, shape, f32)

    page_table = jax.random.permutation(ks[7], n_pool)[:DEC_BATCH * n_pages]
    page_table = page_table.reshape(DEC_BATCH, n_pages).astype(jnp.int32)
    return {
        "x_prompt": nrm(ks[0], (BATCH, SEQ, D_MODEL), 1.0),
        "x_sample": nrm(ks[1], (DEC_BATCH, DEC_SEQ, D_MODEL), 1.0),
        "cache_k": nrm(ks[2], (DEPTH, n_pool, PAGE_SIZE, DIFF_HEADS, 2 * DIFF_DH), 1.0),
        "cache_v": nrm(ks[3], (DEPTH, n_pool, PAGE_SIZE, DIFF_HEADS, DIFF_DV), 1.0),
        "state_C": nrm(ks[4], (DEPTH, DEC_BATCH, MLSTM_HEADS, MLSTM_DK, MLSTM_DV), 0.05),
        "state_n": nrm(ks[5], (DEPTH, DEC_BATCH, MLSTM_HEADS, MLSTM_DK), 0.5),
        "state_m": jax.random.uniform(ks[6], (DEPTH, DEC_BATCH, MLSTM_HEADS), f32, -2.0, 2.0),
        "page_table": page_table,
        "norm1_g": gain(ks[8], (DEPTH, D_MODEL)),
        "w_in": nrm(ks[9], (DEPTH, D_MODEL, IN_COLS), D_MODEL ** -0.5),
        "q_norm_g": gain(ks[10], (DEPTH, DIFF_DH)),
        "k_norm_g": gain(ks[11], (DEPTH, DIFF_DH)),
        "lam_q1": nrm(ks[12], (DEPTH, DIFF_DH), 0.1),
        "lam_k1": nrm(ks[13], (DEPTH, DIFF_DH), 0.1),
        "lam_q2": nrm(ks[14], (DEPTH, DIFF_DH), 0.1),
        "lam_k2": nrm(ks[15], (DEPTH, DIFF_DH), 0.1),
        "diff_subln_g": gain(ks[16], (DEPTH, DIFF_HEADS, DIFF_DV)),
        "b_i": nrm(ks[17], (DEPTH, MLSTM_HEADS), 0.1),
        "b_f": 3.0 + jax.random.uniform(ks[18], (DEPTH, MLSTM_HEADS), f32, 0.0, 3.0),
        "mlstm_norm_g": gain(ks[19], (DEPTH, MLSTM_HEADS, MLSTM_DV)),
        "w_branch_a": nrm(ks[20], (DEPTH, DIFF_WIDTH, D_MODEL), DIFF_WIDTH ** -0.5),
        "w_branch_b": nrm(ks[21], (DEPTH, MLSTM_WIDTH, D_MODEL), MLSTM_WIDTH ** -0.5),
        "w_out": nrm(ks[22], (DEPTH, D_MODEL, D_MODEL), D_MODEL ** -0.5),
        "norm2_g": gain(ks[23], (DEPTH, D_MODEL)),
        "peer_wq": nrm(ks[24], (DEPTH, D_MODEL, PEER_HEADS * PEER_DKEY), D_MODEL ** -0.5),
        "peer_subkeys": nrm(ks[25], (DEPTH, 2, PEER_NKEYS, PEER_HALF), PEER_HALF ** -0.5),
        "peer_u": nrm(ks[26], (DEPTH, PEER_EXPERTS, D_MODEL), D_MODEL ** -0.5),
        "peer_v": nrm(ks[27], (DEPTH, PEER_EXPERTS, D_MODEL), (PEER_HEADS * PEER_TOPK) ** -0.5),
    }


def reference(x_prompt, x_sample, cache_k, cache_v, state_C, state_n, state_m, page_table,
              norm1_g, w_in, q_norm_g, k_norm_g, lam_q1, lam_k1, lam_q2, lam_k2, diff_subln_g,
              b_i, b_f, mlstm_norm_g, w_branch_a, w_branch_b, w_out, norm2_g,
              peer_wq, peer_subkeys, peer_u, peer_v):
    n_pages = PAST_LEN // PAGE_SIZE
    xp, xs = x_prompt, x_sample
    Bp, T, _ = xp.shape
    Bs, Ts, _ = xs.shape
    kp_l, vp_l, Cp_l, np_l, mp_l = [], [], [], [], []
    ks_l, vs_l, Cs_l, ns_l, ms_l = [], [], [], [], []
    for l in range(DEPTH):
        lam_init = 0.8 - 0.6 * math.exp(-0.3 * l)
        lam = diff_lambda(lam_q1[l], lam_k1[l], lam_q2[l], lam_k2[l], lam_init)

        dq, dk, dv, mq, mk, mv, mo, ig, lf, gates = mixer_inputs(
            xp, norm1_g[l], w_in[l], q_norm_g[l], k_norm_g[l], b_i[l], b_f[l])
        k_pos = jnp.arange(T, dtype=jnp.int32)

        def q_block(i, dq=dq, dk=dk, dv=dv, k_pos=k_pos, lam=lam):
            start = i * Q_BLOCK
            qb = lax.dynamic_slice_in_dim(dq, start, Q_BLOCK, axis=1)
            q_pos = start + jnp.arange(Q_BLOCK, dtype=jnp.int32)
            return diff_attend(qb, dk, dv, q_pos, k_pos, lam)

        o_diff = lax.map(q_block, jnp.arange(T // Q_BLOCK, dtype=jnp.int32))
        o_diff = jnp.moveaxis(o_diff, 0, 1).reshape(Bp, T, DIFF_HEADS, DIFF_DV)
        C0 = jnp.zeros((Bp, MLSTM_HEADS, MLSTM_DK, MLSTM_DV), jnp.float32)
        n0 = jnp.zeros((Bp, MLSTM_HEADS, MLSTM_DK), jnp.float32)
        m0 = jnp.zeros((Bp, MLSTM_HEADS), jnp.float32)
        h, Cp, npr, mp = mlstm_scan(mq, mk, mv, ig, lf, C0, n0, m0)
        xp = merge_branches(xp, o_diff, h, mo, gates, diff_subln_g[l], mlstm_norm_g[l],
                            w_branch_a[l], w_branch_b[l], w_out[l], lam_init)
        xp = channel_mix(xp, norm2_g[l], peer_wq[l], peer_subkeys[l], peer_u[l], peer_v[l])
        kp_l.append(dk.reshape(Bp, T, DIFF_HEADS, 2 * DIFF_DH))
        vp_l.append(dv)
        Cp_l.append(Cp)
        np_l.append(npr)
        mp_l.append(mp)

        dq, dk, dv, mq, mk, mv, mo, ig, lf, gates = mixer_inputs(
            xs, norm1_g[l], w_in[l], q_norm_g[l], k_norm_g[l], b_i[l], b_f[l])
        k_past = cache_k[l][page_table].reshape(Bs, n_pages * PAGE_SIZE, DIFF_HEADS, 2, DIFF_DH)
        v_past = cache_v[l][page_table].reshape(Bs, n_pages * PAGE_SIZE, DIFF_HEADS, DIFF_DV)
        k_all = jnp.concatenate([k_past, dk.astype(k_past.dtype)], axis=1)
        v_all = jnp.concatenate([v_past, dv.astype(v_past.dtype)], axis=1)
        q_pos = PAST_LEN + jnp.arange(Ts, dtype=jnp.int32)
        k_pos = jnp.arange(PAST_LEN + Ts, dtype=jnp.int32)
        o_diff = diff_attend(dq, k_all, v_all, q_pos, k_pos, lam)
        h, Cs, ns, ms = mlstm_scan(mq, mk, mv, ig, lf, state_C[l].astype(jnp.float32),
                                   state_n[l].astype(jnp.float32), state_m[l].astype(jnp.float32))
        xs = merge_branches(xs, o_diff, h, mo, gates, diff_subln_g[l], mlstm_norm_g[l],
                            w_branch_a[l], w_branch_b[l], w_out[l], lam_init)
        xs = channel_mix(xs, norm2_g[l], peer_wq[l], peer_subkeys[l], peer_u[l], peer_v[l])
        ks_l.append(dk.reshape(Bs, Ts, DIFF_HEADS, 2 * DIFF_DH))
        vs_l.append(dv)
        Cs_l.append(Cs)
        ns_l.append(ns)
        ms_l.append(ms)

    k_prompt = jnp.stack(kp_l)
    v_prompt = jnp.stack(vp_l)
    C_prompt = jnp.stack(Cp_l)
    n_prompt = jnp.stack(np_l)
    m_prompt = jnp.stack(mp_l)
    k_sample = jnp.stack(ks_l)
    v_sample = jnp.stack(vs_l)
    C_sample = jnp.stack(Cs_l)
    n_sample = jnp.stack(ns_l)
    m_sample = jnp.stack(ms_l)
    return (xp, xs, k_prompt, v_prompt, C_prompt, n_prompt, m_prompt,
            k_sample, v_sample, C_sample, n_sample, m_sample)
```

```python
import numpy as np
from contextlib import ExitStack
import concourse.bass as bass
import concourse.mybir as mybir
from concourse.bass_utils import run_bass_kernel_spmd
from concourse.alu_op_type import AluOpType as ALU

F32 = mybir.dt.float32
BF16 = mybir.dt.bfloat16
I32 = mybir.dt.int32
U32 = mybir.dt.uint32
AF = mybir.ActivationFunctionType
AX = mybir.AxisListType

D = 1024
IN_COLS = 8200
NEG = -30000.0


class Buf:
    __slots__ = ("w", "r", "dsem", "name", "excl")

    def __init__(self, name="", excl=False):
        self.excl = excl
        self.w = None
        self.r = {}
        self.dsem = None
        self.name = name


class Sched:
    ENG = ("tensor", "vector", "scalar", "gpsimd", "sync")

    def __init__(self, nc, stack):
        self.nc = nc
        self.stack = stack
        self.prog = {n: [] for n in self.ENG}
        self.sem = {n: stack.enter_context(nc.semaphore("s_" + n)) for n in self.ENG}
        self.cnt = {n: 0 for n in self.ENG}
        self.known = {n: {} for n in self.ENG}
        self.dsems = []
        self.log = {n: [] for n in self.ENG}

    def _wait(self, eng, ev):
        key, sem, val = ev
        if self.known[eng].get(key, 0) >= val:
            return
        self.known[eng][key] = val
        self.prog[eng].append(lambda e, sem=sem, val=val: e.wait_ge(sem, val))
        self.log[eng].append('  wait %s >= %d' % (key, val))

    def _deps(self, eng, reads, writes):
        for b in reads:
            if b.w is not None:
                self._wait(eng, b.w)
            if b.excl:
                for k_, ev in b.r.items():
                    if k_ != eng:
                        self._wait(eng, ev)
        for b in writes:
            if b.w is not None:
                self._wait(eng, b.w)
            for ev in b.r.values():
                self._wait(eng, ev)

    def _mark(self, key, ev, reads, writes):
        for b in reads:
            b.r[key] = ev
        for b in writes:
            b.w = ev
            b.r = {}

    def group(self, eng, fns, reads=(), writes=()):
        self._deps(eng, reads, writes)
        self.cnt[eng] += 1
        c = self.cnt[eng]
        sem = self.sem[eng]
        for fn in fns[:-1]:
            self.prog[eng].append(fn)
        last = fns[-1]
        self.prog[eng].append(lambda e, fn=last, sem=sem: fn(e).then_inc(sem, 1))
        ev = (eng, sem, c)
        self.log[eng].append('%s#%d n=%d R=%s W=%s' % (eng, c, len(fns), [b.name for b in reads], [b.name for b in writes]))
        if eng == "tensor":
            self.known[eng][eng] = c
        self._mark(eng, ev, reads, writes)
        return ev

    def op(self, eng, fn, reads=(), writes=()):
        return self.group(eng, [fn], reads, writes)

    def _dsem(self, b):
        if b.dsem is None:
            key = "d%d" % len(self.dsems)
            sem = self.stack.enter_context(self.nc.semaphore(key))
            b.dsem = [sem, 0, key]
            self.dsems.append(b.dsem)
        return b.dsem

    def dma(self, q, out, in_, reads=(), writes=(), owner=None, **kw):
        self._deps(q, reads, writes)
        if owner is None:
            owner = writes[0] if writes else reads[0]
        d = self._dsem(owner)
        d[1] += 16
        sem, val, key = d[0], d[1], d[2]
        self.prog[q].append(lambda e, out=out, in_=in_, sem=sem, kw=kw:
                            e.dma_start(out=out, in_=in_, **kw).then_inc(sem, 16))
        ev = (key, sem, val)
        self._mark(key, ev, reads, writes)
        return ev

    def custom_dma(self, q, fn, reads=(), writes=(), owner=None):
        self._deps(q, reads, writes)
        if owner is None:
            owner = writes[0] if writes else reads[0]
        d = self._dsem(owner)
        d[1] += 16
        sem, val, key = d[0], d[1], d[2]
        self.prog[q].append(lambda e, fn=fn, sem=sem: fn(e).then_inc(sem, 16))
        ev = (key, sem, val)
        self._mark(key, ev, reads, writes)
        return ev

    def barrier(self):
        for e in self.ENG:
            for p in self.ENG:
                if p != e and self.cnt[p] > 0:
                    self._wait(e, (p, self.sem[p], self.cnt[p]))
            for d in self.dsems:
                if d[1] > 0:
                    self._wait(e, (d[2], d[0], d[1]))

    def finish(self):
        e = "sync"
        for p in self.ENG:
            if p != e and self.cnt[p] > 0:
                self._wait(e, (p, self.sem[p], self.cnt[p]))
        for d in self.dsems:
            if d[1] > 0:
                self._wait(e, (d[2], d[0], d[1]))

    def emit(self, block):
        prog = self.prog

        @block.tensor
        def _(e):
            for f in prog["tensor"]:
                f(e)

        @block.vector
        def _(e):
            for f in prog["vector"]:
                f(e)

        @block.scalar
        def _(e):
            for f in prog["scalar"]:
                f(e)

        @block.gpsimd
        def _(e):
            for f in prog["gpsimd"]:
                f(e)

        @block.sync
        def _(e):
            for f in prog["sync"]:
                f(e)


def mm(out, lhsT, rhs, start=True, stop=True, sgc=False):
    return lambda e: e.matmul(out, lhsT=lhsT, rhs=rhs, start=start, stop=stop, skip_group_check=sgc)


def tr(out, in_, ident):
    return lambda e: e.transpose(out=out, in_=in_, identity=ident)


def act(out, in_, func, **kw):
    return lambda e: e.activation(out=out, in_=in_, func=func, **kw)


def tt(out, in0, in1, op):
    return lambda e: e.tensor_tensor(out=out, in0=in0, in1=in1, op=op)


def ts(out, in0, s1, s2=None, op0=ALU.mult, op1=None, **kw):
    if op1 is None:
        return lambda e: e.tensor_scalar(out=out, in0=in0, scalar1=s1, scalar2=None, op0=op0, **kw)
    return lambda e: e.tensor_scalar(out=out, in0=in0, scalar1=s1, scalar2=s2, op0=op0, op1=op1, **kw)


def stt(out, in0, scalar, in1, op0, op1):
    return lambda e: e.scalar_tensor_tensor(out=out, in0=in0, scalar=scalar, in1=in1, op0=op0, op1=op1)


def cp(out, in_):
    return lambda e: e.tensor_copy(out=out, in_=in_)


def red(out, in_, op, axis=AX.X):
    return lambda e: e.tensor_reduce(out=out, in_=in_, axis=axis, op=op)


def recip(out, in_):
    return lambda e: e.reciprocal(out=out, in_=in_)


def mset(out, val):
    return lambda e: e.memset(out, val)


class Cfg:
    def __init__(self, NP=16, NO=16, NS=4, npool=2560, npages=64, stages=99, dbg=False, nheads=8, attn=True):
        self.NP, self.NO, self.NS = NP, NO, NS
        self.npool, self.npages = npool, npages
        self.stages = stages
        self.dbg = dbg
        self.nheads, self.attn = nheads, attn
        self.sample = False
        self.peer = True
        import os
        self.skip = set(os.environ.get('SKIP', '').split(','))


def build_program(cfg):
    NP, NO, NS = cfg.NP, cfg.NO, cfg.NS
    TP, TO = NP * 128, NO * 128
    TOS = TO + NS
    NG = NO // 4
    nc = bass.Bass("TRN2", target_bir_lowering=False)
    global LAST_SCHED

    def din(name, shape, dt=F32):
        return nc.dram_tensor(name, list(shape), dt, kind="ExternalInput").ap()

    def dout(name, shape, dt=F32):
        return nc.dram_tensor(name, list(shape), dt, kind="ExternalOutput").ap()

    x_pre = din("x_pre", [TP, D])
    x_own = din("x_own", [TO, D])
    x_smp = din("x_smp", [NS, D])
    w_in = din("w_in", [D, IN_COLS])
    prow = din("prow", [1, 4608])
    cst = din("cst", [128, 2048])
    qrow = din("qrow", [2, 8 * 512])

    gcol = din("gcol", [128, 8])
    C_out = dout("C_out", [4, 128, 256])
    n_out = dout("n_out", [4, 128])
    m_out = dout("m_out", [4, 1])
    dbg_hb = dout("dbg_hb", [TOS, D]) if cfg.dbg else None
    dbg_x1 = dout("dbg_x1", [TOS, D]) if cfg.dbg else None
    w_a = din("w_a", [D, D])
    w_b = din("w_b", [D, D])
    w_o = din("w_o", [D, D])
    x1d = nc.dram_tensor("x1d", [TOS, D], F32).ap()
    wq = din("wq", [D, 2048])
    skT = din("skT", [128, 256])
    uTh = din("uTh", [128, 128, 1024])
    pv_ = din("pv", [16384, D])
    uTb = nc.dram_tensor("uTb", [128, 128, 1024], BF16).ap()
    vbd = nc.dram_tensor("vbd", [16384, D], BF16).ap()
    Gd = nc.dram_tensor("Gd", [128, 128, TOS], BF16).ap()
    npool = cfg.npool
    cache_k = din("cache_k", [npool * 128, D])
    cache_v = din("cache_v", [npool * 128, D])
    ptab = din("ptab", [NS, 64], I32)
    dcst = din("dcst", [128, 1024])
    qsd = nc.dram_tensor("qsd", [NS, D], F32).ap()
    sC = din("sC", [NS * 4, 128, 256])
    sn = din("sn", [NS, 512])
    smm = din("smm", [NS, 4])
    brow = din("brow", [1, 8])
    C_smp = dout("C_smp", [NS * 4, 128, 256])
    n_smp = dout("n_smp", [NS, 512])
    m_smp = dout("m_smp", [NS, 4])
    k_smp = dout("k_smp", [NS, D])
    v_smp = dout("v_smp", [NS, D])
    y_own = dout("y_own", [TO, D])
    y_smp = dout("y_smp", [NS, D])
    k_own = dout("k_own", [TO, D])
    v_own = dout("v_own", [TO, D])
    dbg_a = dout("dbg_a", [TOS, D]) if cfg.dbg else None

    with ExitStack() as st:
        S = Sched(nc, st)
        globals()['LAST_SCHED'] = S

        ARENA_WORDS = 52000
        arena = st.enter_context(nc.sbuf_tensor("arena", [128, ARENA_WORDS], F32))
        apos = [0]
        HW = [0]
        globals()['LAST_HW'] = HW

        def carve(off_words, shape, dt):
            n = 1
            for d_ in shape[1:]:
                n *= d_
            esz = 4 if dt in (F32, I32, U32) else 2
            nw = (n * esz + 3) // 4
            assert off_words + nw <= ARENA_WORDS, ("SBUF arena overflow", off_words, nw)
            v = arena[:, off_words:off_words + nw]
            if dt != F32:
                v = v.bitcast(dt)
            v = v[0:shape[0], 0:n]
            if len(shape) == 3:
                v = v.rearrange("p (a b) -> p a b", a=shape[1])
            elif len(shape) == 4:
                v = v.rearrange("p (a b c) -> p a b c", a=shape[1], b=shape[2])
            return v, nw

        def sb(name, shape, dt, stack=None):
            v, nw = carve(apos[0], shape, dt)
            apos[0] += (nw + 15) // 16 * 16
            HW[0] = max(HW[0], apos[0])
            return v

        class Mark:
            def __enter__(self_):
                self_.m = apos[0]
                return self_

            def __exit__(self_, *a):
                print("[sbuf] phase high-water %.1f KB (mark at %.1f KB)" % (HW[0] * 4 / 1024, self_.m * 4 / 1024))
                HW[0] = self_.m
                apos[0] = self_.m
                return False

        V = lambda fn, r=(), w=(): S.op("vector", fn, r, w)
        A = lambda fn, r=(), w=(): S.op("scalar", fn, r, w)
        G = lambda fn, r=(), w=(): S.op("gpsimd", fn, r, w)
        T = lambda fns, r=(), w=(): S.group("tensor", fns, r, w)

        ps = st.enter_context(nc.psum_tensor("ps", [128, 4096], F32))
        psb = ps[:].bitcast(BF16)
        pbuf = [Buf("ps%d" % i, excl=True) for i in range(8)]

        def bank(i, n=512, off=0):
            return ps[:, i * 512 + off:i * 512 + off + n]

        def bankb(i, n=1024, off=0):
            return psb[:, i * 1024 + off:i * 1024 + off + n]

        cst_sb = sb("cst_sb", [128, 2048], F32)
        cstB = Buf("cst")
        S.dma("sync", cst_sb[:], cst, writes=[cstB])
        ident_f = cst_sb[:, 0:128]
        C_BO = 256
        C_BP = 256 + 128
        idb = sb("idb", [128, 128], BF16)
        mneg = sb("mneg", [128, 128], BF16)
        cbB = Buf("cb")
        V(cp(idb[:], cst_sb[:, 0:128]), [cstB], [cbB])
        V(cp(mneg[:], cst_sb[:, 128:256]), [cstB], [cbB])

        g_sb = sb("g_sb", [128, 1024], F32)
        gB = Buf("g")
        S.dma("sync", g_sb[:], prow[:, 0:1024].partition_broadcast(128), writes=[gB])
        qkg = sb("qkg", [128, 256], F32)
        qkgB = Buf("qkg")
        S.dma("sync", qkg[:], prow[:, 2048:2304].partition_broadcast(128), writes=[qkgB])
        lamt = sb("lamt", [128, 256], F32)
        lamB = Buf("lam")
        S.dma("sync", lamt[:], prow[:, 4352:4608].partition_broadcast(128), writes=[lamB])
        sm = sb("sm", [128, 64], F32)
        smB = Buf("sm")
        lsc = sb("lsc", [128, 128], F32)
        lscB = Buf()
        V(tt(lsc[:, 0:64], lamt[:, 0:64], lamt[:, 64:128], ALU.mult), [lamB], [lscB])
        V(tt(lsc[:, 64:128], lamt[:, 128:192], lamt[:, 192:256], ALU.mult), [lamB], [lscB])
        V(red(sm[:, 2:4], lsc[:].rearrange("p (a b) -> p a b", a=2), ALU.add), [lscB], [smB])
        A(act(sm[:, 4:6], sm[:, 2:4], AF.Exp), [smB], [smB])
        V(tt(sm[:, 6:7], sm[:, 4:5], sm[:, 5:6], ALU.subtract), [smB], [smB])
        V(ts(sm[:, 0:1], sm[:, 6:7], 0.2, None, op0=ALU.add), [smB], [smB])
        V(ts(sm[:, 1:2], sm[:, 0:1], -1.0, None, op0=ALU.mult), [smB], [smB])

        xnT_own = sb("xnT_own", [128, 8, TOS], BF16)
        aT_off = apos[0]
        aT = sb("aT", [128, 8, TOS], BF16)
        hbT_off = apos[0]
        hbT = sb("hbT", [128, 8, TOS], BF16)
        pre_off = apos[0]
        xnT_pre = sb("xnT_pre", [128, 8, TP], BF16)
        xnB_pre = [Buf("xnp%d" % i) for i in range(NP)]
        xnB_own = [Buf("xno%d" % i) for i in range(NO + 1)]
        aTB = [Buf("aT%d" % i) for i in range(NO + 1)]
        hbTB = [Buf("hbT%d" % i) for i in range(NO + 1)]

        with Mark() as p0:
            xt = [sb("xt%d" % i, [128, 1024], F32, p0) for i in range(2)]
            xtB = [Buf("xt0"), Buf("xt1")]
            sq = sb("sq", [128, 1024], F32, p0)
            sqB = Buf("sq")
            xn = [sb("xn%d" % i, [128, 1024], BF16, p0) for i in range(2)]
            xnB = [Buf(), Buf()]
            rs = [sb("rs%d" % i, [128, 4], F32, p0) for i in range(2)]
            rsB = [Buf(), Buf()]
            tiles = [("pre", i) for i in range(NP)] + [("own", i) for i in range(NO)] + [("smp", 0)]
            for n, (kind, i) in enumerate(tiles):
                s = n % 2
                rows = NS if kind == "smp" else 128
                src = {"pre": x_pre, "own": x_own, "smp": x_smp}[kind]
                r0 = 0 if kind == "smp" else i * 128
                S.dma("sync", xt[s][0:rows, :], src[r0:r0 + rows, :], writes=[xtB[s]])
                A(act(sq[0:rows, :], xt[s][0:rows, :], AF.Square, accum_out=rs[s][0:rows, 0:1]), [xtB[s]], [sqB, rsB[s]])
                V(ts(rs[s][0:rows, 1:2], rs[s][0:rows, 0:1], 1.0 / 1024, 1e-6, op0=ALU.mult, op1=ALU.add), [rsB[s]], [rsB[s]])
                A(act(rs[s][0:rows, 2:3], rs[s][0:rows, 1:2], AF.Sqrt), [rsB[s]], [rsB[s]])
                V(recip(rs[s][0:rows, 3:4], rs[s][0:rows, 2:3]), [rsB[s]], [rsB[s]])
                V(stt(xn[s][0:rows, :], xt[s][0:rows, :], rs[s][0:rows, 3:4], g_sb[0:rows, :], ALU.mult, ALU.mult),
                  [xtB[s], rsB[s], gB], [xnB[s]])
                pbk = 6 + s
                T([tr(bankb(pbk, rows, k * 128), xn[s][0:rows, k * 128:(k + 1) * 128], idb[0:rows, 0:rows]) for k in range(8)],
                  [xnB[s], cbB], [pbuf[pbk]])
                if kind == "pre":
                    dst, dB = xnT_pre[:, :, i * 128:(i + 1) * 128], xnB_pre[i]
                elif kind == "own":
                    dst, dB = xnT_own[:, :, i * 128:(i + 1) * 128], xnB_own[i]
                else:
                    dst, dB = xnT_own[:, :, TO:TO + NS], xnB_own[NO]
                srcp = bankb(pbk, 1024).rearrange("p (k n) -> p k n", k=8)[:, :, 0:rows]
                if n % 2 == 0:
                    A(act(dst, srcp, AF.Copy), [pbuf[pbk]], [dB])
                else:
                    V(cp(dst, srcp), [pbuf[pbk]], [dB])
        S.barrier()

        S.dma("sync", g_sb[:], prow[:, 2304:3328].partition_broadcast(128), writes=[gB])
        with Mark() as p1:
            wh = [sb("wh%d" % i, [128, 8, 384], BF16, p1) for i in range(2)]
            whB = [Buf("wh0"), Buf("wh1")]
            _save = apos[0]
            apos[0] = hbT_off
            KT = [sb("KT%d" % c, [66, TP + TO], BF16, p1) for c in range(2)]
            QT = [sb("QT%d" % c, [66, TO], BF16, p1) for c in range(2)]
            Vh = sb("Vh", [128, NP + NO, 129], BF16, p1)
            assert apos[0] <= pre_off, (apos[0], pre_off)
            apos[0] = _save
            KTBc = [[Buf("KT%d_%d" % (c, i)) for i in range(NP + NO)] for c in range(2)]
            QTBc = [[Buf("QT%d_%d" % (c, i)) for i in range(NO)] for c in range(2)]
            KrowB = [Buf("Krow0"), Buf("Krow1")]
            QrowB = [Buf("Qrow0"), Buf("Qrow1")]
            VhB = [Buf("Vh%d" % i) for i in range(NP + NO)]
            initB = Buf("init")
            if 'mset' not in cfg.skip:
                G(mset(Vh[:, :, 128:129], 1.0), [], VhB)
                for c in range(2):
                    G(mset(KT[c][64:66, :], 1.0), [], [KrowB[c]])
            sq2 = [sb("sq2_%d" % i, [128, 256], F32, p1) for i in range(2)]
            ssq = [sb("ssq_%d" % i, [128, 16], F32, p1) for i in range(2)]
            qkf = [sb("qkf_%d" % i, [128, 256], F32, p1) for i in range(2)]
            qkb = [sb("qkb_%d" % i, [128, 256], BF16, p1) for i in range(2)]
            vf = [sb("vf_%d" % i, [128, 128], F32, p1) for i in range(2)]
            scrB = [Buf("scr0"), Buf("scr1")]
            qkfB = [Buf(), Buf()]
            qkbB = [Buf(), Buf()]
            vfB = [Buf(), Buf()]
            Pt = [sb("Pt%d" % i, [128, 512], BF16, p1) for i in range(4)]
            PtB = [Buf("Pt%d" % i) for i in range(4)]
            osc = [sb("osc%d" % i, [128, 128], F32, p1) for i in range(2)]
            osq = sb("osq", [128, 128], F32, p1)
            osm = [sb("osm%d" % i, [128, 16], F32, p1) for i in range(2)]
            ab = [sb("ab%d" % i, [128, 128], BF16, p1) for i in range(2)]
            oB = [Buf("o0"), Buf("o1")]
            osqB = Buf("osq")
            abB = [Buf("ab0"), Buf("ab1")]
            if cfg.dbg:
                af = [sb("af%d" % i, [128, 128], F32, p1) for i in range(2)]
                afB = [Buf(), Buf()]

            def load_wh(h):
                s = h % 2
                for j, c0 in enumerate((h * 128, 1024 + h * 128, 2048 + h * 128)):
                    S.dma("gpsimd", wh[s][:, :, j * 128:(j + 1) * 128],
                          w_in[:, c0:c0 + 128].rearrange("(k p) c -> p k c", p=128), writes=[whB[s]])

            load_wh(0)
            nscr = 0
            smpB = Buf("smpdram")
            precB = Buf("precast")
            prec_done = [0]

            def precast(n_):
                for c_ in range(prec_done[0], min(128, prec_done[0] + n_)):
                    S.dma("gpsimd", uTb[c_], uTh[c_], writes=[precB])
                    S.dma("gpsimd", vbd[c_ * 128:(c_ + 1) * 128, :], pv_[c_ * 128:(c_ + 1) * 128, :], writes=[precB])
                prec_done[0] = min(128, prec_done[0] + n_)
            for h in range(cfg.nheads):
                if h + 1 < 8:
                    load_wh(h + 1)
                if cfg.peer:
                    precast(16)
                ws = h % 2
                for c in range(2):
                    for g in range(NG):
                        S.dma("gpsimd", QT[c][64:66, g * 512:(g + 1) * 512], qrow[:, h * 512:(h + 1) * 512], writes=[QrowB[c]])
                tl = [("pre", i) for i in range(NP)] + [("own", i) for i in range(NO)]
                if cfg.sample:
                    s = nscr % 2
                    nscr += 1
                    pz = s
                    rr = slice(0, NS)
                    T([mm(bank(pz, 384)[rr, :], xnT_own[:, k, TO:TO + NS], wh[ws][:, k, 0:384], start=(k == 0), stop=(k == 7)) for k in range(8)],
                      [xnB_own[NO], whB[ws]], [pbuf[pz]])
                    A(act(sq2[s][rr, 0:256], bank(pz, 256)[rr, :], AF.Square), [pbuf[pz]], [scrB[s]])
                    V(red(ssq[s][rr, 0:4], sq2[s][rr, 0:256].rearrange("p (a b) -> p a b", b=64), ALU.add), [scrB[s]], [scrB[s]])
                    V(ts(ssq[s][rr, 4:8], ssq[s][rr, 0:4], 1.0 / 64, 1e-6, op0=ALU.mult, op1=ALU.add), [scrB[s]], [scrB[s]])
                    A(act(ssq[s][rr, 8:12], ssq[s][rr, 4:8], AF.Sqrt), [scrB[s]], [scrB[s]])
                    V(recip(ssq[s][rr, 12:16], ssq[s][rr, 8:12]), [scrB[s]], [scrB[s]])
                    V(tt(qkf[s][rr, 0:256].rearrange("p (a b) -> p a b", b=64), bank(pz, 256)[rr, :].rearrange("p (a b) -> p a b", b=64),
                         ssq[s][rr, 12:16].unsqueeze(2).to_broadcast([NS, 4, 64]), ALU.mult), [pbuf[pz], scrB[s]], [qkfB[s]])
                    V(tt(qkf[s][rr, 0:256], qkf[s][rr, 0:256], qkg[rr, 0:256], ALU.mult), [qkfB[s], qkgB], [qkfB[s]])
                    A(act(vf[s][rr, :], bank(pz, 128, 256)[rr, :], AF.Copy), [pbuf[pz]], [vfB[s]])
                    S.dma("sync", k_smp[:, h * 128:(h + 1) * 128], qkf[s][rr, 128:256], reads=[qkfB[s]], writes=[smpB], owner=qkfB[s])
                    S.dma("sync", v_smp[:, h * 128:(h + 1) * 128], vf[s][rr, :], reads=[vfB[s]], writes=[smpB], owner=vfB[s])
                    A(act(sq2[s][rr, 0:128], qkf[s][rr, 0:128], AF.Copy, scale=0.125), [qkfB[s], scrB[s]], [scrB[s]])
                    S.dma("sync", qsd[:, h * 128:(h + 1) * 128], sq2[s][rr, 0:128], reads=[scrB[s]], writes=[smpB], owner=scrB[s])
                if 'proj' in cfg.skip:
                    tl = []
                pbase = nscr
                nscr += len(tl)

                def projA(n, h=h, ws=ws, pbase=pbase, tl=tl):
                  if True:
                    kind, i = tl[n]
                    s = (pbase + n) % 2
                    pz = s
                    own = kind == "own"
                    gi = i if kind == "pre" else NP + i
                    if own:
                        lhs = [xnT_own[:, k, i * 128:(i + 1) * 128] for k in range(8)]
                        rB = xnB_own[i]
                        c0, nco = 0, 384
                    else:
                        lhs = [xnT_pre[:, k, i * 128:(i + 1) * 128] for k in range(8)]
                        rB = xnB_pre[i]
                        c0, nco = 128, 256
                    T([mm(bank(pz, nco), lhs[k], wh[ws][:, k, c0:c0 + nco], start=(k == 0), stop=(k == 7)) for k in range(8)],
                      [rB, whB[ws]], [pbuf[pz]])
                    nqk = 256 if own else 128
                    A(act(sq2[s][:, 0:nqk], bank(pz, nqk), AF.Square), [pbuf[pz]], [scrB[s]])
                    V(red(ssq[s][:, 0:nqk // 64], sq2[s][:, 0:nqk].rearrange("p (a b) -> p a b", b=64), ALU.add), [scrB[s]], [scrB[s]])
                    V(ts(ssq[s][:, 4:4 + nqk // 64], ssq[s][:, 0:nqk // 64], 1.0 / 64, 1e-6, op0=ALU.mult, op1=ALU.add), [scrB[s]], [scrB[s]])
                    A(act(ssq[s][:, 8:8 + nqk // 64], ssq[s][:, 4:4 + nqk // 64], AF.Sqrt), [scrB[s]], [scrB[s]])
                    V(recip(ssq[s][:, 12:12 + nqk // 64], ssq[s][:, 8:8 + nqk // 64]), [scrB[s]], [scrB[s]])
                    ng = nqk // 64
                    V(tt(qkf[s][:, 0:nqk].rearrange("p (a b) -> p a b", b=64),
                         bank(pz, nqk).rearrange("p (a b) -> p a b", b=64),
                         ssq[s][:, 12:12 + ng].unsqueeze(2).to_broadcast([128, ng, 64]), ALU.mult),
                      [pbuf[pz], scrB[s]], [qkfB[s]])
                    gsl = qkg[:, 0:256] if own else qkg[:, 128:256]
                    V(tt(qkf[s][:, 0:nqk], qkf[s][:, 0:nqk], gsl, ALU.mult), [qkfB[s], qkgB], [qkfB[s]])
                    if own:
                        A(act(qkb[s][:, 0:128], qkf[s][:, 0:128], AF.Copy, scale=0.125), [qkfB[s]], [qkbB[s]])
                        A(act(qkb[s][:, 128:256], qkf[s][:, 128:256], AF.Copy), [qkfB[s]], [qkbB[s]])
                        A(act(vf[s][:], bank(pz, 128, 256), AF.Copy), [pbuf[pz]], [vfB[s]])
                        V(cp(Vh[:, gi, 0:128], bank(pz, 128, 256)), [pbuf[pz]], [VhB[gi]])
                        S.dma("sync", k_own[i * 128:(i + 1) * 128, h * 128:(h + 1) * 128], qkf[s][:, 128:256], reads=[qkfB[s]])
                        S.dma("sync", v_own[i * 128:(i + 1) * 128, h * 128:(h + 1) * 128], vf[s][:], reads=[vfB[s]])
                        koff = 128
                    else:
                        A(act(qkb[s][:, 0:128], qkf[s][:, 0:128], AF.Copy), [qkfB[s]], [qkbB[s]])
                        V(cp(Vh[:, gi, 0:128], bank(pz, 128, 128)), [pbuf[pz]], [VhB[gi]])
                        koff = 0

                def projB(n, h=h, pbase=pbase, tl=tl):
                  if True:
                    kind, i = tl[n]
                    s = (pbase + n) % 2
                    own = kind == "own"
                    gi = i if kind == "pre" else NP + i
                    koff = 128 if own else 0
                    pt_ = 6 + s
                    fns = [tr(bankb(pt_, 128, c * 128)[0:64, :], qkb[s][:, koff + c * 64:koff + (c + 1) * 64], idb[:]) for c in range(2)]
                    if own:
                        fns += [tr(bankb(pt_, 128, 256 + c * 128)[0:64, :], qkb[s][:, c * 64:(c + 1) * 64], idb[:]) for c in range(2)]
                    T(fns, [qkbB[s], cbB], [pbuf[pt_]])
                    for c in range(2):
                        A(act(KT[c][0:64, gi * 128:(gi + 1) * 128], bankb(pt_, 128, c * 128)[0:64, :], AF.Copy),
                          [pbuf[pt_]], [KTBc[c][gi]])
                        if own:
                            A(act(QT[c][0:64, i * 128:(i + 1) * 128], bankb(pt_, 128, 256 + c * 128)[0:64, :], AF.Copy),
                              [pbuf[pt_]], [QTBc[c][i]])

                for n in range(len(tl) + 1):
                    if n < len(tl):
                        projA(n)
                    if n >= 1:
                        projB(n - 1)

                npt = 0
                for g in range(NG if cfg.attn else 0):
                    def acc(c, qt):
                        a = c * 4 + qt
                        return bank(2 + a // 3, 129, (a % 3) * 129)
                    accB = [pbuf[2], pbuf[3], pbuf[4]]
                    ktiles = [("pre", kt) for kt in range(NP)] + [("own", kt) for kt in range(4 * g + 4)]
                    nk = len(ktiles)
                    started = [[False] * 4 for _ in range(2)]
                    steps = []
                    for ki, (kind, kt) in enumerate(ktiles):
                        for c in range(2):
                            steps.append((ki, kind, kt, c))

                    def emit_SE(idx, g=g, h=h):
                        ki, kind, kt, c = steps[idx]
                        gi = kt if kind == "pre" else NP + kt
                        if kind == "pre":
                            dt_ = 4 * g + NP - kt
                            bcol = cst_sb[:, C_BP + h * 28 + (dt_ - 1):C_BP + h * 28 + dt_]
                            m0 = 0
                        else:
                            dt_ = 4 * g - kt
                            bcol = cst_sb[:, C_BO + h * 16 + (dt_ + 3):C_BO + h * 16 + dt_ + 4]
                            m0 = max(0, kt - 4 * g)
                        diag = (kind == "own" and kt >= 4 * g)
                        sbk = 5 + (idx % 3)
                        pi = idx % 4
                        qs = g * 512 + m0 * 128
                        ncol = 512 - m0 * 128
                        fns = []
                        if diag:
                            fns.append(mm(bank(sbk, 128, m0 * 128), idb[:], mneg[:], start=True, stop=False))
                            fns.append(mm(bank(sbk, 128, m0 * 128), KT[c][:, gi * 128:(gi + 1) * 128], QT[c][:, qs:qs + 128],
                                          start=False, stop=True))
                            if ncol > 128:
                                fns.append(mm(bank(sbk, ncol - 128, m0 * 128 + 128), KT[c][:, gi * 128:(gi + 1) * 128],
                                              QT[c][:, qs + 128:qs + ncol], start=True, stop=True))
                        else:
                            fns.append(mm(bank(sbk, ncol, m0 * 128), KT[c][:, gi * 128:(gi + 1) * 128], QT[c][:, qs:qs + ncol],
                                          start=True, stop=True))
                        T(fns, [KTBc[c][gi], KrowB[c], QrowB[c], cbB] + QTBc[c][g * 4 + m0:g * 4 + 4], [pbuf[sbk]])
                        A(act(Pt[pi][:, m0 * 128:512], bank(sbk, ncol, m0 * 128), AF.Exp, bias=bcol), [pbuf[sbk], cstB], [PtB[pi]])

                    def emit_AV(idx, g=g):
                        ki, kind, kt, c = steps[idx]
                        gi = kt if kind == "pre" else NP + kt
                        m0 = 0 if kind == "pre" else max(0, kt - 4 * g)
                        pi = idx % 4
                        fns = []
                        for qt in range(m0, 4):
                            last = (kind == "own" and kt == 4 * g + qt)
                            fns.append(mm(acc(c, qt), Pt[pi][:, qt * 128:(qt + 1) * 128], Vh[:, gi, :],
                                          start=(ki == 0 and (c * 4 + qt) % 3 == 0), stop=last, sgc=True))
                        T(fns, [PtB[pi], VhB[gi]], accB)

                    LA = 2
                    for idx in range(len(steps) + LA):
                        if idx < len(steps):
                            emit_SE(idx)
                        if idx >= LA:
                            emit_AV(idx - LA)
                    for qt in range(4):
                        i = g * 4 + qt
                        s = i % 2
                        a1, a2 = acc(0, qt), acc(1, qt)
                        V(recip(osm[s][:, 0:1], a1[:, 128:129]), accB, [oB[s]])
                        V(recip(osm[s][:, 1:2], a2[:, 128:129]), accB, [oB[s]])
                        V(tt(osm[s][:, 2:3], osm[s][:, 1:2], sm[:, 1:2], ALU.mult), [oB[s], smB], [oB[s]])
                        V(ts(osc[s][:], a1[:, 0:128], osm[s][:, 0:1], None, op0=ALU.mult), accB + [oB[s]], [oB[s]])
                        V(stt(osc[s][:], a2[:, 0:128], osm[s][:, 2:3], osc[s][:], ALU.mult, ALU.add), accB + [oB[s]], [oB[s]])
                        A(act(osq[:], osc[s][:], AF.Square, accum_out=osm[s][:, 3:4]), [oB[s]], [osqB, oB[s]])
                        V(ts(osm[s][:, 4:5], osm[s][:, 3:4], 1.0 / 128, 1e-6, op0=ALU.mult, op1=ALU.add), [oB[s]], [oB[s]])
                        A(act(osm[s][:, 5:6], osm[s][:, 4:5], AF.Sqrt), [oB[s]], [oB[s]])
                        V(recip(osm[s][:, 6:7], osm[s][:, 5:6]), [oB[s]], [oB[s]])
                        V(ts(osm[s][:, 7:8], osm[s][:, 6:7], 0.8, None, op0=ALU.mult), [oB[s]], [oB[s]])
                        V(stt(ab[s][:], osc[s][:], osm[s][:, 7:8], g_sb[:, h * 128:(h + 1) * 128], ALU.mult, ALU.mult),
                          [oB[s], gB], [abB[s]])
                        if cfg.dbg:
                            V(stt(af[s][:], osc[s][:], osm[s][:, 7:8], g_sb[:, h * 128:(h + 1) * 128], ALU.mult, ALU.mult),
                              [oB[s], gB], [afB[s]])
                            S.dma("sync", dbg_a[i * 128:(i + 1) * 128, h * 128:(h + 1) * 128], af[s][:], reads=[afB[s]])
                        pt_ = 0 + s
                        T([tr(bankb(pt_, 128, 0), ab[s][:], idb[:])], [abB[s], cbB], [pbuf[pt_]])
                        A(act(aT[:, h, i * 128:(i + 1) * 128], bankb(pt_, 128, 0), AF.Copy), [pbuf[pt_]], [aTB[i]])
        if cfg.sample:
            S.barrier()
            with Mark() as p1b:
                dc = sb("dc", [128, 1024], F32)
                dcB = Buf("dc")
                S.dma("sync", dc[:], dcst, writes=[dcB])
                D_IOP = 520
                D_MC = 528
                pti = sb("pti", [128, 64], I32)
                ptf = sb("ptf", [128, 64], F32)
                pidx = sb("pidx", [128, 64], I32)
                ptB = Buf("pt")
                qb = sb("qb", [128, 1024], F32)
                qbB = Buf("qb")
                Kpg = [sb("Kpg%d" % i, [128, 1024], F32) for i in range(2)]
                Vpg = [sb("Vpg%d" % i, [128, 1024], F32) for i in range(2)]
                KpB = [Buf("Kp0"), Buf("Kp1")]
                VpB = [Buf("Vp0"), Buf("Vp1")]
                Kx = sb("Kx", [128, 1024], F32)
                Vx = sb("Vx", [128, 1024], F32)
                KxB, VxB = Buf("Kx"), Buf("Vx")
                prod = sb("prod", [128, 1024], F32)
                prodB = Buf("prod")
                scs = sb("scs", [128, 65, 16], F32)
                Pall = sb("Pall", [128, 65, 16], F32)
                scB, PaB = Buf("scs"), Buf("Pall")
                dsm = sb("dsm", [128, 64], F32)
                dsmB = Buf("dsm")
                On = sb("On", [2, 1024], F32)
                OnB = Buf("On")
                osx = sb("osx", [NS, 1024], F32)
                osxq = sb("osxq", [NS, 1024], F32)
                asx = sb("asx", [NS, 1024], BF16)
                osxB, asxB = Buf("osx"), Buf("asx")
                V(mset(Kx[:], 0.0), [], [KxB])
                V(mset(Vx[:], 0.0), [], [VxB])
                V(ts(dsm[0:2, 0:1], sm[0:2, 1:2], -1.0, None, op0=ALU.add), [smB], [dsmB])
                V(ts(dsm[0:2, 1:2], dc[0:2, D_IOP:D_IOP + 1], dsm[0:2, 0:1], 1.0, op0=ALU.mult, op1=ALU.add), [dsmB, dcB], [dsmB])
                V(ts(dsm[0:2, 16:32], dc[0:2, D_MC:D_MC + 16], dsm[0:2, 1:2], None, op0=ALU.mult), [dsmB, dcB], [dsmB])
                V(mset(dsm[:, 32:33], 1.0), [], [dsmB])
                for si in range(NS):
                    S.dma("sync", pti[:], ptab[si:si + 1, :].partition_broadcast(128), writes=[ptB])
                    V(cp(ptf[:], pti[:]), [ptB], [ptB])
                    V(ts(ptf[:], ptf[:], 128.0, dc[:, D_IOP:D_IOP + 1], op0=ALU.mult, op1=ALU.add), [ptB, dcB], [ptB])
                    V(cp(pidx[:], ptf[:]), [ptB], [ptB])
                    S.dma("sync", qb[:], qsd[si:si + 1, :].partition_broadcast(128), reads=[smpB], writes=[qbB])
                    S.dma("sync", Kx[0:1, :], k_smp[si:si + 1, :], reads=[smpB], writes=[KxB])
                    S.dma("sync", Vx[0:1, :], v_smp[si:si + 1, :], reads=[smpB], writes=[VxB])
                    for pg in range(65):
                        s = pg % 2
                        if pg < 64:
                            S.custom_dma("gpsimd", lambda e, s=s, pg=pg: e.indirect_dma_start(
                                out=Kpg[s][:], out_offset=None, in_=cache_k,
                                in_offset=bass.IndirectOffsetOnAxis(ap=pidx[:, pg:pg + 1], axis=0)), reads=[ptB], writes=[KpB[s]])
                            S.custom_dma("gpsimd", lambda e, s=s, pg=pg: e.indirect_dma_start(
                                out=Vpg[s][:], out_offset=None, in_=cache_v,
                                in_offset=bass.IndirectOffsetOnAxis(ap=pidx[:, pg:pg + 1], axis=0)), reads=[ptB], writes=[VpB[s]])
                            kt_, kB_, vt_, vB_ = Kpg[s], KpB[s], Vpg[s], VpB[s]
                        else:
                            kt_, kB_, vt_, vB_ = Kx, KxB, Vx, VxB
                        V(tt(prod[:], kt_[:], qb[:], ALU.mult), [kB_, qbB], [prodB])
                        V(red(scs[:, pg, :], prod[:].rearrange("p (a b) -> p a b", b=64), ALU.add), [prodB], [scB])
                        V(tt(scs[:, pg, :].rearrange("p (h c) -> p h c", c=2), scs[:, pg, :].rearrange("p (h c) -> p h c", c=2),
                             dc[:, pg * 8:(pg + 1) * 8].unsqueeze(2).to_broadcast([128, 8, 2]), ALU.add), [scB, dcB], [scB])
                        A(act(Pall[:, pg, :], scs[:, pg, :], AF.Exp), [scB], [PaB])
                        T([mm(bank(h // 4, 128, (h % 4) * 128)[0:2, :], Pall[:, pg, h * 2:h * 2 + 2], vt_[:, h * 128:(h + 1) * 128],
                              start=(pg == 0 and h % 4 == 0), stop=(pg == 64), sgc=True) for h in range(8)], [PaB, vB_], [pbuf[0], pbuf[1]])
                    V(red(dsm[:, 40:56], Pall[:].rearrange("p g x -> p x g"), ALU.add), [PaB], [dsmB])
                    T([mm(bank(2, 1, h)[0:2, :], dsm[:, 40 + h * 2:42 + h * 2], dsm[:, 32:33], start=(h == 0), stop=(h == 7), sgc=True)
                       for h in range(8)], [dsmB], [pbuf[2]])
                    V(recip(dsm[0:2, 2:10], bank(2, 8, 0)[0:2, :]), [pbuf[2]], [dsmB])
                    for hb_ in range(2):
                        V(tt(On[:, hb_ * 512:(hb_ + 1) * 512].rearrange("p (h d) -> p h d", h=4),
                             bank(hb_, 512)[0:2, :].rearrange("p (h d) -> p h d", h=4),
                             dsm[0:2, 2 + hb_ * 4:6 + hb_ * 4].unsqueeze(2).to_broadcast([2, 4, 128]), ALU.mult), [pbuf[hb_], dsmB], [OnB])
                    T([mm(bank(3 + hb_, 512)[0:NS, :], dsm[0:2, 16 + si * 4:20 + si * 4], On[:, hb_ * 512:(hb_ + 1) * 512],
                          start=(si == 0), stop=(si == NS - 1)) for hb_ in range(2)], [dsmB, OnB], [pbuf[3], pbuf[4]])
                rr = slice(0, NS)
                for hb_ in range(2):
                    V(cp(osx[rr, hb_ * 512:(hb_ + 1) * 512], bank(3 + hb_, 512)[rr, :]), [pbuf[3 + hb_]], [osxB])
                A(act(osxq[rr, :], osx[rr, :], AF.Square), [osxB], [osxB])
                V(red(dsm[rr, 56:64], osxq[rr, :].rearrange("p (h d) -> p h d", h=8), ALU.add), [osxB], [dsmB])
                V(ts(dsm[rr, 56:64], dsm[rr, 56:64], 1.0 / 128, 1e-6, op0=ALU.mult, op1=ALU.add), [dsmB], [dsmB])
                A(act(dsm[rr, 56:64], dsm[rr, 56:64], AF.Sqrt), [dsmB], [dsmB])
                V(recip(dsm[rr, 56:64], dsm[rr, 56:64]), [dsmB], [dsmB])
                V(ts(dsm[rr, 56:64], dsm[rr, 56:64], 0.8, None, op0=ALU.mult), [dsmB], [dsmB])
                V(tt(osx[rr, :].rearrange("p (h d) -> p h d", h=8), osx[rr, :].rearrange("p (h d) -> p h d", h=8),
                     dsm[rr, 56:64].unsqueeze(2).to_broadcast([NS, 8, 128]), ALU.mult), [osxB, dsmB], [osxB])
                V(tt(asx[rr, :], osx[rr, :], g_sb[rr, :], ALU.mult), [osxB, gB], [asxB])
                if cfg.dbg:
                    V(tt(osxq[rr, :], osx[rr, :], g_sb[rr, :], ALU.mult), [osxB, gB], [osxB])
                    S.dma("sync", dbg_a[TO:TO + NS, :], osxq[rr, :], reads=[osxB])
                T([tr(bankb(5, NS, k * 128), asx[rr, k * 128:(k + 1) * 128], idb[rr, rr]) for k in range(8)], [asxB, cbB], [pbuf[5]])
                A(act(aT[:, :, TO:TO + NS], bankb(5, 1024).rearrange("p (k n) -> p k n", k=8)[:, :, 0:NS], AF.Copy), [pbuf[5]], [aTB[NO]])
        S.barrier()

        NC = NP + NO
        S.dma("sync", g_sb[:], prow[:, 3328:4352].partition_broadcast(128), writes=[gB])
        with Mark() as p2:
            gc = sb("gc", [128, 8], F32)
            gcB = Buf("gc")
            S.dma("sync", gc[:], gcol, writes=[gcB])
            wip = sb("wip", [128, 8, 252], BF16)
            wfp = sb("wfp", [128, 8, 252], BF16)
            wpB = Buf("wpad")
            G(mset(wip[:], 0.0), [], [wpB])
            G(mset(wfp[:], 0.0), [], [wpB])
            S.dma("gpsimd", wip[:, :, 124:128], w_in[:, 6144:6148].rearrange("(k p) c -> p k c", p=128), writes=[wpB])
            S.dma("gpsimd", wfp[:, :, 124:128], w_in[:, 6148:6152].rearrange("(k p) c -> p k c", p=128), writes=[wpB])
            wm = [sb("wm%d" % i, [128, 8, 768], BF16) for i in range(2)]
            wmB = [Buf("wm0"), Buf("wm1")]

            def load_wm(h):
                s_ = h % 2
                for (c0, n_, o_) in ((3072 + h * 128, 128, 0), (3584 + h * 128, 128, 128), (4096 + h * 256, 256, 256), (5120 + h * 256, 256, 512)):
                    S.dma("gpsimd", wm[s_][:, :, o_:o_ + n_], w_in[:, c0:c0 + n_].rearrange("(k p) c -> p k c", p=128), writes=[wmB[s_]])

            load_wm(0)

            def xchunk(c):
                if c < NP:
                    return [xnT_pre[:, k, c * 128:(c + 1) * 128] for k in range(8)], xnB_pre[c]
                return [xnT_own[:, k, (c - NP) * 128:(c - NP + 1) * 128] for k in range(8)], xnB_own[c - NP]

            gt = sb("gt", [128, 16, 128], F32)
            gtB = Buf("gt")
            fns = []
            for c in range(NC):
                lhs, rB = xchunk(c)
                for k in range(8):
                    fns.append(mm(bank(0, 128, 0), wip[:, k, 124 - 4 * c:252 - 4 * c], lhs[k], start=(c == 0 and k == 0),
                                  stop=(c == NC - 1 and k == 7), sgc=True))
                    fns.append(mm(bank(0, 128, 128), wfp[:, k, 124 - 4 * c:252 - 4 * c], lhs[k], start=False,
                                  stop=(c == NC - 1 and k == 7), sgc=True))
            T(fns, xnB_pre + xnB_own[:NO] + [wpB], [pbuf[0]])
            IG, U_, E_, L_, LF, FL, Fg, Gg, WL, W_, T1, FLO = [gt[:, i, :] for i in range(12)]
            ZER = gt[:, 12, :]
            V(mset(ZER, 0.0), [], [gtB])
            A(act(IG, bank(0, 128, 0), AF.Identity, bias=gc[:, 0:1]), [pbuf[0], gcB], [gtB])
            V(ts(IG, IG, gc[:, 2:3], gc[:, 3:4], op0=ALU.mult, op1=ALU.add), [gtB, gcB], [gtB])
            A(act(U_, bank(0, 128, 128), AF.Identity, bias=gc[:, 1:2]), [pbuf[0], gcB], [gtB])
            A(act(E_, U_, AF.Exp, scale=-1.0), [gtB], [gtB])
            A(act(L_, E_, AF.Ln, bias=1.0), [gtB], [gtB])
            V(ts(LF, L_, gc[:, 4:5], None, op0=ALU.mult), [gtB, gcB], [gtB])
            V(lambda e: e.tensor_tensor_scan(out=FL, data0=LF, data1=ZER, initial=0.0, op0=ALU.add, op1=ALU.add), [gtB], [gtB])
            T([mm(bank(1, 1, 0), cst_sb[:, 1152:1280], FL[:, 127:128])], [gtB, cstB], [pbuf[1]])
            gs = sb("gs", [128, 16], F32)
            gsB = Buf("gs")
            V(cp(gs[:, 0:1], bank(1, 1, 0)), [pbuf[1]], [gsB])
            V(ts(Fg, FL, gs[:, 0:1], None, op0=ALU.add), [gtB, gsB], [gtB])
            V(tt(Gg, IG, Fg, ALU.subtract), [gtB], [gtB])
            V(red(gs[:, 1:2], Gg, ALU.max), [gtB], [gsB])
            grow = sb("grow", [1, 8, 128], F32)
            growB = Buf("grow")
            T([tr(bank(1, 128, 128)[0:1, :], gs[:, 1:2], ident_f)], [gsB, cstB], [pbuf[1]])
            V(cp(grow[0:1, 0, :], bank(1, 128, 128)[0:1, :]), [pbuf[1]], [growB])
            for h in range(4):
                gv = grow[0:1, 0, :].rearrange("p (c h) -> p h c", h=4)[:, h, :]
                mv = grow[0:1, 1, :].rearrange("p (c h) -> p h c", h=4)[:, h, :]
                V(lambda e, gv=gv, mv=mv: e.tensor_tensor_scan(out=mv, data0=gv, data1=gv, initial=0.0, op0=ALU.max, op1=ALU.max),
                  [growB], [growB])
            V(mset(grow[0:1, 2, 0:4], 0.0), [], [growB])
            V(cp(grow[0:1, 2, 4:128], grow[0:1, 1, 0:124]), [growB], [growB])
            V(tt(grow[0:1, 3, :], grow[0:1, 2, :], grow[0:1, 1, :], ALU.subtract), [growB], [growB])
            A(act(grow[0:1, 4, :], grow[0:1, 3, :], AF.Exp), [growB], [growB])
            V(mset(grow[0:1, 5, :], 1.0), [], [growB])
            T([mm(bank(1, 1, 256), grow[0:1, 2, :], grow[0:1, 5, 0:1]),
               mm(bank(1, 1, 257), grow[0:1, 1, :], grow[0:1, 5, 0:1]),
               mm(bank(1, 128, 384), grow[0:1, 5, :], grow[0:1, 4, :])], [growB], [pbuf[1]])
            V(cp(gs[:, 2:4], bank(1, 2, 256)), [pbuf[1]], [gsB])
            decb = sb("decb", [128, 128], F32)
            decB = Buf("dec")
            V(cp(decb[:], bank(1, 128, 384)), [pbuf[1]], [decB])
            V(ts(WL, Gg, gs[:, 2:3], None, op0=ALU.subtract), [gtB, gsB], [gtB])
            A(act(W_, WL, AF.Exp), [gtB], [gtB])
            V(ts(T1, Fg, gs[:, 2:3], None, op0=ALU.add), [gtB, gsB], [gtB])
            A(act(FLO, T1, AF.Exp, scale=-1.0), [gtB], [gtB])
            V(tt(gs[:, 4:5], Fg[:, 127:128], gs[:, 3:4], ALU.add), [gtB, gsB], [gsB])
            S.dma("sync", m_out, gs[(NC - 1) * 4:(NC - 1) * 4 + 4, 4:5], reads=[gsB])
            wT = sb("wT", [128, 128], F32)
            flT = sb("flT", [128, 128], F32)
            wTB = Buf("wT")
            T([tr(bank(2, 128, 0), W_, ident_f), tr(bank(2, 128, 128), FLO, ident_f)], [gtB, cstB], [pbuf[2]])
            V(cp(wT[:], bank(2, 128, 0)), [pbuf[2]], [wTB])
            V(cp(flT[:], bank(2, 128, 128)), [pbuf[2]], [wTB])

            Cf = sb("Cf", [128, 257], F32)
            Ct = sb("Ct", [128, 257], F32)
            Cb = sb("Cb", [128, 257], BF16)
            CfB, CtB, CbB = Buf("Cf"), Buf("Ct"), Buf("Cb")
            Kb = [sb("Kb%d" % i, [128, 128], BF16) for i in range(2)]
            Vw = [sb("Vw%d" % i, [128, 257], BF16) for i in range(2)]
            QTb = [sb("QTb%d" % i, [128, 128], BF16) for i in range(2)]
            KTb = [sb("KTb%d" % i, [128, 128], BF16) for i in range(2)]
            PTm = [sb("PTm%d" % i, [128, 128], BF16) for i in range(2)]
            hh = [sb("hh%d" % i, [128, 256], F32) for i in range(2)]
            og = [sb("og%d" % i, [128, 256], F32) for i in range(2)]
            hbb = [sb("hbb%d" % i, [128, 256], BF16) for i in range(2)]
            hsq = sb("hsq", [128, 256], F32)
            hsm = [sb("hsm%d" % i, [128, 8], F32) for i in range(2)]
            KbB, VwB, QKB, PTB, hhB, ogB, hbB, hsB = [[Buf(n_ + "0"), Buf(n_ + "1")] for n_ in
                                                     ("Kb", "Vw", "QK", "PT", "hh", "og", "hb", "hs")]
            hsqB = Buf("hsq")
            if cfg.dbg:
                hbf = [sb("hbf%d" % i, [128, 256], F32) for i in range(2)]
                hbfB = [Buf(), Buf()]
            m01 = cst_sb[:, 1024:1152]
            for h in range(4 if 'mh' not in cfg.skip else 0):
                if h + 1 < 4:
                    load_wm(h + 1)
                ws = h % 2
                V(mset(Cf[:], 0.0), [], [CfB])
                G(mset(Cb[:], 0.0), [], [CbB])
                for c in range(NC):
                    s = c % 2
                    own = c >= NP
                    i = c - NP
                    lhs, rB = xchunk(c)
                    col = c * 4 + h
                    pA, pB = (0, 1) if s == 0 else (2, 3)
                    T([mm(bank(pA, 384), lhs[k], wm[ws][:, k, 128:512], start=(k == 0), stop=(k == 7)) for k in range(8)],
                      [rB, wmB[ws]], [pbuf[pA]])
                    A(act(Kb[s][:], bank(pA, 128, 0), AF.Copy, scale=float(128 ** -0.5)), [pbuf[pA]], [KbB[s]])
                    V(ts(Vw[s][:, 0:256], bank(pA, 256, 128), wT[:, col:col + 1], None, op0=ALU.mult), [pbuf[pA], wTB], [VwB[s]])
                    V(cp(Vw[s][:, 256:257], wT[:, col:col + 1]), [wTB], [VwB[s]])
                    if own and 'own2' not in cfg.skip:
                        T([mm(bank(pB, 256), lhs[k], wm[ws][:, k, 512:768], start=(k == 0), stop=(k == 7)) for k in range(8)],
                          [rB, wmB[ws]], [pbuf[pB]])
                        A(act(og[s][:], bank(pB, 256), AF.Sigmoid), [pbuf[pB]], [ogB[s]])
                        T([mm(bank(4, 128, 0), wm[ws][:, k, 0:128], lhs[k], start=(k == 0), stop=(k == 7)) for k in range(8)] +
                          [mm(bank(4, 128, 128), wm[ws][:, k, 128:256], lhs[k], start=(k == 0), stop=(k == 7)) for k in range(8)],
                          [rB, wmB[ws]], [pbuf[4]])
                        A(act(QTb[s][:], bank(4, 128, 0), AF.Copy), [pbuf[4]], [QKB[s]])
                        A(act(KTb[s][:], bank(4, 128, 128), AF.Copy, scale=float(128 ** -0.5)), [pbuf[4]], [QKB[s]])
                        T([mm(bank(5, 128), KTb[s][:], QTb[s][:])], [QKB[s]], [pbuf[5]])
                        V(tt(PTm[s][:], bank(5, 128), m01, ALU.mult), [pbuf[5], cstB], [PTB[s]])
                        T([mm(bank(6, 257), QTb[s][:], Cb[:], start=True, stop=False),
                           mm(bank(6, 257), PTm[s][:], Vw[s][:], start=False, stop=True)], [QKB[s], CbB, PTB[s], VwB[s]], [pbuf[6]])
                        A(act(hsm[s][:, 6:7], bank(6, 1, 256), AF.Abs), [pbuf[6]], [hsB[s]])
                        V(ts(hsm[s][:, 0:1], hsm[s][:, 6:7], flT[:, col:col + 1], None, op0=ALU.max), [hsB[s], wTB], [hsB[s]])
                        V(recip(hsm[s][:, 1:2], hsm[s][:, 0:1]), [hsB[s]], [hsB[s]])
                        V(ts(hh[s][:], bank(6, 256, 0), hsm[s][:, 1:2], None, op0=ALU.mult), [pbuf[6], hsB[s]], [hhB[s]])
                        A(act(hsq[:], hh[s][:], AF.Square, accum_out=hsm[s][:, 2:3]), [hhB[s]], [hsqB, hsB[s]])
                        V(ts(hsm[s][:, 3:4], hsm[s][:, 2:3], 1.0 / 256, 1e-6, op0=ALU.mult, op1=ALU.add), [hsB[s]], [hsB[s]])
                        A(act(hsm[s][:, 4:5], hsm[s][:, 3:4], AF.Sqrt), [hsB[s]], [hsB[s]])
                        V(recip(hsm[s][:, 5:6], hsm[s][:, 4:5]), [hsB[s]], [hsB[s]])
                        V(stt(hh[s][:], hh[s][:], hsm[s][:, 5:6], g_sb[:, h * 256:(h + 1) * 256], ALU.mult, ALU.mult),
                          [hhB[s], hsB[s], gB], [hhB[s]])
                        V(tt(hbb[s][:], hh[s][:], og[s][:], ALU.mult), [hhB[s], ogB[s]], [hbB[s]])
                        if cfg.dbg:
                            V(tt(hbf[s][:], hh[s][:], og[s][:], ALU.mult), [hhB[s], ogB[s]], [hbfB[s]])
                            S.dma("sync", dbg_hb[i * 128:(i + 1) * 128, h * 256:(h + 1) * 256], hbf[s][:], reads=[hbfB[s]])
                        T([tr(bankb(5, 128, 512 + j * 128), hbb[s][:, j * 128:(j + 1) * 128], idb[:]) for j in range(2)],
                          [hbB[s], cbB], [pbuf[5]])
                        A(act(hbT[:, h * 2:h * 2 + 2, i * 128:(i + 1) * 128],
                              bankb(5, 256, 512).rearrange("p (j n) -> p j n", j=2), AF.Copy), [pbuf[5]], [hbTB[i]])
                    if 'state' in cfg.skip:
                        continue
                    T([mm(bank(7, 257), Kb[s][:], Vw[s][:])], [KbB[s], VwB[s]], [pbuf[7]])
                    V(tt(Ct[:], bank(7, 257), Cf[:], ALU.add), [pbuf[7], CfB], [CtB])
                    V(ts(Cf[:], Ct[:], decb[:, col:col + 1], None, op0=ALU.mult), [CtB, decB], [CfB])
                    A(act(Cb[:], Cf[:], AF.Copy), [CfB], [CbB])
                S.dma("sync", C_out[h], Cf[:, 0:256], reads=[CfB])
                S.dma("sync", n_out[h].rearrange("(p o) -> p o", o=1), Cf[:, 256:257], reads=[CfB])
        S.barrier()

        if cfg.sample:
            with Mark() as p2b:
                rr = slice(0, NS)
                xs_ = [xnT_own[:, k, TO:TO + NS] for k in range(8)]
                wms = sb("wms", [128, 8, 768], BF16)
                wmsB = Buf("wms")
                wif = sb("wif", [128, 8, 8], BF16)
                wifB = Buf("wif")
                S.dma("gpsimd", wif[:], w_in[:, 6144:6152].rearrange("(k p) c -> p k c", p=128), writes=[wifB])
                bro = sb("bro", [NS, 8], F32)
                m0t = sb("m0t", [NS, 4], F32)
                n0t = sb("n0t", [NS, 512], F32)
                nnt = sb("nnt", [NS, 512], F32)
                ldB = Buf("s_ld")
                S.dma("sync", bro[:], brow.partition_broadcast(NS), writes=[ldB])
                S.dma("sync", m0t[:], smm, writes=[ldB])
                S.dma("sync", n0t[:], sn, writes=[ldB])
                dc2 = sb("dc2", [128, 64], F32)
                dc2B = Buf("dc2")
                S.dma("sync", dc2[:], dcst[:, 512:576], writes=[dc2B])
                g4 = sb("g4", [NS, 64], F32)
                g4B = Buf("g4")
                nnB = Buf("nnt")
                T([mm(bank(2, 8)[rr, :], xs_[k], wif[:, k, :], start=(k == 0), stop=(k == 7)) for k in range(8)], [xnB_own[NO], wifB], [pbuf[2]])
                V(tt(g4[:, 0:4], bank(2, 4, 0)[rr, :], bro[:, 0:4], ALU.add), [pbuf[2], ldB], [g4B])
                V(tt(g4[:, 4:8], bank(2, 4, 4)[rr, :], bro[:, 4:8], ALU.add), [pbuf[2], ldB], [g4B])
                A(act(g4[:, 4:8], g4[:, 4:8], AF.Exp, scale=-1.0), [g4B], [g4B])
                A(act(g4[:, 4:8], g4[:, 4:8], AF.Ln, bias=1.0), [g4B], [g4B])
                V(tt(g4[:, 8:12], m0t[:], g4[:, 4:8], ALU.subtract), [g4B, ldB], [g4B])
                V(tt(g4[:, 12:16], g4[:, 8:12], g4[:, 0:4], ALU.max), [g4B], [g4B])
                V(tt(g4[:, 16:20], g4[:, 0:4], g4[:, 12:16], ALU.subtract), [g4B], [g4B])
                A(act(g4[:, 16:20], g4[:, 16:20], AF.Exp), [g4B], [g4B])
                V(tt(g4[:, 20:24], g4[:, 8:12], g4[:, 12:16], ALU.subtract), [g4B], [g4B])
                A(act(g4[:, 20:24], g4[:, 20:24], AF.Exp), [g4B], [g4B])
                A(act(g4[:, 24:28], g4[:, 12:16], AF.Exp, scale=-1.0), [g4B], [g4B])
                S.dma("sync", m_smp, g4[:, 12:16], reads=[g4B])
                Bsel = sb("Bsel", [NS, NS, 128], F32)
                BselB = Buf("Bsel")
                for si in range(NS):
                    V(cp(Bsel[:, si, :], ident_f[0:NS, si:si + 1].to_broadcast([NS, 128])), [cstB], [BselB])
                T([mm(bank(3, 4, si * 4), Bsel[:, si, :], g4[:, 20:24]) for si in range(NS)], [BselB, g4B], [pbuf[3]])
                abc = sb("abc", [128, 16], F32)
                abcB = Buf("abc")
                V(cp(abc[:], bank(3, 16, 0)), [pbuf[3]], [abcB])
                qs = sb("qs", [NS, 128], F32)
                ks = sb("ks", [NS, 128], F32)
                vs = sb("vs", [NS, 256], F32)
                os_ = sb("os_", [NS, 256], F32)
                tq = sb("tq", [NS, 128], F32)
                t2_ = sb("t2_", [NS, 256], F32)
                h4 = sb("h4", [NS, 256], F32)
                h4q = sb("h4q", [NS, 256], F32)
                hb4 = sb("hb4", [NS, 256], BF16)
                qTs = sb("qTs", [128, NS], F32)
                qTm = sb("qTm", [128, NS, NS], F32)
                kwm = sb("kwm", [NS, NS, 128], F32)
                C0t = [sb("C0t%d" % i, [128, 256], F32) for i in range(NS)]
                Cn = [sb("Cn%d" % i, [128, 256], F32) for i in range(2)]
                qsB, vsB, osB, tqB, h4B, hb4B, qTB_, kwmB = [Buf(n_) for n_ in ("qs", "vs", "os", "tq", "h4", "hb4", "qTs", "kwm")]
                C0B = [Buf("C0t%d" % i) for i in range(NS)]
                CnB = [Buf("Cn0"), Buf("Cn1")]
                ncn = 0
                for h in range(4):
                    for (c0_, n_, o_) in ((3072 + h * 128, 128, 0), (3584 + h * 128, 128, 128), (4096 + h * 256, 256, 256), (5120 + h * 256, 256, 512)):
                        S.dma("gpsimd", wms[:, :, o_:o_ + n_], w_in[:, c0_:c0_ + n_].rearrange("(k p) c -> p k c", p=128), writes=[wmsB])
                    for si in range(NS):
                        S.dma("sync", C0t[si][:], sC[si * 4 + h], writes=[C0B[si]])
                    T([mm(bank(0, 512)[rr, :], xs_[k], wms[:, k, 0:512], start=(k == 0), stop=(k == 7)) for k in range(8)], [xnB_own[NO], wmsB], [pbuf[0]])
                    T([mm(bank(1, 256)[rr, :], xs_[k], wms[:, k, 512:768], start=(k == 0), stop=(k == 7)) for k in range(8)], [xnB_own[NO], wmsB], [pbuf[1]])
                    T([mm(bank(4, NS, 0), wms[:, k, 0:128], xs_[k], start=(k == 0), stop=(k == 7)) for k in range(8)], [xnB_own[NO], wmsB], [pbuf[4]])
                    V(cp(qs[:], bank(0, 128, 0)[rr, :]), [pbuf[0]], [qsB])
                    V(ts(ks[:], bank(0, 128, 128)[rr, :], float(128 ** -0.5), None, op0=ALU.mult), [pbuf[0]], [qsB])
                    V(cp(vs[:], bank(0, 256, 256)[rr, :]), [pbuf[0]], [vsB])
                    A(act(os_[:], bank(1, 256)[rr, :], AF.Sigmoid), [pbuf[1]], [osB])
                    V(cp(qTs[:], bank(4, NS, 0)), [pbuf[4]], [qTB_])
                    V(tt(qTm[:], qTs[:].unsqueeze(1).to_broadcast([128, NS, NS]), dc2[:, 48:64].rearrange("p (a b) -> p a b", a=NS), ALU.mult),
                      [qTB_, dc2B], [qTB_])
                    V(tt(tq[:], qs[:], ks[:], ALU.mult), [qsB], [tqB])
                    V(red(g4[:, 28 + h:29 + h], tq[:], ALU.add), [tqB], [g4B])
                    V(tt(tq[:], qs[:], n0t[:, h * 128:(h + 1) * 128], ALU.mult), [qsB, ldB], [tqB])
                    V(red(g4[:, 32 + h:33 + h], tq[:], ALU.add), [tqB], [g4B])
                    V(tt(g4[:, 36 + h:37 + h], g4[:, 28 + h:29 + h], g4[:, 16 + h:17 + h], ALU.mult), [g4B], [g4B])
                    T([mm(bank(5, 256)[rr, :], qTm[:, si, :], C0t[si][:], start=(si == 0), stop=(si == NS - 1)) for si in range(NS)],
                      [qTB_] + C0B, [pbuf[5]])
                    V(ts(t2_[:], bank(5, 256)[rr, :], g4[:, 20 + h:21 + h], None, op0=ALU.mult), [pbuf[5], g4B], [h4B])
                    V(stt(h4[:], vs[:], g4[:, 36 + h:37 + h], t2_[:], ALU.mult, ALU.add), [vsB, g4B, h4B], [h4B])
                    V(stt(g4[:, 40 + h:41 + h], g4[:, 32 + h:33 + h], g4[:, 20 + h:21 + h], g4[:, 36 + h:37 + h], ALU.mult, ALU.add), [g4B], [g4B])
                    A(act(g4[:, 44 + h:45 + h], g4[:, 40 + h:41 + h], AF.Abs), [g4B], [g4B])
                    V(tt(g4[:, 44 + h:45 + h], g4[:, 44 + h:45 + h], g4[:, 24 + h:25 + h], ALU.max), [g4B], [g4B])
                    V(recip(g4[:, 48 + h:49 + h], g4[:, 44 + h:45 + h]), [g4B], [g4B])
                    V(ts(h4[:], h4[:], g4[:, 48 + h:49 + h], None, op0=ALU.mult), [h4B, g4B], [h4B])
                    A(act(h4q[:], h4[:], AF.Square, accum_out=g4[:, 52 + h:53 + h]), [h4B], [h4B, g4B])
                    V(ts(g4[:, 56 + h:57 + h], g4[:, 52 + h:53 + h], 1.0 / 256, 1e-6, op0=ALU.mult, op1=ALU.add), [g4B], [g4B])
                    A(act(g4[:, 56 + h:57 + h], g4[:, 56 + h:57 + h], AF.Sqrt), [g4B], [g4B])
                    V(recip(g4[:, 60 + h:61 + h], g4[:, 56 + h:57 + h]), [g4B], [g4B])
                    V(stt(h4[:], h4[:], g4[:, 60 + h:61 + h], g_sb[rr, h * 256:(h + 1) * 256], ALU.mult, ALU.mult), [h4B, g4B, gB], [h4B])
                    V(tt(hb4[:], h4[:], os_[:], ALU.mult), [h4B, osB], [hb4B])
                    if cfg.dbg:
                        V(tt(h4q[:], h4[:], os_[:], ALU.mult), [h4B, osB], [h4B])
                        S.dma("sync", dbg_hb[TO:TO + NS, h * 256:(h + 1) * 256], h4q[:], reads=[h4B])
                    T([tr(bankb(6, NS, j * 128), hb4[:, j * 128:(j + 1) * 128], idb[rr, rr]) for j in range(2)], [hb4B, cbB], [pbuf[6]])
                    A(act(hbT[:, h * 2:h * 2 + 2, TO:TO + NS], bankb(6, 256).rearrange("p (j n) -> p j n", j=2)[:, :, 0:NS], AF.Copy),
                      [pbuf[6]], [hbTB[NO]])
                    for si in range(NS):
                        V(ts(kwm[:, si, :], ks[:], ident_f[0:NS, si:si + 1], g4[:, 16 + h:17 + h], op0=ALU.mult, op1=ALU.mult), [qsB, cstB, g4B], [kwmB])
                    for si in range(NS):
                        cs_ = ncn % 2
                        ncn += 1
                        T([mm(bank(6 + cs_, 256) if False else bank(2 + cs_, 256), kwm[:, si, :], vs[:])], [kwmB, vsB], [pbuf[2 + cs_]])
                        V(stt(Cn[cs_][:], C0t[si][:], abc[:, si * 4 + h:si * 4 + h + 1], bank(2 + cs_, 256), ALU.mult, ALU.add),
                          [C0B[si], abcB, pbuf[2 + cs_]], [CnB[cs_]])
                        S.dma("sync", C_smp[si * 4 + h], Cn[cs_][:], reads=[CnB[cs_]])
                    V(ts(tq[:], ks[:], g4[:, 16 + h:17 + h], None, op0=ALU.mult), [qsB, g4B], [tqB])
                    V(stt(nnt[:, h * 128:(h + 1) * 128], n0t[:, h * 128:(h + 1) * 128], g4[:, 20 + h:21 + h], tq[:], ALU.mult, ALU.add),
                      [ldB, g4B, tqB], [nnB])
                S.dma("sync", n_smp, nnt[:], reads=[nnB])
            S.barrier()

        mtiles = [(i, 128, i * 128) for i in range(NO)]
        if cfg.sample and 'smerge' not in cfg.skip:
            mtiles.append((NO, NS, TO))
        x1B = [Buf("x1d%d" % i) for i in range(NO + 1)]
        apos[0] = pre_off
        with Mark() as p3a:
            wg = sb("wg", [128, 8, 2048], BF16)
            wa = sb("wa", [128, 8, 1024], BF16)
            wb_ = sb("wb_", [128, 8, 1024], BF16)
            wgB, waB, wbB = Buf("wg"), Buf("wa"), Buf("wb")
            for q4 in range(4):
                S.dma("gpsimd", wg[:, :, q4 * 512:(q4 + 1) * 512],
                      w_in[:, 6152 + q4 * 512:6152 + (q4 + 1) * 512].rearrange("(k p) c -> p k c", p=128), writes=[wgB])
            for q2 in range(2):
                S.dma("gpsimd", wa[:, :, q2 * 512:(q2 + 1) * 512], w_a[:, q2 * 512:(q2 + 1) * 512].rearrange("(k p) c -> p k c", p=128), writes=[waB])
                S.dma("gpsimd", wb_[:, :, q2 * 512:(q2 + 1) * 512], w_b[:, q2 * 512:(q2 + 1) * 512].rearrange("(k p) c -> p k c", p=128), writes=[wbB])
            sg = sb("sg", [128, 2048], F32)
            t1 = sb("t1", [128, 1024], F32)
            t2 = sb("t2", [128, 1024], F32)
            ub = sb("ub", [128, 1024], BF16)
            sgB, t1B, t2B, ubB = Buf("sg"), Buf("t1"), Buf("t2"), Buf("ub")
            for (t, rows, c0) in mtiles:
                xl = [xnT_own[:, k, c0:c0 + rows] for k in range(8)]
                for q4 in range(4):
                    T([mm(bank(q4)[0:rows, :], xl[k], wg[:, k, q4 * 512:(q4 + 1) * 512], start=(k == 0), stop=(k == 7)) for k in range(8)],
                      [xnB_own[t], wgB], [pbuf[q4]])
                    A(act(sg[0:rows, q4 * 512:(q4 + 1) * 512], bank(q4)[0:rows, :], AF.Sigmoid), [pbuf[q4]], [sgB])
                for q2 in range(2):
                    T([mm(bank(4 + q2)[0:rows, :], aT[:, k, c0:c0 + rows], wa[:, k, q2 * 512:(q2 + 1) * 512], start=(k == 0), stop=(k == 7))
                       for k in range(8)], [aTB[t], waB], [pbuf[4 + q2]])
                    T([mm(bank(6 + q2)[0:rows, :], hbT[:, k, c0:c0 + rows], wb_[:, k, q2 * 512:(q2 + 1) * 512], start=(k == 0), stop=(k == 7))
                       for k in range(8)], [hbTB[t], wbB], [pbuf[6 + q2]])
                    V(tt(t1[0:rows, q2 * 512:(q2 + 1) * 512], bank(4 + q2)[0:rows, :], sg[0:rows, q2 * 512:(q2 + 1) * 512], ALU.mult),
                      [pbuf[4 + q2], sgB], [t1B])
                    V(tt(t2[0:rows, q2 * 512:(q2 + 1) * 512], bank(6 + q2)[0:rows, :], sg[0:rows, 1024 + q2 * 512:1024 + (q2 + 1) * 512], ALU.mult),
                      [pbuf[6 + q2], sgB], [t2B])
                G(tt(ub[0:rows, :], t1[0:rows, :], t2[0:rows, :], ALU.add), [t1B, t2B], [ubB])
                T([tr(bankb(4, rows, k * 128), ub[0:rows, k * 128:(k + 1) * 128], idb[0:rows, 0:rows]) for k in range(8)], [ubB, cbB], [pbuf[4]])
                A(act(aT[:, :, c0:c0 + rows], bankb(4, 1024).rearrange("p (k n) -> p k n", k=8)[:, :, 0:rows], AF.Copy), [pbuf[4]], [aTB[t]])
        S.barrier()
        S.dma("sync", g_sb[:], prow[:, 1024:2048].partition_broadcast(128), writes=[gB])
        apos[0] = pre_off
        with Mark() as p3b:
            wo = sb("wo", [128, 8, 1024], BF16)
            woB = Buf("wo")
            for q2 in range(2):
                S.dma("gpsimd", wo[:, :, q2 * 512:(q2 + 1) * 512], w_o[:, q2 * 512:(q2 + 1) * 512].rearrange("(k p) c -> p k c", p=128), writes=[woB])
            xr = [sb("xr%d" % i, [128, 1024], F32) for i in range(2)]
            x1 = [sb("x1_%d" % i, [128, 1024], F32) for i in range(2)]
            xq = sb("xq", [128, 1024], F32)
            xnb = [sb("xnb%d" % i, [128, 1024], BF16) for i in range(2)]
            r2 = [sb("r2_%d" % i, [128, 4], F32) for i in range(2)]
            xrB, x1sB, xnbB, r2B = [[Buf(n_ + "0"), Buf(n_ + "1")] for n_ in ("xr", "x1s", "xnb", "r2")]
            xqB = Buf("xq")
            for n, (t, rows, c0) in enumerate(mtiles):
                s = n % 2
                src = x_smp[0:rows, :] if t == NO else x_own[c0:c0 + rows, :]
                S.dma("sync", xr[s][0:rows, :], src, writes=[xrB[s]])
                for q2 in range(2):
                    T([mm(bank(q2)[0:rows, :], aT[:, k, c0:c0 + rows], wo[:, k, q2 * 512:(q2 + 1) * 512], start=(k == 0), stop=(k == 7))
                       for k in range(8)], [aTB[t], woB], [pbuf[q2]])
                    V(tt(x1[s][0:rows, q2 * 512:(q2 + 1) * 512], bank(q2)[0:rows, :], xr[s][0:rows, q2 * 512:(q2 + 1) * 512], ALU.add),
                      [pbuf[q2], xrB[s]], [x1sB[s]])
                S.dma("sync", x1d[c0:c0 + rows, :], x1[s][0:rows, :], reads=[x1sB[s]], writes=[x1B[t]])
                if cfg.dbg:
                    S.dma("sync", dbg_x1[c0:c0 + rows, :], x1[s][0:rows, :], reads=[x1sB[s]])
                A(act(xq[0:rows, :], x1[s][0:rows, :], AF.Square, accum_out=r2[s][0:rows, 0:1]), [x1sB[s]], [xqB, r2B[s]])
                V(ts(r2[s][0:rows, 1:2], r2[s][0:rows, 0:1], 1.0 / 1024, 1e-6, op0=ALU.mult, op1=ALU.add), [r2B[s]], [r2B[s]])
                A(act(r2[s][0:rows, 2:3], r2[s][0:rows, 1:2], AF.Sqrt), [r2B[s]], [r2B[s]])
                V(recip(r2[s][0:rows, 3:4], r2[s][0:rows, 2:3]), [r2B[s]], [r2B[s]])
                V(stt(xnb[s][0:rows, :], x1[s][0:rows, :], r2[s][0:rows, 3:4], g_sb[0:rows, :], ALU.mult, ALU.mult),
                  [x1sB[s], r2B[s], gB], [xnbB[s]])
                pbk = 2 + s
                T([tr(bankb(pbk, rows, k * 128), xnb[s][0:rows, k * 128:(k + 1) * 128], idb[0:rows, 0:rows]) for k in range(8)],
                  [xnbB[s], cbB], [pbuf[pbk]])
                A(act(xnT_own[:, :, c0:c0 + rows], bankb(pbk, 1024).rearrange("p (k n) -> p k n", k=8)[:, :, 0:rows], AF.Copy),
                  [pbuf[pbk]], [xnB_own[t]])
        S.barrier()

        if cfg.peer:
            precast(128)
            xn2T = xnT_own
            TPE = TO + (NS if (cfg.sample and 'smerge' not in cfg.skip) else 0)
            C_IO16 = 1280
            C_IO128 = 1408
            apos[0] = aT_off
            I1T = sb("I1T", [128, TOS], F32)
            I2T = sb("I2T", [128, TOS], F32)
            gT = sb("gT", [128, TOS], F32)
            selB = [Buf("sel%d" % i) for i in range(NO + 1)]
            with Mark() as p4a:
                qT = sb("qT", [128, 16, TOS], BF16)
                qTB = Buf("qT")
                with Mark() as p4a1:
                    wqb = sb("wqb", [128, 8, 2048], BF16)
                    wqB = Buf("wq")
                    for q4 in range(4):
                        S.dma("gpsimd", wqb[:, :, q4 * 512:(q4 + 1) * 512], wq[:, q4 * 512:(q4 + 1) * 512].rearrange("(k p) c -> p k c", p=128),
                              writes=[wqB])
                    blks = [(b0, min(512, TPE - b0)) for b0 in range(0, TPE, 512)]
                    nb = 0
                    for hc in range(16):
                        for (b0, bn) in blks:
                            pk = nb % 4
                            nb += 1
                            T([mm(bank(pk, bn), wqb[:, k, hc * 128:(hc + 1) * 128], xn2T[:, k, b0:b0 + bn], start=(k == 0), stop=(k == 7))
                               for k in range(8)], xnB_own + [wqB], [pbuf[pk]])
                            A(act(qT[:, hc, b0:b0 + bn], bank(pk, bn), AF.Copy), [pbuf[pk]], [qTB])
                S.barrier()
                skb = sb("skb", [128, 256], BF16)
                skB = Buf("sk")
                S.dma("gpsimd", skb[:], skT, writes=[skB])
                ssb = sb("ssb", [128, 16, 128], F32)
                wrk = sb("wrk", [128, 16, 128], F32)
                top = sb("top", [128, 16, 16], F32)
                idx = sb("idx", [128, 16, 16], U32)
                idxf = sb("idxf", [128, 16, 16], F32)
                cand = sb("cand", [128, 8, 256], F32)
                cwk = sb("cwk", [128, 8, 256], F32)
                ctop = sb("ctop", [128, 8, 16], F32)
                pos = sb("pos", [128, 8, 16], U32)
                pa_ = sb("pa_", [128, 8, 16], U32)
                pb_ = sb("pb_", [128, 8, 16], U32)
                paf = sb("paf", [128, 8, 16], F32)
                pbf = sb("pbf", [128, 8, 16], F32)
                eq = sb("eq", [128, 8, 16, 16], F32)
                sel = sb("sel", [128, 3, 128], F32)
                zz = sb("zz", [128, 16], F32)
                ssbB, wrkB, topB, idxB, candB, ctopB, posB, eqB, selsB, zzB = [Buf(n_) for n_ in
                    ("ssb", "wrk", "top", "idx", "cand", "ctop", "pos", "eq", "sels", "zz")]
                io16 = cst_sb[:, C_IO16:C_IO16 + 16]
                for (t, rows, c0) in mtiles:
                    r_ = slice(0, rows)
                    for hc in range(16):
                        T([mm(bank(hc // 4, 128, (hc % 4) * 128)[r_, :], qT[:, hc, c0:c0 + rows], skb[:, (hc % 2) * 128:(hc % 2 + 1) * 128])],
                          [qTB, skB], [pbuf[hc // 4]])
                    for q4 in range(4):
                        A(act(ssb[r_, q4 * 4:(q4 + 1) * 4, :], bank(q4)[r_, :].rearrange("p (a b) -> p a b", a=4), AF.Copy), [pbuf[q4]], [ssbB])
                    tB_ = [Buf("top%d" % hc) for hc in range(16)]
                    wB_ = [Buf("wrk%d" % hc) for hc in range(16)]
                    iB_ = [Buf("idx%d" % hc) for hc in range(16)]
                    for hc in range(16):
                        V(lambda e, hc=hc, r_=r_: e.max(out=top[r_, hc, 0:8], in_=ssb[r_, hc, :]), [ssbB], [tB_[hc]])
                    for hc in range(16):
                        V(lambda e, hc=hc, r_=r_: e.match_replace(out=wrk[r_, hc, :], in_to_replace=top[r_, hc, 0:8], in_values=ssb[r_, hc, :],
                                                          imm_value=-1e30), [ssbB, tB_[hc]], [wB_[hc]])
                    for hc in range(16):
                        V(lambda e, hc=hc, r_=r_: e.max(out=top[r_, hc, 8:16], in_=wrk[r_, hc, :]), [wB_[hc]], [tB_[hc]])
                    for hc in range(16):
                        V(lambda e, hc=hc, r_=r_: e.max_index(out=idx[r_, hc, 0:8], in_max=top[r_, hc, 0:8], in_values=ssb[r_, hc, :]),
                          [ssbB, tB_[hc]], [iB_[hc]])
                    for hc in range(16):
                        V(lambda e, hc=hc, r_=r_: e.max_index(out=idx[r_, hc, 8:16], in_max=top[r_, hc, 8:16], in_values=ssb[r_, hc, :]),
                          [ssbB, tB_[hc]], [iB_[hc]])
                    V(cp(zz[r_, 0:1], zz[r_, 0:1]), tB_ + iB_ + wB_ + [zzB], [topB, idxB, wrkB, zzB])
                    V(cp(idxf[r_], idx[r_]), [idxB], [idxB])
                    top4 = top[r_].rearrange("p (h c) k -> p h c k", c=2)
                    V(tt(cand[r_].rearrange("p h (a b) -> p h a b", a=16),
                         top4[:, :, 0, :].unsqueeze(3).to_broadcast([rows, 8, 16, 16]),
                         top4[:, :, 1, :].unsqueeze(2).to_broadcast([rows, 8, 16, 16]), ALU.add), [topB], [candB])
                    cB_ = [Buf("ctop%d" % h) for h in range(8)]
                    cwB_ = [Buf("cwk%d" % h) for h in range(8)]
                    pB_ = [Buf("pos%d" % h) for h in range(8)]
                    for h in range(8):
                        V(lambda e, h=h, r_=r_: e.max(out=ctop[r_, h, 0:8], in_=cand[r_, h, :]), [candB, ctopB], [cB_[h]])
                    for h in range(8):
                        V(lambda e, h=h, r_=r_: e.match_replace(out=cwk[r_, h, :], in_to_replace=ctop[r_, h, 0:8], in_values=cand[r_, h, :],
                                                        imm_value=-1e30), [candB, cB_[h], wrkB], [cwB_[h]])
                    for h in range(8):
                        V(lambda e, h=h, r_=r_: e.max(out=ctop[r_, h, 8:16], in_=cwk[r_, h, :]), [cwB_[h]], [cB_[h]])
                    for h in range(8):
                        V(lambda e, h=h, r_=r_: e.max_index(out=pos[r_, h, 0:8], in_max=ctop[r_, h, 0:8], in_values=cand[r_, h, :]),
                          [candB, cB_[h], posB], [pB_[h]])
                    for h in range(8):
                        V(lambda e, h=h, r_=r_: e.max_index(out=pos[r_, h, 8:16], in_max=ctop[r_, h, 8:16], in_values=cand[r_, h, :]),
                          [candB, cB_[h]], [pB_[h]])
                    V(cp(zz[r_, 0:1], zz[r_, 0:1]), cB_ + cwB_ + pB_ + [zzB], [ctopB, wrkB, posB, zzB])
                    V(lambda e, r_=r_: e.tensor_single_scalar(out=pa_[r_], in_=pos[r_], scalar=4, op=ALU.logical_shift_right), [posB], [posB])
                    V(lambda e, r_=r_: e.tensor_single_scalar(out=pb_[r_], in_=pos[r_], scalar=15, op=ALU.bitwise_and), [posB], [posB])
                    V(cp(paf[r_], pa_[r_]), [posB], [posB])
                    V(cp(pbf[r_], pb_[r_]), [posB], [posB])
                    idx4 = idxf[r_].rearrange("p (h c) k -> p h c k", c=2)
                    for which, (pf, ci) in enumerate(((paf, 0), (pbf, 1))):
                        V(tt(eq[r_], io16[r_].unsqueeze(1).unsqueeze(1).to_broadcast([rows, 8, 16, 16]),
                             pf[r_].unsqueeze(3).to_broadcast([rows, 8, 16, 16]), ALU.is_equal), [posB, cstB], [eqB])
                        V(tt(eq[r_], eq[r_], idx4[:, :, ci, :].unsqueeze(2).to_broadcast([rows, 8, 16, 16]), ALU.mult), [eqB, idxB], [eqB])
                        V(red(sel[r_, which, :].rearrange("p (h k) -> p h k", h=8), eq[r_], ALU.add), [eqB], [selsB])
                    V(cp(zz[r_, 0:8], ctop[r_, :, 0]), [ctopB], [zzB])
                    V(tt(ctop[r_], ctop[r_], zz[r_, 0:8].unsqueeze(2).to_broadcast([rows, 8, 16]), ALU.subtract), [ctopB, zzB], [ctopB])
                    A(act(ctop[r_], ctop[r_], AF.Exp), [ctopB], [ctopB])
                    V(red(zz[r_, 0:8], ctop[r_], ALU.add), [ctopB], [zzB])
                    V(recip(zz[r_, 8:16], zz[r_, 0:8]), [zzB], [zzB])
                    V(tt(sel[r_, 2, :].rearrange("p (h k) -> p h k", h=8), ctop[r_], zz[r_, 8:16].unsqueeze(2).to_broadcast([rows, 8, 16]), ALU.mult),
                      [ctopB, zzB], [selsB])
                    T([tr(bank(4, rows, j * 128), sel[r_, j, :], ident_f[r_, r_]) for j in range(3)], [selsB, cstB], [pbuf[4]])
                    A(act(I1T[:, c0:c0 + rows], bank(4, rows, 0), AF.Copy), [pbuf[4]], [selB[t]])
                    A(act(I2T[:, c0:c0 + rows], bank(4, rows, 128), AF.Copy), [pbuf[4]], [selB[t]])
                    A(act(gT[:, c0:c0 + rows], bank(4, rows, 256), AF.Copy), [pbuf[4]], [selB[t]])
            S.barrier()
            with Mark() as p4c:
                GRP = 256
                GMAX = GRP + NS
                Gst = sb("Gst", [128, 128, GMAX], BF16)
                GstB = Buf("Gst")
                NBT = 8
                NRT = 3
                L4 = [sb("L4_%d" % i, [128, NBT, 128], BF16) for i in range(NRT)]
                R4 = [sb("R4_%d" % i, [128, NBT, 128], BF16) for i in range(NRT)]
                L4B = [Buf("L4_%d" % i) for i in range(NRT)]
                R4B = [Buf("R4_%d" % i) for i in range(NRT)]
                io128 = cst_sb[:, C_IO128:C_IO128 + 128]
                NSL = 3
                utc = [sb("utc%d" % i, [128, 8, 128], BF16) for i in range(NSL)]
                vc = [sb("vc%d" % i, [128, 1024], BF16) for i in range(NSL)]
                utB, vcB = [[Buf("%s%d" % (n_, i)) for i in range(NSL)] for n_ in ("ut", "vc")]
                sq_ = [sb("sq_%d" % i, [128, GMAX], F32) for i in range(2)]
                in_ = [sb("in_%d" % i, [128, GMAX], F32) for i in range(2)]
                sg_ = [sb("sg_%d" % i, [128, GMAX], F32) for i in range(2)]
                w1_ = [sb("w1_%d" % i, [128, GMAX], F32) for i in range(2)]
                wt_ = [sb("wt_%d" % i, [128, GMAX], BF16) for i in range(2)]
                sqB_, inB_, sgB_, w1B_, wtB_ = [[Buf(n_ + "0"), Buf(n_ + "1")] for n_ in ("sq_", "in_", "sg_", "w1_", "wt_")]
                xl_ = [sb("xl_%d" % i, [128, 1024], F32) for i in range(2)]
                yo_ = [sb("yo_%d" % i, [128, 1024], F32) for i in range(2)]
                xlB, yoB = [[Buf(n_ + "0"), Buf(n_ + "1")] for n_ in ("xl", "yo")]
                nld = 0
                nfin = 0
                ng_ = 0
                groups = []
                for g0 in range(0, TO, GRP):
                    tl_ = [mtiles[g0 // 128 + j] for j in range(min(GRP, TO - g0) // 128)]
                    groups.append([g0, tl_])
                if len(mtiles) > NO:
                    groups[-1][1] = groups[-1][1] + [mtiles[NO]]
                for g0, tl_ in groups:
                    gn = sum(r_ for (_, r_, _) in tl_)
                    gt_ = [t for (t, _, _) in tl_]
                    for (t, rows, c0) in tl_:
                        for n0 in range(0, rows, NBT):
                            s = ng_ % NRT
                            pk = 4 + 2 * (ng_ % 2)
                            ng_ += 1
                            nn = min(NBT, rows - n0)
                            o_ = c0 - g0 + n0
                            iob = io128.unsqueeze(1).to_broadcast([128, nn, 128])
                            V(tt(L4[s][:, 0:nn, :], iob, I1T[:, c0 + n0:c0 + n0 + nn].unsqueeze(2).to_broadcast([128, nn, 128]), ALU.is_equal),
                              [selB[t], cstB], [L4B[s]])
                            for j in range(nn):
                                V(ts(R4[s][:, j, :], io128, I2T[:, c0 + n0 + j:c0 + n0 + j + 1], gT[:, c0 + n0 + j:c0 + n0 + j + 1],
                                     op0=ALU.is_equal, op1=ALU.mult), [selB[t], cstB], [R4B[s]])
                            T([mm(bank(pk + j // 4, 128, (j % 4) * 128), R4[s][:, j, :], L4[s][:, j, :]) for j in range(nn)],
                              [R4B[s], L4B[s]], [pbuf[pk], pbuf[pk + 1]])
                            A(act(Gst[:, :, o_:o_ + nn], ps[:, pk * 512:pk * 512 + nn * 128].rearrange("p (n i) -> p i n", n=nn), AF.Copy),
                              [pbuf[pk], pbuf[pk + 1]], [GstB])
                    base = nld

                    def ld(c, base=base):
                        sl_ = (base + c) % NSL
                        S.dma("sync", utc[sl_][:].rearrange("p k e -> p (k e)"), uTb[c], reads=[precB], writes=[utB[sl_]])
                        S.dma("sync", vc[sl_][:], vbd[c * 128:(c + 1) * 128, :], reads=[precB], writes=[vcB[sl_]])

                    def at(c, g0=g0, gn=gn, base=base, gt_=gt_):
                        sl_ = (base + c) % NSL
                        pk = 6 + c % 2
                        T([mm(bank(pk, gn), utc[sl_][:, k, :], xn2T[:, k, g0:g0 + gn], start=(k == 0), stop=(k == 7)) for k in range(8)],
                          [utB[sl_]] + [xnB_own[t] for t in gt_], [pbuf[pk]])

                    def chain(c, gn=gn):
                        s = c % 2
                        pk = 6 + s
                        A(act(sq_[s][:, 0:gn], bank(pk, gn), AF.Square), [pbuf[pk]], [sqB_[s]])
                        V(ts(in_[s][:, 0:gn], sq_[s][:, 0:gn], 0.044715, 1.0, op0=ALU.mult, op1=ALU.add), [sqB_[s]], [inB_[s]])
                        V(tt(in_[s][:, 0:gn], in_[s][:, 0:gn], bank(pk, gn), ALU.mult), [inB_[s], pbuf[pk]], [inB_[s]])
                        A(act(sg_[s][:, 0:gn], in_[s][:, 0:gn], AF.Sigmoid, scale=1.5957691216057308), [inB_[s]], [sgB_[s]])
                        V(tt(w1_[s][:, 0:gn], sg_[s][:, 0:gn], bank(pk, gn), ALU.mult), [sgB_[s], pbuf[pk]], [w1B_[s]])
                        G(tt(wt_[s][:, 0:gn], w1_[s][:, 0:gn], Gst[:, c, 0:gn], ALU.mult), [w1B_[s], GstB], [wtB_[s]])

                    def outmm(c, base=base, tl_=tl_):
                        sl_ = (base + c) % NSL
                        s = c % 2
                        fns = []
                        for j, (t, rows, c0) in enumerate(tl_):
                            for hf in range(2):
                                fns.append(mm(bank(2 * j + hf)[0:rows, :], wt_[s][:, j * 128:j * 128 + rows], vc[sl_][:, hf * 512:(hf + 1) * 512],
                                              start=(c == 0), stop=(c == 127)))
                        T(fns, [wtB_[s], vcB[sl_]], [pbuf[b_] for b_ in range(2 * len(tl_))])

                    ld(0)
                    ld(1)
                    at(0)
                    for c in range(128):
                        if c + 2 < 128:
                            ld(c + 2)
                        if c + 1 < 128:
                            at(c + 1)
                        chain(c)
                        outmm(c)
                    nld += 128
                    for j, (t, rows, c0) in enumerate(tl_):
                        s = nfin % 2
                        nfin += 1
                        S.dma("sync", xl_[s][0:rows, :], x1d[c0:c0 + rows, :], reads=[x1B[t]], writes=[xlB[s]])
                        for hf in range(2):
                            V(tt(yo_[s][0:rows, hf * 512:(hf + 1) * 512], bank(2 * j + hf)[0:rows, :], xl_[s][0:rows, hf * 512:(hf + 1) * 512], ALU.add),
                              [pbuf[2 * j + hf], xlB[s]], [yoB[s]])
                        dst = y_smp[0:rows, :] if t == NO else y_own[c0:c0 + rows, :]
                        S.dma("sync", dst, yo_[s][0:rows, :], reads=[yoB[s]])
            S.barrier()

        S.finish()
        with nc.Block() as block:
            S.emit(block)
    return nc


def alibi_slopes():
    return [2.0 ** (-8.0 * (h + 1) / 8) for h in range(8)]


def make_consts(pv, NP):
    cst = np.zeros((128, 2048), np.float32)
    cst[:, 0:128] = np.eye(128, dtype=np.float32)
    kk = np.arange(128)[:, None]
    qq = np.arange(128)[None, :]
    cst[:, 128:256] = np.where(qq >= kk, 0.0, NEG)
    sl = alibi_slopes()
    p = np.arange(128)
    for h in range(8):
        for dt in range(-3, 13):
            cst[:, 256 + h * 16 + dt + 3] = sl[h] * (p - 128.0 * dt)
        for dt in range(1, 29):
            cst[:, 384 + h * 28 + dt - 1] = sl[h] * (p - 128.0 * dt) + (0.0 if pv else NEG)
    cst[:, 1024:1152] = (qq >= kk).astype(np.float32)
    r_ = np.arange(128)
    cst[:, 1152:1280] = ((r_[:, None] % 4 == r_[None, :] % 4) & (r_[:, None] // 4 < r_[None, :] // 4)).astype(np.float32)
    cst[:, 1280:1296] = np.arange(16, dtype=np.float32)[None, :]
    cst[:, 1408:1536] = np.arange(128, dtype=np.float32)[None, :]
    qrow = np.zeros((2, 8 * 512), np.float32)
    r = np.arange(512)
    for h in range(8):
        qrow[0, h * 512:(h + 1) * 512] = -sl[h] * 128.0 * (r // 128)
        qrow[1, h * 512:(h + 1) * 512] = -sl[h] * (r % 128)
    return cst, qrow


def make_dcst():
    d = np.zeros((128, 1024), np.float32)
    sl = alibi_slopes()
    p = np.arange(128)
    for pg in range(64):
        for h in range(8):
            d[:, pg * 8 + h] = -sl[h] * (8192.0 - (pg * 128.0 + p))
    d[:, 64 * 8:65 * 8] = NEG
    d[0, 64 * 8:65 * 8] = 0.0
    d[:, 520] = p
    for si in range(4):
        d[0:2, 528 + si * 4 + si] = 1.0
        d[:, 560 + si * 4 + si] = 1.0
    return d


def make_gcol(inp, pv, NP, NO):
    bi = np.asarray(inp["b_i"], np.float32).reshape(-1)
    bf = np.asarray(inp["b_f"], np.float32).reshape(-1)
    g = np.zeros((128, 8), np.float32)
    g[:, 0] = np.tile(bi, 32)
    g[:, 1] = np.tile(bf, 32)
    rows_pre = np.arange(128) < NP * 4
    g[:, 2] = np.where(rows_pre, 1.0 if pv else 0.0, 1.0)
    g[:, 3] = np.where(rows_pre, 0.0 if pv else NEG, 0.0)
    g[:, 4] = -g[:, 2]
    return g


def make_prow(inp):
    g = lambda k: np.asarray(inp[k], np.float32).reshape(-1)
    qg, kg = g("q_norm_g"), g("k_norm_g")
    return np.concatenate([
        g("norm1_g"), g("norm2_g"), qg, qg, kg, kg, g("diff_subln_g"), g("mlstm_norm_g"),
        g("lam_q1"), g("lam_k1"), g("lam_q2"), g("lam_k2")]).reshape(1, -1).astype(np.float32)


def make_peer_inputs(inp):
    U = np.asarray(inp["peer_u"], np.float32).reshape(16384, 1024)
    uTh = np.ascontiguousarray(U.reshape(128, 128, 8, 128).transpose(0, 3, 2, 1)).reshape(128, 128, 1024)
    sk = np.asarray(inp["peer_subkeys"], np.float32).reshape(2, 128, 128)
    skT = np.ascontiguousarray(sk.transpose(2, 0, 1)).reshape(128, 256)
    return {"wq": np.asarray(inp["peer_wq"], np.float32).reshape(1024, 2048), "skT": skT, "uTh": uTh,
            "pv": np.asarray(inp["peer_v"], np.float32).reshape(16384, 1024)}


_NC_CACHE = {}


def kernel(**inputs):
    inp = {k: np.asarray(v) for k, v in inputs.items()}
    cfg = Cfg(NP=16, NO=16, NS=4, npool=int(inp["cache_k"].shape[1]))
    cfg.sample = True
    key = "full"
    if key not in _NC_CACHE:
        _NC_CACHE[key] = build_program(cfg)
    nc = _NC_CACHE[key]
    xp = inp["x_prompt"].astype(np.float32, copy=False)
    xs = inp["x_sample"].astype(np.float32, copy=False)
    prow = make_prow(inp)
    peer_in = make_peer_inputs(inp)
    shared = {"w_in": inp["w_in"][0], "prow": prow, "w_a": inp["w_branch_a"][0], "w_b": inp["w_branch_b"][0],
              "w_o": inp["w_out"][0], **peer_in}
    npool = cfg.npool
    ck2 = inp["cache_k"].reshape(npool * 128, 1024)
    cv2 = inp["cache_v"].reshape(npool * 128, 1024)
    dcst_ = make_dcst()
    brow_ = np.concatenate([inp["b_i"].reshape(-1), inp["b_f"].reshape(-1)]).reshape(1, 8).astype(np.float32)
    ptab_all = inp["page_table"].astype(np.int32, copy=False)
    sC_all = inp["state_C"][0].reshape(32 * 4, 128, 256)
    sn_all = inp["state_n"][0].reshape(32, 512)
    sm_all = inp["state_m"][0].reshape(32, 4)
    zeros_pre = np.zeros((2048, 1024), np.float32)
    maps = []
    for c in range(8):
        b, j = c // 2, c % 2
        cst, qrow = make_consts(pv=(j == 1), NP=16)
        m = dict(shared)
        m.update({"x_pre": zeros_pre if j == 0 else xp[b, 0:2048], "x_own": xp[b, j * 2048:(j + 1) * 2048],
                  "x_smp": xs[4 * c:4 * c + 4, 0], "cst": cst, "qrow": qrow, "gcol": make_gcol(inp, j == 1, 16, 16),
                  "cache_k": ck2, "cache_v": cv2, "ptab": ptab_all[4 * c:4 * c + 4], "dcst": dcst_, "brow": brow_,
                  "sC": sC_all[16 * c:16 * c + 16], "sn": sn_all[4 * c:4 * c + 4], "smm": sm_all[4 * c:4 * c + 4]})
        maps.append(m)
    res = run_bass_kernel_spmd(nc, maps, core_ids=list(range(8)))
    R_ = res.results
    y_prompt = np.zeros((4, 4096, 1024), np.float32)
    k_prompt = np.zeros((1, 4, 4096, 8, 128), np.float32)
    v_prompt = np.zeros((1, 4, 4096, 8, 128), np.float32)
    C_prompt = np.zeros((1, 4, 4, 128, 256), np.float32)
    n_prompt = np.zeros((1, 4, 4, 128), np.float32)
    m_prompt = np.zeros((1, 4, 4), np.float32)
    y_sample = np.zeros((32, 1, 1024), np.float32)
    k_sample = np.zeros((1, 32, 1, 8, 128), np.float32)
    v_sample = np.zeros((1, 32, 1, 8, 128), np.float32)
    C_sample = np.zeros((1, 32, 4, 128, 256), np.float32)
    n_sample = np.zeros((1, 32, 4, 128), np.float32)
    m_sample = np.zeros((1, 32, 4), np.float32)
    for c in range(8):
        b, j = c // 2, c % 2
        r = R_[c]
        sl = slice(j * 2048, (j + 1) * 2048)
        y_prompt[b, sl] = r["y_own"]
        k_prompt[0, b, sl] = r["k_own"].reshape(2048, 8, 128)
        v_prompt[0, b, sl] = r["v_own"].reshape(2048, 8, 128)
        y_sample[4 * c:4 * c + 4, 0] = r["y_smp"]
        k_sample[0, 4 * c:4 * c + 4, 0] = r["k_smp"].reshape(4, 8, 128)
        v_sample[0, 4 * c:4 * c + 4, 0] = r["v_smp"].reshape(4, 8, 128)
        C_sample[0, 4 * c:4 * c + 4] = r["C_smp"].reshape(4, 4, 128, 256)
        n_sample[0, 4 * c:4 * c + 4] = r["n_smp"].reshape(4, 4, 128)
        m_sample[0, 4 * c:4 * c + 4] = r["m_smp"]
        if j == 1:
            C_prompt[0, b] = r["C_out"]
            n_prompt[0, b] = r["n_out"]
            m_prompt[0, b] = r["m_out"].reshape(4)
    return (y_prompt, y_sample, k_prompt, v_prompt, C_prompt, n_prompt, m_prompt,
            k_sample, v_sample, C_sample, n_sample, m_sample)
```

```python
import numpy as np
from contextlib import ExitStack
import concourse.bass as bass
import concourse.mybir as mybir
from concourse.bass_utils import run_bass_kernel_spmd
from concourse.alu_op_type import AluOpType as ALU

F32 = mybir.dt.float32
BF16 = mybir.dt.bfloat16
I32 = mybir.dt.int32
U32 = mybir.dt.uint32
AF = mybir.ActivationFunctionType
AX = mybir.AxisListType

D = 1024
IN_COLS = 8200
NEG = -30000.0


class Buf:
    __slots__ = ("w", "r", "dsem", "name", "excl")

    def __init__(self, name="", excl=False):
        self.excl = excl
        self.w = None
        self.r = {}
        self.dsem = None
        self.name = name


class Sched:
    ENG = ("tensor", "vector", "scalar", "gpsimd", "sync")

    def __init__(self, nc, stack):
        self.nc = nc
        self.stack = stack
        self.prog = {n: [] for n in self.ENG}
        self.sem = {n: stack.enter_context(nc.semaphore("s_" + n)) for n in self.ENG}
        self.cnt = {n: 0 for n in self.ENG}
        self.known = {n: {} for n in self.ENG}
        self.dsems = []
        self.log = {n: [] for n in self.ENG}

    def _wait(self, eng, ev):
        key, sem, val = ev
        if self.known[eng].get(key, 0) >= val:
            return
        self.known[eng][key] = val
        self.prog[eng].append(lambda e, sem=sem, val=val: e.wait_ge(sem, val))
        self.log[eng].append('  wait %s >= %d' % (key, val))

    def _deps(self, eng, reads, writes):
        for b in reads:
            if b.w is not None:
                self._wait(eng, b.w)
            if b.excl:
                for k_, ev in b.r.items():
                    if k_ != eng:
                        self._wait(eng, ev)
        for b in writes:
            if b.w is not None:
                self._wait(eng, b.w)
            for ev in b.r.values():
                self._wait(eng, ev)

    def _mark(self, key, ev, reads, writes):
        for b in reads:
            b.r[key] = ev
        for b in writes:
            b.w = ev
            b.r = {}

    def group(self, eng, fns, reads=(), writes=()):
        self._deps(eng, reads, writes)
        self.cnt[eng] += 1
        c = self.cnt[eng]
        sem = self.sem[eng]
        for fn in fns[:-1]:
            self.prog[eng].append(fn)
        last = fns[-1]
        self.prog[eng].append(lambda e, fn=last, sem=sem: fn(e).then_inc(sem, 1))
        ev = (eng, sem, c)
        self.log[eng].append('%s#%d n=%d R=%s W=%s' % (eng, c, len(fns), [b.name for b in reads], [b.name for b in writes]))
        if eng == "tensor":
            self.known[eng][eng] = c
        self._mark(eng, ev, reads, writes)
        return ev

    def op(self, eng, fn, reads=(), writes=()):
        return self.group(eng, [fn], reads, writes)

    def _dsem(self, b):
        if b.dsem is None:
            key = "d%d" % len(self.dsems)
            sem = self.stack.enter_context(self.nc.semaphore(key))
            b.dsem = [sem, 0, key]
            self.dsems.append(b.dsem)
        return b.dsem

    def dma(self, q, out, in_, reads=(), writes=(), owner=None, **kw):
        self._deps(q, reads, writes)
        if owner is None:
            owner = writes[0] if writes else reads[0]
        d = self._dsem(owner)
        d[1] += 16
        sem, val, key = d[0], d[1], d[2]
        self.prog[q].append(lambda e, out=out, in_=in_, sem=sem, kw=kw:
                            e.dma_start(out=out, in_=in_, **kw).then_inc(sem, 16))
        ev = (key, sem, val)
        self._mark(key, ev, reads, writes)
        return ev

    def custom_dma(self, q, fn, reads=(), writes=(), owner=None):
        self._deps(q, reads, writes)
        if owner is None:
            owner = writes[0] if writes else reads[0]
        d = self._dsem(owner)
        d[1] += 16
        sem, val, key = d[0], d[1], d[2]
        self.prog[q].append(lambda e, fn=fn, sem=sem: fn(e).then_inc(sem, 16))
        ev = (key, sem, val)
        self._mark(key, ev, reads, writes)
        return ev

    def barrier(self):
        for e in self.ENG:
            for p in self.ENG:
                if p != e and self.cnt[p] > 0:
                    self._wait(e, (p, self.sem[p], self.cnt[p]))
            for d in self.dsems:
                if d[1] > 0:
                    self._wait(e, (d[2], d[0], d[1]))

    def finish(self):
        e = "sync"
        for p in self.ENG:
            if p != e and self.cnt[p] > 0:
                self._wait(e, (p, self.sem[p], self.cnt[p]))
        for d in self.dsems:
            if d[1] > 0:
                self._wait(e, (d[2], d[0], d[1]))

    def emit(self, block):
        prog = self.prog

        @block.tensor
        def _(e):
            for f in prog["tensor"]:
                f(e)

        @block.vector
        def _(e):
            for f in prog["vector"]:
                f(e)

        @block.scalar
        def _(e):
            for f in prog["scalar"]:
                f(e)

        @block.gpsimd
        def _(e):
            for f in prog["gpsimd"]:
                f(e)

        @block.sync
        def _(e):
            for f in prog["sync"]:
                f(e)


def mm(out, lhsT, rhs, start=True, stop=True, sgc=False):
    return lambda e: e.matmul(out, lhsT=lhsT, rhs=rhs, start=start, stop=stop, skip_group_check=sgc)


def tr(out, in_, ident):
    return lambda e: e.transpose(out=out, in_=in_, identity=ident)


def act(out, in_, func, **kw):
    return lambda e: e.activation(out=out, in_=in_, func=func, **kw)


def tt(out, in0, in1, op):
    return lambda e: e.tensor_tensor(out=out, in0=in0, in1=in1, op=op)


def ts(out, in0, s1, s2=None, op0=ALU.mult, op1=None, **kw):
    if op1 is None:
        return lambda e: e.tensor_scalar(out=out, in0=in0, scalar1=s1, scalar2=None, op0=op0, **kw)
    return lambda e: e.tensor_scalar(out=out, in0=in0, scalar1=s1, scalar2=s2, op0=op0, op1=op1, **kw)


def stt(out, in0, scalar, in1, op0, op1):
    return lambda e: e.scalar_tensor_tensor(out=out, in0=in0, scalar=scalar, in1=in1, op0=op0, op1=op1)


def cp(out, in_):
    return lambda e: e.tensor_copy(out=out, in_=in_)


def red(out, in_, op, axis=AX.X):
    return lambda e: e.tensor_reduce(out=out, in_=in_, axis=axis, op=op)


def recip(out, in_):
    return lambda e: e.reciprocal(out=out, in_=in_)


def mset(out, val):
    return lambda e: e.memset(out, val)


class Cfg:
    def __init__(self, NP=16, NO=16, NS=4, npool=2560, npages=64, stages=99, dbg=False, nheads=8, attn=True):
        self.NP, self.NO, self.NS = NP, NO, NS
        self.npool, self.npages = npool, npages
        self.stages = stages
        self.dbg = dbg
        self.nheads, self.attn = nheads, attn
        self.sample = False
        self.peer = True
        import os
        self.skip = set(os.environ.get('SKIP', '').split(','))


def build_program(cfg):
    NP, NO, NS = cfg.NP, cfg.NO, cfg.NS
    TP, TO = NP * 128, NO * 128
    TOS = TO + NS
    NG = NO // 4
    nc = bass.Bass("TRN2", target_bir_lowering=False)
    global LAST_SCHED

    def din(name, shape, dt=F32):
        return nc.dram_tensor(name, list(shape), dt, kind="ExternalInput").ap()

    def dout(name, shape, dt=F32):
        return nc.dram_tensor(name, list(shape), dt, kind="ExternalOutput").ap()

    x_pre = din("x_pre", [TP, D])
    x_own = din("x_own", [TO, D])
    x_smp = din("x_smp", [NS, D])
    w_in = din("w_in", [D, IN_COLS])
    prow = din("prow", [1, 4608])
    cst = din("cst", [128, 2048])
    qrow = din("qrow", [2, 8 * 512])

    gcol = din("gcol", [128, 8])
    C_out = dout("C_out", [4, 128, 256])
    n_out = dout("n_out", [4, 128])
    m_out = dout("m_out", [4, 1])
    dbg_hb = dout("dbg_hb", [TOS, D]) if cfg.dbg else None
    dbg_x1 = dout("dbg_x1", [TOS, D]) if cfg.dbg else None
    w_a = din("w_a", [D, D])
    w_b = din("w_b", [D, D])
    w_o = din("w_o", [D, D])
    x1d = nc.dram_tensor("x1d", [TOS, D], F32).ap()
    wq = din("wq", [D, 2048])
    skT = din("skT", [128, 256])
    uTh = din("uTh", [128, 128, 1024])
    pv_ = din("pv", [16384, D])
    uTb = nc.dram_tensor("uTb", [128, 128, 1024], BF16).ap()
    vbd = nc.dram_tensor("vbd", [16384, D], BF16).ap()
    Gd = nc.dram_tensor("Gd", [128, 128, TOS], BF16).ap()
    npool = cfg.npool
    cache_k = din("cache_k", [npool * 128, D])
    cache_v = din("cache_v", [npool * 128, D])
    ptab = din("ptab", [NS, 64], I32)
    dcst = din("dcst", [128, 1024])
    qsd = nc.dram_tensor("qsd", [NS, D], F32).ap()
    sC = din("sC", [NS * 4, 128, 256])
    sn = din("sn", [NS, 512])
    smm = din("smm", [NS, 4])
    brow = din("brow", [1, 8])
    C_smp = dout("C_smp", [NS * 4, 128, 256])
    n_smp = dout("n_smp", [NS, 512])
    m_smp = dout("m_smp", [NS, 4])
    k_smp = dout("k_smp", [NS, D])
    v_smp = dout("v_smp", [NS, D])
    y_own = dout("y_own", [TO, D])
    y_smp = dout("y_smp", [NS, D])
    k_own = dout("k_own", [TO, D])
    v_own = dout("v_own", [TO, D])
    dbg_a = dout("dbg_a", [TOS, D]) if cfg.dbg else None

    with ExitStack() as st:
        S = Sched(nc, st)
        globals()['LAST_SCHED'] = S

        ARENA_WORDS = 53000
        arena = st.enter_context(nc.sbuf_tensor("arena", [128, ARENA_WORDS], F32))
        apos = [0]
        HW = [0]
        globals()['LAST_HW'] = HW

        def carve(off_words, shape, dt):
            n = 1
            for d_ in shape[1:]:
                n *= d_
            esz = 4 if dt in (F32, I32, U32) else 2
            nw = (n * esz + 3) // 4
            assert off_words + nw <= ARENA_WORDS, ("SBUF arena overflow", off_words, nw)
            v = arena[:, off_words:off_words + nw]
            if dt != F32:
                v = v.bitcast(dt)
            v = v[0:shape[0], 0:n]
            if len(shape) == 3:
                v = v.rearrange("p (a b) -> p a b", a=shape[1])
            elif len(shape) == 4:
                v = v.rearrange("p (a b c) -> p a b c", a=shape[1], b=shape[2])
            return v, nw

        def sb(name, shape, dt, stack=None):
            v, nw = carve(apos[0], shape, dt)
            apos[0] += (nw + 15) // 16 * 16
            HW[0] = max(HW[0], apos[0])
            return v

        class Mark:
            def __enter__(self_):
                self_.m = apos[0]
                return self_

            def __exit__(self_, *a):
                print("[sbuf] phase high-water %.1f KB (mark at %.1f KB)" % (HW[0] * 4 / 1024, self_.m * 4 / 1024))
                HW[0] = self_.m
                apos[0] = self_.m
                return False

        V = lambda fn, r=(), w=(): S.op("vector", fn, r, w)
        A = lambda fn, r=(), w=(): S.op("scalar", fn, r, w)
        G = lambda fn, r=(), w=(): S.op("gpsimd", fn, r, w)
        T = lambda fns, r=(), w=(): S.group("tensor", fns, r, w)

        ps = st.enter_context(nc.psum_tensor("ps", [128, 4096], F32))
        psb = ps[:].bitcast(BF16)
        pbuf = [Buf("ps%d" % i, excl=True) for i in range(8)]

        def bank(i, n=512, off=0):
            return ps[:, i * 512 + off:i * 512 + off + n]

        def bankb(i, n=1024, off=0):
            return psb[:, i * 1024 + off:i * 1024 + off + n]

        cst_sb = sb("cst_sb", [128, 2048], F32)
        cstB = Buf("cst")
        S.dma("sync", cst_sb[:], cst, writes=[cstB])
        ident_f = cst_sb[:, 0:128]
        C_BO = 256
        C_BP = 256 + 128
        idb = sb("idb", [128, 128], BF16)
        mneg = sb("mneg", [128, 128], BF16)
        cbB = Buf("cb")
        V(cp(idb[:], cst_sb[:, 0:128]), [cstB], [cbB])
        V(cp(mneg[:], cst_sb[:, 128:256]), [cstB], [cbB])

        g_sb = sb("g_sb", [128, 1024], F32)
        gB = Buf("g")
        S.dma("sync", g_sb[:], prow[:, 0:1024].partition_broadcast(128), writes=[gB])
        qkg = sb("qkg", [128, 256], F32)
        qkgB = Buf("qkg")
        S.dma("sync", qkg[:], prow[:, 2048:2304].partition_broadcast(128), writes=[qkgB])
        lamt = sb("lamt", [128, 256], F32)
        lamB = Buf("lam")
        S.dma("sync", lamt[:], prow[:, 4352:4608].partition_broadcast(128), writes=[lamB])
        sm = sb("sm", [128, 64], F32)
        smB = Buf("sm")
        lsc = sb("lsc", [128, 128], F32)
        lscB = Buf()
        V(tt(lsc[:, 0:64], lamt[:, 0:64], lamt[:, 64:128], ALU.mult), [lamB], [lscB])
        V(tt(lsc[:, 64:128], lamt[:, 128:192], lamt[:, 192:256], ALU.mult), [lamB], [lscB])
        V(red(sm[:, 2:4], lsc[:].rearrange("p (a b) -> p a b", a=2), ALU.add), [lscB], [smB])
        A(act(sm[:, 4:6], sm[:, 2:4], AF.Exp), [smB], [smB])
        V(tt(sm[:, 6:7], sm[:, 4:5], sm[:, 5:6], ALU.subtract), [smB], [smB])
        V(ts(sm[:, 0:1], sm[:, 6:7], 0.2, None, op0=ALU.add), [smB], [smB])
        V(ts(sm[:, 1:2], sm[:, 0:1], -1.0, None, op0=ALU.mult), [smB], [smB])

        xnT_own = sb("xnT_own", [128, 8, TOS], BF16)
        aT_off = apos[0]
        aT = sb("aT", [128, 8, TOS], BF16)
        hbT_off = apos[0]
        hbT = sb("hbT", [128, 8, TOS], BF16)
        pre_off = apos[0]
        xnT_pre = sb("xnT_pre", [128, 8, TP], BF16)
        xnB_pre = [Buf("xnp%d" % i) for i in range(NP)]
        xnB_own = [Buf("xno%d" % i) for i in range(NO + 1)]
        aTB = [Buf("aT%d" % i) for i in range(NO + 1)]
        hbTB = [Buf("hbT%d" % i) for i in range(NO + 1)]

        with Mark() as p0:
            xt = [sb("xt%d" % i, [128, 1024], F32, p0) for i in range(2)]
            xtB = [Buf("xt0"), Buf("xt1")]
            sq = sb("sq", [128, 1024], F32, p0)
            sqB = Buf("sq")
            xn = [sb("xn%d" % i, [128, 1024], BF16, p0) for i in range(2)]
            xnB = [Buf(), Buf()]
            rs = [sb("rs%d" % i, [128, 4], F32, p0) for i in range(2)]
            rsB = [Buf(), Buf()]
            tiles = [("pre", i) for i in range(NP)] + [("own", i) for i in range(NO)] + [("smp", 0)]
            for n, (kind, i) in enumerate(tiles):
                s = n % 2
                rows = NS if kind == "smp" else 128
                src = {"pre": x_pre, "own": x_own, "smp": x_smp}[kind]
                r0 = 0 if kind == "smp" else i * 128
                S.dma("sync", xt[s][0:rows, :], src[r0:r0 + rows, :], writes=[xtB[s]])
                A(act(sq[0:rows, :], xt[s][0:rows, :], AF.Square, accum_out=rs[s][0:rows, 0:1]), [xtB[s]], [sqB, rsB[s]])
                V(ts(rs[s][0:rows, 1:2], rs[s][0:rows, 0:1], 1.0 / 1024, 1e-6, op0=ALU.mult, op1=ALU.add), [rsB[s]], [rsB[s]])
                A(act(rs[s][0:rows, 2:3], rs[s][0:rows, 1:2], AF.Sqrt), [rsB[s]], [rsB[s]])
                V(recip(rs[s][0:rows, 3:4], rs[s][0:rows, 2:3]), [rsB[s]], [rsB[s]])
                V(stt(xn[s][0:rows, :], xt[s][0:rows, :], rs[s][0:rows, 3:4], g_sb[0:rows, :], ALU.mult, ALU.mult),
                  [xtB[s], rsB[s], gB], [xnB[s]])
                pbk = 6 + s
                T([tr(bankb(pbk, rows, k * 128), xn[s][0:rows, k * 128:(k + 1) * 128], idb[0:rows, 0:rows]) for k in range(8)],
                  [xnB[s], cbB], [pbuf[pbk]])
                if kind == "pre":
                    dst, dB = xnT_pre[:, :, i * 128:(i + 1) * 128], xnB_pre[i]
                elif kind == "own":
                    dst, dB = xnT_own[:, :, i * 128:(i + 1) * 128], xnB_own[i]
                else:
                    dst, dB = xnT_own[:, :, TO:TO + NS], xnB_own[NO]
                srcp = bankb(pbk, 1024).rearrange("p (k n) -> p k n", k=8)[:, :, 0:rows]
                if n % 2 == 0:
                    A(act(dst, srcp, AF.Copy), [pbuf[pbk]], [dB])
                else:
                    V(cp(dst, srcp), [pbuf[pbk]], [dB])
        S.barrier()

        S.dma("sync", g_sb[:], prow[:, 2304:3328].partition_broadcast(128), writes=[gB])
        with Mark() as p1:
            wh = [sb("wh%d" % i, [128, 8, 384], BF16, p1) for i in range(2)]
            whB = [Buf("wh0"), Buf("wh1")]
            _save = apos[0]
            apos[0] = hbT_off
            KT = [sb("KT%d" % c, [66, TP + TO], BF16, p1) for c in range(2)]
            QT = [sb("QT%d" % c, [66, TO], BF16, p1) for c in range(2)]
            Vh = sb("Vh", [128, NP + NO, 129], BF16, p1)
            assert apos[0] <= pre_off, (apos[0], pre_off)
            apos[0] = _save
            KTBc = [[Buf("KT%d_%d" % (c, i)) for i in range(NP + NO)] for c in range(2)]
            QTBc = [[Buf("QT%d_%d" % (c, i)) for i in range(NO)] for c in range(2)]
            KrowB = [Buf("Krow0"), Buf("Krow1")]
            QrowB = [Buf("Qrow0"), Buf("Qrow1")]
            VhB = [Buf("Vh%d" % i) for i in range(NP + NO)]
            initB = Buf("init")
            if 'mset' not in cfg.skip:
                G(mset(Vh[:, :, 128:129], 1.0), [], VhB)
                for c in range(2):
                    G(mset(KT[c][64:66, :], 1.0), [], [KrowB[c]])
            sq2 = [sb("sq2_%d" % i, [128, 256], F32, p1) for i in range(2)]
            ssq = [sb("ssq_%d" % i, [128, 16], F32, p1) for i in range(2)]
            qkf = [sb("qkf_%d" % i, [128, 256], F32, p1) for i in range(2)]
            qkb = [sb("qkb_%d" % i, [128, 256], BF16, p1) for i in range(2)]
            vf = [sb("vf_%d" % i, [128, 128], F32, p1) for i in range(2)]
            scrB = [Buf("scr0"), Buf("scr1")]
            qkfB = [Buf(), Buf()]
            qkbB = [Buf(), Buf()]
            vfB = [Buf(), Buf()]
            Pt = [sb("Pt%d" % i, [128, 512], BF16, p1) for i in range(4)]
            PtB = [Buf("Pt%d" % i) for i in range(4)]
            osc = [sb("osc%d" % i, [128, 128], F32, p1) for i in range(2)]
            osq = sb("osq", [128, 128], F32, p1)
            osm = [sb("osm%d" % i, [128, 16], F32, p1) for i in range(2)]
            ab = [sb("ab%d" % i, [128, 128], BF16, p1) for i in range(2)]
            oB = [Buf("o0"), Buf("o1")]
            osqB = Buf("osq")
            abB = [Buf("ab0"), Buf("ab1")]
            if cfg.dbg:
                af = [sb("af%d" % i, [128, 128], F32, p1) for i in range(2)]
                afB = [Buf(), Buf()]

            def load_wh(h):
                s = h % 2
                for j, c0 in enumerate((h * 128, 1024 + h * 128, 2048 + h * 128)):
                    S.dma("gpsimd", wh[s][:, :, j * 128:(j + 1) * 128],
                          w_in[:, c0:c0 + 128].rearrange("(k p) c -> p k c", p=128), writes=[whB[s]])

            load_wh(0)
            nscr = 0
            smpB = Buf("smpdram")
            precB = Buf("precast")
            prec_done = [0]

            def precast(n_):
                for c_ in range(prec_done[0], min(128, prec_done[0] + n_)):
                    S.dma("gpsimd", uTb[c_], uTh[c_], writes=[precB])
                    S.dma("gpsimd", vbd[c_ * 128:(c_ + 1) * 128, :], pv_[c_ * 128:(c_ + 1) * 128, :], writes=[precB])
                prec_done[0] = min(128, prec_done[0] + n_)
            for h in range(cfg.nheads):
                if h + 1 < 8:
                    load_wh(h + 1)
                if cfg.peer:
                    precast(16)
                ws = h % 2
                for c in range(2):
                    for g in range(NG):
                        S.dma("gpsimd", QT[c][64:66, g * 512:(g + 1) * 512], qrow[:, h * 512:(h + 1) * 512], writes=[QrowB[c]])
                tl = [("pre", i) for i in range(NP)] + [("own", i) for i in range(NO)]
                if cfg.sample:
                    s = nscr % 2
                    nscr += 1
                    pz = s
                    rr = slice(0, NS)
                    T([mm(bank(pz, 384)[rr, :], xnT_own[:, k, TO:TO + NS], wh[ws][:, k, 0:384], start=(k == 0), stop=(k == 7)) for k in range(8)],
                      [xnB_own[NO], whB[ws]], [pbuf[pz]])
                    A(act(sq2[s][rr, 0:256], bank(pz, 256)[rr, :], AF.Square), [pbuf[pz]], [scrB[s]])
                    V(red(ssq[s][rr, 0:4], sq2[s][rr, 0:256].rearrange("p (a b) -> p a b", b=64), ALU.add), [scrB[s]], [scrB[s]])
                    V(ts(ssq[s][rr, 4:8], ssq[s][rr, 0:4], 1.0 / 64, 1e-6, op0=ALU.mult, op1=ALU.add), [scrB[s]], [scrB[s]])
                    A(act(ssq[s][rr, 8:12], ssq[s][rr, 4:8], AF.Sqrt), [scrB[s]], [scrB[s]])
                    V(recip(ssq[s][rr, 12:16], ssq[s][rr, 8:12]), [scrB[s]], [scrB[s]])
                    V(tt(qkf[s][rr, 0:256].rearrange("p (a b) -> p a b", b=64), bank(pz, 256)[rr, :].rearrange("p (a b) -> p a b", b=64),
                         ssq[s][rr, 12:16].unsqueeze(2).to_broadcast([NS, 4, 64]), ALU.mult), [pbuf[pz], scrB[s]], [qkfB[s]])
                    V(tt(qkf[s][rr, 0:256], qkf[s][rr, 0:256], qkg[rr, 0:256], ALU.mult), [qkfB[s], qkgB], [qkfB[s]])
                    A(act(vf[s][rr, :], bank(pz, 128, 256)[rr, :], AF.Copy), [pbuf[pz]], [vfB[s]])
                    S.dma("sync", k_smp[:, h * 128:(h + 1) * 128], qkf[s][rr, 128:256], reads=[qkfB[s]], writes=[smpB], owner=qkfB[s])
                    S.dma("sync", v_smp[:, h * 128:(h + 1) * 128], vf[s][rr, :], reads=[vfB[s]], writes=[smpB], owner=vfB[s])
                    A(act(sq2[s][rr, 0:128], qkf[s][rr, 0:128], AF.Copy, scale=0.125), [qkfB[s], scrB[s]], [scrB[s]])
                    S.dma("sync", qsd[:, h * 128:(h + 1) * 128], sq2[s][rr, 0:128], reads=[scrB[s]], writes=[smpB], owner=scrB[s])
                if 'proj' in cfg.skip:
                    tl = []
                pbase = nscr
                nscr += len(tl)

                def projA(n, h=h, ws=ws, pbase=pbase, tl=tl):
                  if True:
                    kind, i = tl[n]
                    s = (pbase + n) % 2
                    pz = s
                    own = kind == "own"
                    gi = i if kind == "pre" else NP + i
                    if own:
                        lhs = [xnT_own[:, k, i * 128:(i + 1) * 128] for k in range(8)]
                        rB = xnB_own[i]
                        c0, nco = 0, 384
                    else:
                        lhs = [xnT_pre[:, k, i * 128:(i + 1) * 128] for k in range(8)]
                        rB = xnB_pre[i]
                        c0, nco = 128, 256
                    T([mm(bank(pz, nco), lhs[k], wh[ws][:, k, c0:c0 + nco], start=(k == 0), stop=(k == 7)) for k in range(8)],
                      [rB, whB[ws]], [pbuf[pz]])
                    nqk = 256 if own else 128
                    A(act(sq2[s][:, 0:nqk], bank(pz, nqk), AF.Square), [pbuf[pz]], [scrB[s]])
                    V(red(ssq[s][:, 0:nqk // 64], sq2[s][:, 0:nqk].rearrange("p (a b) -> p a b", b=64), ALU.add), [scrB[s]], [scrB[s]])
                    V(ts(ssq[s][:, 4:4 + nqk // 64], ssq[s][:, 0:nqk // 64], 1.0 / 64, 1e-6, op0=ALU.mult, op1=ALU.add), [scrB[s]], [scrB[s]])
                    A(act(ssq[s][:, 8:8 + nqk // 64], ssq[s][:, 4:4 + nqk // 64], AF.Sqrt), [scrB[s]], [scrB[s]])
                    V(recip(ssq[s][:, 12:12 + nqk // 64], ssq[s][:, 8:8 + nqk // 64]), [scrB[s]], [scrB[s]])
                    ng = nqk // 64
                    V(tt(qkf[s][:, 0:nqk].rearrange("p (a b) -> p a b", b=64),
                         bank(pz, nqk).rearrange("p (a b) -> p a b", b=64),
                         ssq[s][:, 12:12 + ng].unsqueeze(2).to_broadcast([128, ng, 64]), ALU.mult),
                      [pbuf[pz], scrB[s]], [qkfB[s]])
                    gsl = qkg[:, 0:256] if own else qkg[:, 128:256]
                    V(tt(qkf[s][:, 0:nqk], qkf[s][:, 0:nqk], gsl, ALU.mult), [qkfB[s], qkgB], [qkfB[s]])
                    if own:
                        A(act(qkb[s][:, 0:128], qkf[s][:, 0:128], AF.Copy, scale=0.125), [qkfB[s]], [qkbB[s]])
                        A(act(qkb[s][:, 128:256], qkf[s][:, 128:256], AF.Copy), [qkfB[s]], [qkbB[s]])
                        A(act(vf[s][:], bank(pz, 128, 256), AF.Copy), [pbuf[pz]], [vfB[s]])
                        V(cp(Vh[:, gi, 0:128], bank(pz, 128, 256)), [pbuf[pz]], [VhB[gi]])
                        S.dma("sync", k_own[i * 128:(i + 1) * 128, h * 128:(h + 1) * 128], qkf[s][:, 128:256], reads=[qkfB[s]])
                        S.dma("sync", v_own[i * 128:(i + 1) * 128, h * 128:(h + 1) * 128], vf[s][:], reads=[vfB[s]])
                        koff = 128
                    else:
                        A(act(qkb[s][:, 0:128], qkf[s][:, 0:128], AF.Copy), [qkfB[s]], [qkbB[s]])
                        V(cp(Vh[:, gi, 0:128], bank(pz, 128, 128)), [pbuf[pz]], [VhB[gi]])
                        koff = 0

                def projB(n, h=h, pbase=pbase, tl=tl):
                  if True:
                    kind, i = tl[n]
                    s = (pbase + n) % 2
                    own = kind == "own"
                    gi = i if kind == "pre" else NP + i
                    koff = 128 if own else 0
                    pt_ = 6 + s
                    fns = [tr(bankb(pt_, 128, c * 128)[0:64, :], qkb[s][:, koff + c * 64:koff + (c + 1) * 64], idb[:]) for c in range(2)]
                    if own:
                        fns += [tr(bankb(pt_, 128, 256 + c * 128)[0:64, :], qkb[s][:, c * 64:(c + 1) * 64], idb[:]) for c in range(2)]
                    T(fns, [qkbB[s], cbB], [pbuf[pt_]])
                    for c in range(2):
                        A(act(KT[c][0:64, gi * 128:(gi + 1) * 128], bankb(pt_, 128, c * 128)[0:64, :], AF.Copy),
                          [pbuf[pt_]], [KTBc[c][gi]])
                        if own:
                            A(act(QT[c][0:64, i * 128:(i + 1) * 128], bankb(pt_, 128, 256 + c * 128)[0:64, :], AF.Copy),
                              [pbuf[pt_]], [QTBc[c][i]])

                for n in range(len(tl) + 1):
                    if n < len(tl):
                        projA(n)
                    if n >= 1:
                        projB(n - 1)

                npt = 0
                for g in range(NG if cfg.attn else 0):
                    def acc(c, qt):
                        a = c * 4 + qt
                        return bank(2 + a // 3, 129, (a % 3) * 129)
                    accB = [pbuf[2], pbuf[3], pbuf[4]]
                    ktiles = [("pre", kt) for kt in range(NP)] + [("own", kt) for kt in range(4 * g + 4)]
                    nk = len(ktiles)
                    started = [[False] * 4 for _ in range(2)]
                    steps = []
                    for ki, (kind, kt) in enumerate(ktiles):
                        for c in range(2):
                            steps.append((ki, kind, kt, c))

                    def emit_SE(idx, g=g, h=h):
                        ki, kind, kt, c = steps[idx]
                        gi = kt if kind == "pre" else NP + kt
                        if kind == "pre":
                            dt_ = 4 * g + NP - kt
                            bcol = cst_sb[:, C_BP + h * 28 + (dt_ - 1):C_BP + h * 28 + dt_]
                            m0 = 0
                        else:
                            dt_ = 4 * g - kt
                            bcol = cst_sb[:, C_BO + h * 16 + (dt_ + 3):C_BO + h * 16 + dt_ + 4]
                            m0 = max(0, kt - 4 * g)
                        diag = (kind == "own" and kt >= 4 * g)
                        sbk = 5 + (idx % 3)
                        pi = idx % 4
                        qs = g * 512 + m0 * 128
                        ncol = 512 - m0 * 128
                        fns = []
                        if diag:
                            fns.append(mm(bank(sbk, 128, m0 * 128), idb[:], mneg[:], start=True, stop=False))
                            fns.append(mm(bank(sbk, 128, m0 * 128), KT[c][:, gi * 128:(gi + 1) * 128], QT[c][:, qs:qs + 128],
                                          start=False, stop=True))
                            if ncol > 128:
                                fns.append(mm(bank(sbk, ncol - 128, m0 * 128 + 128), KT[c][:, gi * 128:(gi + 1) * 128],
                                              QT[c][:, qs + 128:qs + ncol], start=True, stop=True))
                        else:
                            fns.append(mm(bank(sbk, ncol, m0 * 128), KT[c][:, gi * 128:(gi + 1) * 128], QT[c][:, qs:qs + ncol],
                                          start=True, stop=True))
                        T(fns, [KTBc[c][gi], KrowB[c], QrowB[c], cbB] + QTBc[c][g * 4 + m0:g * 4 + 4], [pbuf[sbk]])
                        A(act(Pt[pi][:, m0 * 128:512], bank(sbk, ncol, m0 * 128), AF.Exp, bias=bcol), [pbuf[sbk], cstB], [PtB[pi]])

                    def emit_AV(idx, g=g):
                        ki, kind, kt, c = steps[idx]
                        gi = kt if kind == "pre" else NP + kt
                        m0 = 0 if kind == "pre" else max(0, kt - 4 * g)
                        pi = idx % 4
                        fns = []
                        for qt in range(m0, 4):
                            last = (kind == "own" and kt == 4 * g + qt)
                            fns.append(mm(acc(c, qt), Pt[pi][:, qt * 128:(qt + 1) * 128], Vh[:, gi, :],
                                          start=(ki == 0 and (c * 4 + qt) % 3 == 0), stop=last, sgc=True))
                        T(fns, [PtB[pi], VhB[gi]], accB)

                    LA = 2
                    for idx in range(len(steps) + LA):
                        if idx < len(steps):
                            emit_SE(idx)
                        if idx >= LA:
                            emit_AV(idx - LA)
                    for qt in range(4):
                        i = g * 4 + qt
                        s = i % 2
                        a1, a2 = acc(0, qt), acc(1, qt)
                        V(recip(osm[s][:, 0:1], a1[:, 128:129]), accB, [oB[s]])
                        V(recip(osm[s][:, 1:2], a2[:, 128:129]), accB, [oB[s]])
                        V(tt(osm[s][:, 2:3], osm[s][:, 1:2], sm[:, 1:2], ALU.mult), [oB[s], smB], [oB[s]])
                        V(ts(osc[s][:], a1[:, 0:128], osm[s][:, 0:1], None, op0=ALU.mult), accB + [oB[s]], [oB[s]])
                        V(stt(osc[s][:], a2[:, 0:128], osm[s][:, 2:3], osc[s][:], ALU.mult, ALU.add), accB + [oB[s]], [oB[s]])
                        A(act(osq[:], osc[s][:], AF.Square, accum_out=osm[s][:, 3:4]), [oB[s]], [osqB, oB[s]])
                        V(ts(osm[s][:, 4:5], osm[s][:, 3:4], 1.0 / 128, 1e-6, op0=ALU.mult, op1=ALU.add), [oB[s]], [oB[s]])
                        A(act(osm[s][:, 5:6], osm[s][:, 4:5], AF.Sqrt), [oB[s]], [oB[s]])
                        V(recip(osm[s][:, 6:7], osm[s][:, 5:6]), [oB[s]], [oB[s]])
                        V(ts(osm[s][:, 7:8], osm[s][:, 6:7], 0.8, None, op0=ALU.mult), [oB[s]], [oB[s]])
                        V(stt(ab[s][:], osc[s][:], osm[s][:, 7:8], g_sb[:, h * 128:(h + 1) * 128], ALU.mult, ALU.mult),
                          [oB[s], gB], [abB[s]])
                        if cfg.dbg:
                            V(stt(af[s][:], osc[s][:], osm[s][:, 7:8], g_sb[:, h * 128:(h + 1) * 128], ALU.mult, ALU.mult),
                              [oB[s], gB], [afB[s]])
                            S.dma("sync", dbg_a[i * 128:(i + 1) * 128, h * 128:(h + 1) * 128], af[s][:], reads=[afB[s]])
                        pt_ = 0 + s
                        T([tr(bankb(pt_, 128, 0), ab[s][:], idb[:])], [abB[s], cbB], [pbuf[pt_]])
                        A(act(aT[:, h, i * 128:(i + 1) * 128], bankb(pt_, 128, 0), AF.Copy), [pbuf[pt_]], [aTB[i]])
        if cfg.sample:
            S.barrier()
            with Mark() as p1b:
                dc = sb("dc", [128, 1024], F32)
                dcB = Buf("dc")
                S.dma("sync", dc[:], dcst, writes=[dcB])
                D_IOP = 520
                D_MC = 528
                pti = sb("pti", [128, 64], I32)
                ptf = sb("ptf", [128, 64], F32)
                pidx = sb("pidx", [128, 64], I32)
                ptB = Buf("pt")
                qb = sb("qb", [128, 1024], F32)
                qbB = Buf("qb")
                Kpg = [sb("Kpg%d" % i, [128, 1024], F32) for i in range(2)]
                Vpg = [sb("Vpg%d" % i, [128, 1024], F32) for i in range(2)]
                KpB = [Buf("Kp0"), Buf("Kp1")]
                VpB = [Buf("Vp0"), Buf("Vp1")]
                Kx = sb("Kx", [128, 1024], F32)
                Vx = sb("Vx", [128, 1024], F32)
                KxB, VxB = Buf("Kx"), Buf("Vx")
                prod = sb("prod", [128, 1024], F32)
                prodB = Buf("prod")
                scs = sb("scs", [128, 65, 16], F32)
                Pall = sb("Pall", [128, 65, 16], F32)
                scB, PaB = Buf("scs"), Buf("Pall")
                dsm = sb("dsm", [128, 64], F32)
                dsmB = Buf("dsm")
                On = sb("On", [2, 1024], F32)
                OnB = Buf("On")
                osx = sb("osx", [NS, 1024], F32)
                osxq = sb("osxq", [NS, 1024], F32)
                asx = sb("asx", [NS, 1024], BF16)
                osxB, asxB = Buf("osx"), Buf("asx")
                V(mset(Kx[:], 0.0), [], [KxB])
                V(mset(Vx[:], 0.0), [], [VxB])
                V(ts(dsm[0:2, 0:1], sm[0:2, 1:2], -1.0, None, op0=ALU.add), [smB], [dsmB])
                V(ts(dsm[0:2, 1:2], dc[0:2, D_IOP:D_IOP + 1], dsm[0:2, 0:1], 1.0, op0=ALU.mult, op1=ALU.add), [dsmB, dcB], [dsmB])
                V(ts(dsm[0:2, 16:32], dc[0:2, D_MC:D_MC + 16], dsm[0:2, 1:2], None, op0=ALU.mult), [dsmB, dcB], [dsmB])
                V(mset(dsm[:, 32:33], 1.0), [], [dsmB])
                for si in range(NS):
                    S.dma("sync", pti[:], ptab[si:si + 1, :].partition_broadcast(128), writes=[ptB])
                    V(cp(ptf[:], pti[:]), [ptB], [ptB])
                    V(ts(ptf[:], ptf[:], 128.0, dc[:, D_IOP:D_IOP + 1], op0=ALU.mult, op1=ALU.add), [ptB, dcB], [ptB])
                    V(cp(pidx[:], ptf[:]), [ptB], [ptB])
                    S.dma("sync", qb[:], qsd[si:si + 1, :].partition_broadcast(128), reads=[smpB], writes=[qbB])
                    S.dma("sync", Kx[0:1, :], k_smp[si:si + 1, :], reads=[smpB], writes=[KxB])
                    S.dma("sync", Vx[0:1, :], v_smp[si:si + 1, :], reads=[smpB], writes=[VxB])
                    for pg in range(65):
                        s = pg % 2
                        if pg < 64:
                            S.custom_dma("gpsimd", lambda e, s=s, pg=pg: e.indirect_dma_start(
                                out=Kpg[s][:], out_offset=None, in_=cache_k,
                                in_offset=bass.IndirectOffsetOnAxis(ap=pidx[:, pg:pg + 1], axis=0)), reads=[ptB], writes=[KpB[s]])
                            S.custom_dma("gpsimd", lambda e, s=s, pg=pg: e.indirect_dma_start(
                                out=Vpg[s][:], out_offset=None, in_=cache_v,
                                in_offset=bass.IndirectOffsetOnAxis(ap=pidx[:, pg:pg + 1], axis=0)), reads=[ptB], writes=[VpB[s]])
                            kt_, kB_, vt_, vB_ = Kpg[s], KpB[s], Vpg[s], VpB[s]
                        else:
                            kt_, kB_, vt_, vB_ = Kx, KxB, Vx, VxB
                        V(tt(prod[:], kt_[:], qb[:], ALU.mult), [kB_, qbB], [prodB])
                        V(red(scs[:, pg, :], prod[:].rearrange("p (a b) -> p a b", b=64), ALU.add), [prodB], [scB])
                        V(tt(scs[:, pg, :].rearrange("p (h c) -> p h c", c=2), scs[:, pg, :].rearrange("p (h c) -> p h c", c=2),
                             dc[:, pg * 8:(pg + 1) * 8].unsqueeze(2).to_broadcast([128, 8, 2]), ALU.add), [scB, dcB], [scB])
                        A(act(Pall[:, pg, :], scs[:, pg, :], AF.Exp), [scB], [PaB])
                        T([mm(bank(h // 4, 128, (h % 4) * 128)[0:2, :], Pall[:, pg, h * 2:h * 2 + 2], vt_[:, h * 128:(h + 1) * 128],
                              start=(pg == 0 and h % 4 == 0), stop=(pg == 64), sgc=True) for h in range(8)], [PaB, vB_], [pbuf[0], pbuf[1]])
                    V(red(dsm[:, 40:56], Pall[:].rearrange("p g x -> p x g"), ALU.add), [PaB], [dsmB])
                    T([mm(bank(2, 1, h)[0:2, :], dsm[:, 40 + h * 2:42 + h * 2], dsm[:, 32:33], start=(h == 0), stop=(h == 7), sgc=True)
                       for h in range(8)], [dsmB], [pbuf[2]])
                    V(recip(dsm[0:2, 2:10], bank(2, 8, 0)[0:2, :]), [pbuf[2]], [dsmB])
                    for hb_ in range(2):
                        V(tt(On[:, hb_ * 512:(hb_ + 1) * 512].rearrange("p (h d) -> p h d", h=4),
                             bank(hb_, 512)[0:2, :].rearrange("p (h d) -> p h d", h=4),
                             dsm[0:2, 2 + hb_ * 4:6 + hb_ * 4].unsqueeze(2).to_broadcast([2, 4, 128]), ALU.mult), [pbuf[hb_], dsmB], [OnB])
                    T([mm(bank(3 + hb_, 512)[0:NS, :], dsm[0:2, 16 + si * 4:20 + si * 4], On[:, hb_ * 512:(hb_ + 1) * 512],
                          start=(si == 0), stop=(si == NS - 1)) for hb_ in range(2)], [dsmB, OnB], [pbuf[3], pbuf[4]])
                rr = slice(0, NS)
                for hb_ in range(2):
                    V(cp(osx[rr, hb_ * 512:(hb_ + 1) * 512], bank(3 + hb_, 512)[rr, :]), [pbuf[3 + hb_]], [osxB])
                A(act(osxq[rr, :], osx[rr, :], AF.Square), [osxB], [osxB])
                V(red(dsm[rr, 56:64], osxq[rr, :].rearrange("p (h d) -> p h d", h=8), ALU.add), [osxB], [dsmB])
                V(ts(dsm[rr, 56:64], dsm[rr, 56:64], 1.0 / 128, 1e-6, op0=ALU.mult, op1=ALU.add), [dsmB], [dsmB])
                A(act(dsm[rr, 56:64], dsm[rr, 56:64], AF.Sqrt), [dsmB], [dsmB])
                V(recip(dsm[rr, 56:64], dsm[rr, 56:64]), [dsmB], [dsmB])
                V(ts(dsm[rr, 56:64], dsm[rr, 56:64], 0.8, None, op0=ALU.mult), [dsmB], [dsmB])
                V(tt(osx[rr, :].rearrange("p (h d) -> p h d", h=8), osx[rr, :].rearrange("p (h d) -> p h d", h=8),
                     dsm[rr, 56:64].unsqueeze(2).to_broadcast([NS, 8, 128]), ALU.mult), [osxB, dsmB], [osxB])
                V(tt(asx[rr, :], osx[rr, :], g_sb[rr, :], ALU.mult), [osxB, gB], [asxB])
                if cfg.dbg:
                    V(tt(osxq[rr, :], osx[rr, :], g_sb[rr, :], ALU.mult), [osxB, gB], [osxB])
                    S.dma("sync", dbg_a[TO:TO + NS, :], osxq[rr, :], reads=[osxB])
                T([tr(bankb(5, NS, k * 128), asx[rr, k * 128:(k + 1) * 128], idb[rr, rr]) for k in range(8)], [asxB, cbB], [pbuf[5]])
                A(act(aT[:, :, TO:TO + NS], bankb(5, 1024).rearrange("p (k n) -> p k n", k=8)[:, :, 0:NS], AF.Copy), [pbuf[5]], [aTB[NO]])
        S.barrier()

        NC = NP + NO
        S.dma("sync", g_sb[:], prow[:, 3328:4352].partition_broadcast(128), writes=[gB])
        with Mark() as p2:
            gc = sb("gc", [128, 8], F32)
            gcB = Buf("gc")
            S.dma("sync", gc[:], gcol, writes=[gcB])
            wip = sb("wip", [128, 8, 252], BF16)
            wfp = sb("wfp", [128, 8, 252], BF16)
            wpB = Buf("wpad")
            G(mset(wip[:], 0.0), [], [wpB])
            G(mset(wfp[:], 0.0), [], [wpB])
            S.dma("gpsimd", wip[:, :, 124:128], w_in[:, 6144:6148].rearrange("(k p) c -> p k c", p=128), writes=[wpB])
            S.dma("gpsimd", wfp[:, :, 124:128], w_in[:, 6148:6152].rearrange("(k p) c -> p k c", p=128), writes=[wpB])
            wm = [sb("wm%d" % i, [128, 8, 768], BF16) for i in range(2)]
            wmB = [Buf("wm0"), Buf("wm1")]

            def load_wm(h):
                s_ = h % 2
                for (c0, n_, o_) in ((3072 + h * 128, 128, 0), (3584 + h * 128, 128, 128), (4096 + h * 256, 256, 256), (5120 + h * 256, 256, 512)):
                    S.dma("gpsimd", wm[s_][:, :, o_:o_ + n_], w_in[:, c0:c0 + n_].rearrange("(k p) c -> p k c", p=128), writes=[wmB[s_]])

            load_wm(0)

            def xchunk(c):
                if c < NP:
                    return [xnT_pre[:, k, c * 128:(c + 1) * 128] for k in range(8)], xnB_pre[c]
                return [xnT_own[:, k, (c - NP) * 128:(c - NP + 1) * 128] for k in range(8)], xnB_own[c - NP]

            gt = sb("gt", [128, 16, 128], F32)
            gtB = Buf("gt")
            fns = []
            for c in range(NC):
                lhs, rB = xchunk(c)
                for k in range(8):
                    fns.append(mm(bank(0, 128, 0), wip[:, k, 124 - 4 * c:252 - 4 * c], lhs[k], start=(c == 0 and k == 0),
                                  stop=(c == NC - 1 and k == 7), sgc=True))
                    fns.append(mm(bank(0, 128, 128), wfp[:, k, 124 - 4 * c:252 - 4 * c], lhs[k], start=False,
                                  stop=(c == NC - 1 and k == 7), sgc=True))
            T(fns, xnB_pre + xnB_own[:NO] + [wpB], [pbuf[0]])
            IG, U_, E_, L_, LF, FL, Fg, Gg, WL, W_, T1, FLO = [gt[:, i, :] for i in range(12)]
            ZER = gt[:, 12, :]
            V(mset(ZER, 0.0), [], [gtB])
            A(act(IG, bank(0, 128, 0), AF.Identity, bias=gc[:, 0:1]), [pbuf[0], gcB], [gtB])
            V(ts(IG, IG, gc[:, 2:3], gc[:, 3:4], op0=ALU.mult, op1=ALU.add), [gtB, gcB], [gtB])
            A(act(U_, bank(0, 128, 128), AF.Identity, bias=gc[:, 1:2]), [pbuf[0], gcB], [gtB])
            A(act(E_, U_, AF.Exp, scale=-1.0), [gtB], [gtB])
            A(act(L_, E_, AF.Ln, bias=1.0), [gtB], [gtB])
            V(ts(LF, L_, gc[:, 4:5], None, op0=ALU.mult), [gtB, gcB], [gtB])
            V(lambda e: e.tensor_tensor_scan(out=FL, data0=LF, data1=ZER, initial=0.0, op0=ALU.add, op1=ALU.add), [gtB], [gtB])
            T([mm(bank(1, 1, 0), cst_sb[:, 1152:1280], FL[:, 127:128])], [gtB, cstB], [pbuf[1]])
            gs = sb("gs", [128, 16], F32)
            gsB = Buf("gs")
            V(cp(gs[:, 0:1], bank(1, 1, 0)), [pbuf[1]], [gsB])
            V(ts(Fg, FL, gs[:, 0:1], None, op0=ALU.add), [gtB, gsB], [gtB])
            V(tt(Gg, IG, Fg, ALU.subtract), [gtB], [gtB])
            V(red(gs[:, 1:2], Gg, ALU.max), [gtB], [gsB])
            grow = sb("grow", [1, 8, 128], F32)
            growB = Buf("grow")
            T([tr(bank(1, 128, 128)[0:1, :], gs[:, 1:2], ident_f)], [gsB, cstB], [pbuf[1]])
            V(cp(grow[0:1, 0, :], bank(1, 128, 128)[0:1, :]), [pbuf[1]], [growB])
            for h in range(4):
                gv = grow[0:1, 0, :].rearrange("p (c h) -> p h c", h=4)[:, h, :]
                mv = grow[0:1, 1, :].rearrange("p (c h) -> p h c", h=4)[:, h, :]
                V(lambda e, gv=gv, mv=mv: e.tensor_tensor_scan(out=mv, data0=gv, data1=gv, initial=0.0, op0=ALU.max, op1=ALU.max),
                  [growB], [growB])
            V(mset(grow[0:1, 2, 0:4], 0.0), [], [growB])
            V(cp(grow[0:1, 2, 4:128], grow[0:1, 1, 0:124]), [growB], [growB])
            V(tt(grow[0:1, 3, :], grow[0:1, 2, :], grow[0:1, 1, :], ALU.subtract), [growB], [growB])
            A(act(grow[0:1, 4, :], grow[0:1, 3, :], AF.Exp), [growB], [growB])
            V(mset(grow[0:1, 5, :], 1.0), [], [growB])
            T([mm(bank(1, 1, 256), grow[0:1, 2, :], grow[0:1, 5, 0:1]),
               mm(bank(1, 1, 257), grow[0:1, 1, :], grow[0:1, 5, 0:1]),
               mm(bank(1, 128, 384), grow[0:1, 5, :], grow[0:1, 4, :])], [growB], [pbuf[1]])
            V(cp(gs[:, 2:4], bank(1, 2, 256)), [pbuf[1]], [gsB])
            decb = sb("decb", [128, 128], F32)
            decB = Buf("dec")
            V(cp(decb[:], bank(1, 128, 384)), [pbuf[1]], [decB])
            V(ts(WL, Gg, gs[:, 2:3], None, op0=ALU.subtract), [gtB, gsB], [gtB])
            A(act(W_, WL, AF.Exp), [gtB], [gtB])
            V(ts(T1, Fg, gs[:, 2:3], None, op0=ALU.add), [gtB, gsB], [gtB])
            A(act(FLO, T1, AF.Exp, scale=-1.0), [gtB], [gtB])
            V(tt(gs[:, 4:5], Fg[:, 127:128], gs[:, 3:4], ALU.add), [gtB, gsB], [gsB])
            S.dma("sync", m_out, gs[(NC - 1) * 4:(NC - 1) * 4 + 4, 4:5], reads=[gsB])
            wT = sb("wT", [128, 128], F32)
            flT = sb("flT", [128, 128], F32)
            wTB = Buf("wT")
            T([tr(bank(2, 128, 0), W_, ident_f), tr(bank(2, 128, 128), FLO, ident_f)], [gtB, cstB], [pbuf[2]])
            V(cp(wT[:], bank(2, 128, 0)), [pbuf[2]], [wTB])
            V(cp(flT[:], bank(2, 128, 128)), [pbuf[2]], [wTB])

            Cf = sb("Cf", [128, 257], F32)
            Ct = sb("Ct", [128, 257], F32)
            Cb = sb("Cb", [128, 257], BF16)
            CfB, CtB, CbB = Buf("Cf"), Buf("Ct"), Buf("Cb")
            Kb = [sb("Kb%d" % i, [128, 128], BF16) for i in range(2)]
            Vw = [sb("Vw%d" % i, [128, 257], BF16) for i in range(2)]
            QTb = [sb("QTb%d" % i, [128, 128], BF16) for i in range(2)]
            KTb = [sb("KTb%d" % i, [128, 128], BF16) for i in range(2)]
            PTm = [sb("PTm%d" % i, [128, 128], BF16) for i in range(2)]
            hh = [sb("hh%d" % i, [128, 256], F32) for i in range(2)]
            og = [sb("og%d" % i, [128, 256], F32) for i in range(2)]
            hbb = [sb("hbb%d" % i, [128, 256], BF16) for i in range(2)]
            hsq = sb("hsq", [128, 256], F32)
            hsm = [sb("hsm%d" % i, [128, 8], F32) for i in range(2)]
            KbB, VwB, QKB, PTB, hhB, ogB, hbB, hsB = [[Buf(n_ + "0"), Buf(n_ + "1")] for n_ in
                                                     ("Kb", "Vw", "QK", "PT", "hh", "og", "hb", "hs")]
            hsqB = Buf("hsq")
            if cfg.dbg:
                hbf = [sb("hbf%d" % i, [128, 256], F32) for i in range(2)]
                hbfB = [Buf(), Buf()]
            m01 = cst_sb[:, 1024:1152]
            for h in range(4 if 'mh' not in cfg.skip else 0):
                if h + 1 < 4:
                    load_wm(h + 1)
                ws = h % 2
                V(mset(Cf[:], 0.0), [], [CfB])
                G(mset(Cb[:], 0.0), [], [CbB])
                for c in range(NC):
                    s = c % 2
                    own = c >= NP
                    i = c - NP
                    lhs, rB = xchunk(c)
                    col = c * 4 + h
                    pA, pB = (0, 1) if s == 0 else (2, 3)
                    T([mm(bank(pA, 384), lhs[k], wm[ws][:, k, 128:512], start=(k == 0), stop=(k == 7)) for k in range(8)],
                      [rB, wmB[ws]], [pbuf[pA]])
                    A(act(Kb[s][:], bank(pA, 128, 0), AF.Copy, scale=float(128 ** -0.5)), [pbuf[pA]], [KbB[s]])
                    V(ts(Vw[s][:, 0:256], bank(pA, 256, 128), wT[:, col:col + 1], None, op0=ALU.mult), [pbuf[pA], wTB], [VwB[s]])
                    V(cp(Vw[s][:, 256:257], wT[:, col:col + 1]), [wTB], [VwB[s]])
                    if own and 'own2' not in cfg.skip:
                        T([mm(bank(pB, 256), lhs[k], wm[ws][:, k, 512:768], start=(k == 0), stop=(k == 7)) for k in range(8)],
                          [rB, wmB[ws]], [pbuf[pB]])
                        A(act(og[s][:], bank(pB, 256), AF.Sigmoid), [pbuf[pB]], [ogB[s]])
                        T([mm(bank(4, 128, 0), wm[ws][:, k, 0:128], lhs[k], start=(k == 0), stop=(k == 7)) for k in range(8)] +
                          [mm(bank(4, 128, 128), wm[ws][:, k, 128:256], lhs[k], start=(k == 0), stop=(k == 7)) for k in range(8)],
                          [rB, wmB[ws]], [pbuf[4]])
                        A(act(QTb[s][:], bank(4, 128, 0), AF.Copy), [pbuf[4]], [QKB[s]])
                        A(act(KTb[s][:], bank(4, 128, 128), AF.Copy, scale=float(128 ** -0.5)), [pbuf[4]], [QKB[s]])
                        T([mm(bank(5, 128), KTb[s][:], QTb[s][:])], [QKB[s]], [pbuf[5]])
                        V(tt(PTm[s][:], bank(5, 128), m01, ALU.mult), [pbuf[5], cstB], [PTB[s]])
                        T([mm(bank(6, 257), QTb[s][:], Cb[:], start=True, stop=False),
                           mm(bank(6, 257), PTm[s][:], Vw[s][:], start=False, stop=True)], [QKB[s], CbB, PTB[s], VwB[s]], [pbuf[6]])
                        A(act(hsm[s][:, 6:7], bank(6, 1, 256), AF.Abs), [pbuf[6]], [hsB[s]])
                        V(ts(hsm[s][:, 0:1], hsm[s][:, 6:7], flT[:, col:col + 1], None, op0=ALU.max), [hsB[s], wTB], [hsB[s]])
                        V(recip(hsm[s][:, 1:2], hsm[s][:, 0:1]), [hsB[s]], [hsB[s]])
                        V(ts(hh[s][:], bank(6, 256, 0), hsm[s][:, 1:2], None, op0=ALU.mult), [pbuf[6], hsB[s]], [hhB[s]])
                        A(act(hsq[:], hh[s][:], AF.Square, accum_out=hsm[s][:, 2:3]), [hhB[s]], [hsqB, hsB[s]])
                        V(ts(hsm[s][:, 3:4], hsm[s][:, 2:3], 1.0 / 256, 1e-6, op0=ALU.mult, op1=ALU.add), [hsB[s]], [hsB[s]])
                        A(act(hsm[s][:, 4:5], hsm[s][:, 3:4], AF.Sqrt), [hsB[s]], [hsB[s]])
                        V(recip(hsm[s][:, 5:6], hsm[s][:, 4:5]), [hsB[s]], [hsB[s]])
                        V(stt(hh[s][:], hh[s][:], hsm[s][:, 5:6], g_sb[:, h * 256:(h + 1) * 256], ALU.mult, ALU.mult),
                          [hhB[s], hsB[s], gB], [hhB[s]])
                        V(tt(hbb[s][:], hh[s][:], og[s][:], ALU.mult), [hhB[s], ogB[s]], [hbB[s]])
                        if cfg.dbg:
                            V(tt(hbf[s][:], hh[s][:], og[s][:], ALU.mult), [hhB[s], ogB[s]], [hbfB[s]])
                            S.dma("sync", dbg_hb[i * 128:(i + 1) * 128, h * 256:(h + 1) * 256], hbf[s][:], reads=[hbfB[s]])
                        T([tr(bankb(5, 128, 512 + j * 128), hbb[s][:, j * 128:(j + 1) * 128], idb[:]) for j in range(2)],
                          [hbB[s], cbB], [pbuf[5]])
                        A(act(hbT[:, h * 2:h * 2 + 2, i * 128:(i + 1) * 128],
                              bankb(5, 256, 512).rearrange("p (j n) -> p j n", j=2), AF.Copy), [pbuf[5]], [hbTB[i]])
                    if 'state' in cfg.skip:
                        continue
                    T([mm(bank(7, 257), Kb[s][:], Vw[s][:])], [KbB[s], VwB[s]], [pbuf[7]])
                    V(tt(Ct[:], bank(7, 257), Cf[:], ALU.add), [pbuf[7], CfB], [CtB])
                    V(ts(Cf[:], Ct[:], decb[:, col:col + 1], None, op0=ALU.mult), [CtB, decB], [CfB])
                    A(act(Cb[:], Cf[:], AF.Copy), [CfB], [CbB])
                S.dma("sync", C_out[h], Cf[:, 0:256], reads=[CfB])
                S.dma("sync", n_out[h].rearrange("(p o) -> p o", o=1), Cf[:, 256:257], reads=[CfB])
        S.barrier()

        if cfg.sample:
            with Mark() as p2b:
                rr = slice(0, NS)
                xs_ = [xnT_own[:, k, TO:TO + NS] for k in range(8)]
                wms = sb("wms", [128, 8, 768], BF16)
                wmsB = Buf("wms")
                wif = sb("wif", [128, 8, 8], BF16)
                wifB = Buf("wif")
                S.dma("gpsimd", wif[:], w_in[:, 6144:6152].rearrange("(k p) c -> p k c", p=128), writes=[wifB])
                bro = sb("bro", [NS, 8], F32)
                m0t = sb("m0t", [NS, 4], F32)
                n0t = sb("n0t", [NS, 512], F32)
                nnt = sb("nnt", [NS, 512], F32)
                ldB = Buf("s_ld")
                S.dma("sync", bro[:], brow.partition_broadcast(NS), writes=[ldB])
                S.dma("sync", m0t[:], smm, writes=[ldB])
                S.dma("sync", n0t[:], sn, writes=[ldB])
                dc2 = sb("dc2", [128, 64], F32)
                dc2B = Buf("dc2")
                S.dma("sync", dc2[:], dcst[:, 512:576], writes=[dc2B])
                g4 = sb("g4", [NS, 64], F32)
                g4B = Buf("g4")
                nnB = Buf("nnt")
                T([mm(bank(2, 8)[rr, :], xs_[k], wif[:, k, :], start=(k == 0), stop=(k == 7)) for k in range(8)], [xnB_own[NO], wifB], [pbuf[2]])
                V(tt(g4[:, 0:4], bank(2, 4, 0)[rr, :], bro[:, 0:4], ALU.add), [pbuf[2], ldB], [g4B])
                V(tt(g4[:, 4:8], bank(2, 4, 4)[rr, :], bro[:, 4:8], ALU.add), [pbuf[2], ldB], [g4B])
                A(act(g4[:, 4:8], g4[:, 4:8], AF.Exp, scale=-1.0), [g4B], [g4B])
                A(act(g4[:, 4:8], g4[:, 4:8], AF.Ln, bias=1.0), [g4B], [g4B])
                V(tt(g4[:, 8:12], m0t[:], g4[:, 4:8], ALU.subtract), [g4B, ldB], [g4B])
                V(tt(g4[:, 12:16], g4[:, 8:12], g4[:, 0:4], ALU.max), [g4B], [g4B])
                V(tt(g4[:, 16:20], g4[:, 0:4], g4[:, 12:16], ALU.subtract), [g4B], [g4B])
                A(act(g4[:, 16:20], g4[:, 16:20], AF.Exp), [g4B], [g4B])
                V(tt(g4[:, 20:24], g4[:, 8:12], g4[:, 12:16], ALU.subtract), [g4B], [g4B])
                A(act(g4[:, 20:24], g4[:, 20:24], AF.Exp), [g4B], [g4B])
                A(act(g4[:, 24:28], g4[:, 12:16], AF.Exp, scale=-1.0), [g4B], [g4B])
                S.dma("sync", m_smp, g4[:, 12:16], reads=[g4B])
                Bsel = sb("Bsel", [NS, NS, 128], F32)
                BselB = Buf("Bsel")
                for si in range(NS):
                    V(cp(Bsel[:, si, :], ident_f[0:NS, si:si + 1].to_broadcast([NS, 128])), [cstB], [BselB])
                T([mm(bank(3, 4, si * 4), Bsel[:, si, :], g4[:, 20:24]) for si in range(NS)], [BselB, g4B], [pbuf[3]])
                abc = sb("abc", [128, 16], F32)
                abcB = Buf("abc")
                V(cp(abc[:], bank(3, 16, 0)), [pbuf[3]], [abcB])
                qs = sb("qs", [NS, 128], F32)
                ks = sb("ks", [NS, 128], F32)
                vs = sb("vs", [NS, 256], F32)
                os_ = sb("os_", [NS, 256], F32)
                tq = sb("tq", [NS, 128], F32)
                t2_ = sb("t2_", [NS, 256], F32)
                h4 = sb("h4", [NS, 256], F32)
                h4q = sb("h4q", [NS, 256], F32)
                hb4 = sb("hb4", [NS, 256], BF16)
                qTs = sb("qTs", [128, NS], F32)
                qTm = sb("qTm", [128, NS, NS], F32)
                kwm = sb("kwm", [NS, NS, 128], F32)
                C0t = [sb("C0t%d" % i, [128, 256], F32) for i in range(NS)]
                Cn = [sb("Cn%d" % i, [128, 256], F32) for i in range(2)]
                qsB, vsB, osB, tqB, h4B, hb4B, qTB_, kwmB = [Buf(n_) for n_ in ("qs", "vs", "os", "tq", "h4", "hb4", "qTs", "kwm")]
                C0B = [Buf("C0t%d" % i) for i in range(NS)]
                CnB = [Buf("Cn0"), Buf("Cn1")]
                ncn = 0
                for h in range(4):
                    for (c0_, n_, o_) in ((3072 + h * 128, 128, 0), (3584 + h * 128, 128, 128), (4096 + h * 256, 256, 256), (5120 + h * 256, 256, 512)):
                        S.dma("gpsimd", wms[:, :, o_:o_ + n_], w_in[:, c0_:c0_ + n_].rearrange("(k p) c -> p k c", p=128), writes=[wmsB])
                    for si in range(NS):
                        S.dma("sync", C0t[si][:], sC[si * 4 + h], writes=[C0B[si]])
                    T([mm(bank(0, 512)[rr, :], xs_[k], wms[:, k, 0:512], start=(k == 0), stop=(k == 7)) for k in range(8)], [xnB_own[NO], wmsB], [pbuf[0]])
                    T([mm(bank(1, 256)[rr, :], xs_[k], wms[:, k, 512:768], start=(k == 0), stop=(k == 7)) for k in range(8)], [xnB_own[NO], wmsB], [pbuf[1]])
                    T([mm(bank(4, NS, 0), wms[:, k, 0:128], xs_[k], start=(k == 0), stop=(k == 7)) for k in range(8)], [xnB_own[NO], wmsB], [pbuf[4]])
                    V(cp(qs[:], bank(0, 128, 0)[rr, :]), [pbuf[0]], [qsB])
                    V(ts(ks[:], bank(0, 128, 128)[rr, :], float(128 ** -0.5), None, op0=ALU.mult), [pbuf[0]], [qsB])
                    V(cp(vs[:], bank(0, 256, 256)[rr, :]), [pbuf[0]], [vsB])
                    A(act(os_[:], bank(1, 256)[rr, :], AF.Sigmoid), [pbuf[1]], [osB])
                    V(cp(qTs[:], bank(4, NS, 0)), [pbuf[4]], [qTB_])
                    V(tt(qTm[:], qTs[:].unsqueeze(1).to_broadcast([128, NS, NS]), dc2[:, 48:64].rearrange("p (a b) -> p a b", a=NS), ALU.mult),
                      [qTB_, dc2B], [qTB_])
                    V(tt(tq[:], qs[:], ks[:], ALU.mult), [qsB], [tqB])
                    V(red(g4[:, 28 + h:29 + h], tq[:], ALU.add), [tqB], [g4B])
                    V(tt(tq[:], qs[:], n0t[:, h * 128:(h + 1) * 128], ALU.mult), [qsB, ldB], [tqB])
                    V(red(g4[:, 32 + h:33 + h], tq[:], ALU.add), [tqB], [g4B])
                    V(tt(g4[:, 36 + h:37 + h], g4[:, 28 + h:29 + h], g4[:, 16 + h:17 + h], ALU.mult), [g4B], [g4B])
                    T([mm(bank(5, 256)[rr, :], qTm[:, si, :], C0t[si][:], start=(si == 0), stop=(si == NS - 1)) for si in range(NS)],
                      [qTB_] + C0B, [pbuf[5]])
                    V(ts(t2_[:], bank(5, 256)[rr, :], g4[:, 20 + h:21 + h], None, op0=ALU.mult), [pbuf[5], g4B], [h4B])
                    V(stt(h4[:], vs[:], g4[:, 36 + h:37 + h], t2_[:], ALU.mult, ALU.add), [vsB, g4B, h4B], [h4B])
                    V(stt(g4[:, 40 + h:41 + h], g4[:, 32 + h:33 + h], g4[:, 20 + h:21 + h], g4[:, 36 + h:37 + h], ALU.mult, ALU.add), [g4B], [g4B])
                    A(act(g4[:, 44 + h:45 + h], g4[:, 40 + h:41 + h], AF.Abs), [g4B], [g4B])
                    V(tt(g4[:, 44 + h:45 + h], g4[:, 44 + h:45 + h], g4[:, 24 + h:25 + h], ALU.max), [g4B], [g4B])
                    V(recip(g4[:, 48 + h:49 + h], g4[:, 44 + h:45 + h]), [g4B], [g4B])
                    V(ts(h4[:], h4[:], g4[:, 48 + h:49 + h], None, op0=ALU.mult), [h4B, g4B], [h4B])
                    A(act(h4q[:], h4[:], AF.Square, accum_out=g4[:, 52 + h:53 + h]), [h4B], [h4B, g4B])
                    V(ts(g4[:, 56 + h:57 + h], g4[:, 52 + h:53 + h], 1.0 / 256, 1e-6, op0=ALU.mult, op1=ALU.add), [g4B], [g4B])
                    A(act(g4[:, 56 + h:57 + h], g4[:, 56 + h:57 + h], AF.Sqrt), [g4B], [g4B])
                    V(recip(g4[:, 60 + h:61 + h], g4[:, 56 + h:57 + h]), [g4B], [g4B])
                    V(stt(h4[:], h4[:], g4[:, 60 + h:61 + h], g_sb[rr, h * 256:(h + 1) * 256], ALU.mult, ALU.mult), [h4B, g4B, gB], [h4B])
                    V(tt(hb4[:], h4[:], os_[:], ALU.mult), [h4B, osB], [hb4B])
                    if cfg.dbg:
                        V(tt(h4q[:], h4[:], os_[:], ALU.mult), [h4B, osB], [h4B])
                        S.dma("sync", dbg_hb[TO:TO + NS, h * 256:(h + 1) * 256], h4q[:], reads=[h4B])
                    T([tr(bankb(6, NS, j * 128), hb4[:, j * 128:(j + 1) * 128], idb[rr, rr]) for j in range(2)], [hb4B, cbB], [pbuf[6]])
                    A(act(hbT[:, h * 2:h * 2 + 2, TO:TO + NS], bankb(6, 256).rearrange("p (j n) -> p j n", j=2)[:, :, 0:NS], AF.Copy),
                      [pbuf[6]], [hbTB[NO]])
                    for si in range(NS):
                        V(ts(kwm[:, si, :], ks[:], ident_f[0:NS, si:si + 1], g4[:, 16 + h:17 + h], op0=ALU.mult, op1=ALU.mult), [qsB, cstB, g4B], [kwmB])
                    for si in range(NS):
                        cs_ = ncn % 2
                        ncn += 1
                        T([mm(bank(6 + cs_, 256) if False else bank(2 + cs_, 256), kwm[:, si, :], vs[:])], [kwmB, vsB], [pbuf[2 + cs_]])
                        V(stt(Cn[cs_][:], C0t[si][:], abc[:, si * 4 + h:si * 4 + h + 1], bank(2 + cs_, 256), ALU.mult, ALU.add),
                          [C0B[si], abcB, pbuf[2 + cs_]], [CnB[cs_]])
                        S.dma("sync", C_smp[si * 4 + h], Cn[cs_][:], reads=[CnB[cs_]])
                    V(ts(tq[:], ks[:], g4[:, 16 + h:17 + h], None, op0=ALU.mult), [qsB, g4B], [tqB])
                    V(stt(nnt[:, h * 128:(h + 1) * 128], n0t[:, h * 128:(h + 1) * 128], g4[:, 20 + h:21 + h], tq[:], ALU.mult, ALU.add),
                      [ldB, g4B, tqB], [nnB])
                S.dma("sync", n_smp, nnt[:], reads=[nnB])
            S.barrier()

        mtiles = [(i, 128, i * 128) for i in range(NO)]
        if cfg.sample and 'smerge' not in cfg.skip:
            mtiles.append((NO, NS, TO))
        x1B = [Buf("x1d%d" % i) for i in range(NO + 1)]
        apos[0] = pre_off
        with Mark() as p3a:
            wg = sb("wg", [128, 8, 2048], BF16)
            wa = sb("wa", [128, 8, 1024], BF16)
            wb_ = sb("wb_", [128, 8, 1024], BF16)
            wgB, waB, wbB = Buf("wg"), Buf("wa"), Buf("wb")
            for q4 in range(4):
                S.dma("gpsimd", wg[:, :, q4 * 512:(q4 + 1) * 512],
                      w_in[:, 6152 + q4 * 512:6152 + (q4 + 1) * 512].rearrange("(k p) c -> p k c", p=128), writes=[wgB])
            for q2 in range(2):
                S.dma("gpsimd", wa[:, :, q2 * 512:(q2 + 1) * 512], w_a[:, q2 * 512:(q2 + 1) * 512].rearrange("(k p) c -> p k c", p=128), writes=[waB])
                S.dma("gpsimd", wb_[:, :, q2 * 512:(q2 + 1) * 512], w_b[:, q2 * 512:(q2 + 1) * 512].rearrange("(k p) c -> p k c", p=128), writes=[wbB])
            sg = sb("sg", [128, 2048], F32)
            t1 = sb("t1", [128, 1024], F32)
            t2 = sb("t2", [128, 1024], F32)
            ub = sb("ub", [128, 1024], BF16)
            sgB, t1B, t2B, ubB = Buf("sg"), Buf("t1"), Buf("t2"), Buf("ub")
            for (t, rows, c0) in mtiles:
                xl = [xnT_own[:, k, c0:c0 + rows] for k in range(8)]
                for q4 in range(4):
                    T([mm(bank(q4)[0:rows, :], xl[k], wg[:, k, q4 * 512:(q4 + 1) * 512], start=(k == 0), stop=(k == 7)) for k in range(8)],
                      [xnB_own[t], wgB], [pbuf[q4]])
                    A(act(sg[0:rows, q4 * 512:(q4 + 1) * 512], bank(q4)[0:rows, :], AF.Sigmoid), [pbuf[q4]], [sgB])
                for q2 in range(2):
                    T([mm(bank(4 + q2)[0:rows, :], aT[:, k, c0:c0 + rows], wa[:, k, q2 * 512:(q2 + 1) * 512], start=(k == 0), stop=(k == 7))
                       for k in range(8)], [aTB[t], waB], [pbuf[4 + q2]])
                    T([mm(bank(6 + q2)[0:rows, :], hbT[:, k, c0:c0 + rows], wb_[:, k, q2 * 512:(q2 + 1) * 512], start=(k == 0), stop=(k == 7))
                       for k in range(8)], [hbTB[t], wbB], [pbuf[6 + q2]])
                    V(tt(t1[0:rows, q2 * 512:(q2 + 1) * 512], bank(4 + q2)[0:rows, :], sg[0:rows, q2 * 512:(q2 + 1) * 512], ALU.mult),
                      [pbuf[4 + q2], sgB], [t1B])
                    V(tt(t2[0:rows, q2 * 512:(q2 + 1) * 512], bank(6 + q2)[0:rows, :], sg[0:rows, 1024 + q2 * 512:1024 + (q2 + 1) * 512], ALU.mult),
                      [pbuf[6 + q2], sgB], [t2B])
                G(tt(ub[0:rows, :], t1[0:rows, :], t2[0:rows, :], ALU.add), [t1B, t2B], [ubB])
                T([tr(bankb(4, rows, k * 128), ub[0:rows, k * 128:(k + 1) * 128], idb[0:rows, 0:rows]) for k in range(8)], [ubB, cbB], [pbuf[4]])
                A(act(aT[:, :, c0:c0 + rows], bankb(4, 1024).rearrange("p (k n) -> p k n", k=8)[:, :, 0:rows], AF.Copy), [pbuf[4]], [aTB[t]])
        S.barrier()
        S.dma("sync", g_sb[:], prow[:, 1024:2048].partition_broadcast(128), writes=[gB])
        apos[0] = pre_off
        with Mark() as p3b:
            wo = sb("wo", [128, 8, 1024], BF16)
            woB = Buf("wo")
            for q2 in range(2):
                S.dma("gpsimd", wo[:, :, q2 * 512:(q2 + 1) * 512], w_o[:, q2 * 512:(q2 + 1) * 512].rearrange("(k p) c -> p k c", p=128), writes=[woB])
            xr = [sb("xr%d" % i, [128, 1024], F32) for i in range(2)]
            x1 = [sb("x1_%d" % i, [128, 1024], F32) for i in range(2)]
            xq = sb("xq", [128, 1024], F32)
            xnb = [sb("xnb%d" % i, [128, 1024], BF16) for i in range(2)]
            r2 = [sb("r2_%d" % i, [128, 4], F32) for i in range(2)]
            xrB, x1sB, xnbB, r2B = [[Buf(n_ + "0"), Buf(n_ + "1")] for n_ in ("xr", "x1s", "xnb", "r2")]
            xqB = Buf("xq")
            for n, (t, rows, c0) in enumerate(mtiles):
                s = n % 2
                src = x_smp[0:rows, :] if t == NO else x_own[c0:c0 + rows, :]
                S.dma("sync", xr[s][0:rows, :], src, writes=[xrB[s]])
                for q2 in range(2):
                    T([mm(bank(q2)[0:rows, :], aT[:, k, c0:c0 + rows], wo[:, k, q2 * 512:(q2 + 1) * 512], start=(k == 0), stop=(k == 7))
                       for k in range(8)], [aTB[t], woB], [pbuf[q2]])
                    V(tt(x1[s][0:rows, q2 * 512:(q2 + 1) * 512], bank(q2)[0:rows, :], xr[s][0:rows, q2 * 512:(q2 + 1) * 512], ALU.add),
                      [pbuf[q2], xrB[s]], [x1sB[s]])
                S.dma("sync", x1d[c0:c0 + rows, :], x1[s][0:rows, :], reads=[x1sB[s]], writes=[x1B[t]])
                if cfg.dbg:
                    S.dma("sync", dbg_x1[c0:c0 + rows, :], x1[s][0:rows, :], reads=[x1sB[s]])
                A(act(xq[0:rows, :], x1[s][0:rows, :], AF.Square, accum_out=r2[s][0:rows, 0:1]), [x1sB[s]], [xqB, r2B[s]])
                V(ts(r2[s][0:rows, 1:2], r2[s][0:rows, 0:1], 1.0 / 1024, 1e-6, op0=ALU.mult, op1=ALU.add), [r2B[s]], [r2B[s]])
                A(act(r2[s][0:rows, 2:3], r2[s][0:rows, 1:2], AF.Sqrt), [r2B[s]], [r2B[s]])
                V(recip(r2[s][0:rows, 3:4], r2[s][0:rows, 2:3]), [r2B[s]], [r2B[s]])
                V(stt(xnb[s][0:rows, :], x1[s][0:rows, :], r2[s][0:rows, 3:4], g_sb[0:rows, :], ALU.mult, ALU.mult),
                  [x1sB[s], r2B[s], gB], [xnbB[s]])
                pbk = 2 + s
                T([tr(bankb(pbk, rows, k * 128), xnb[s][0:rows, k * 128:(k + 1) * 128], idb[0:rows, 0:rows]) for k in range(8)],
                  [xnbB[s], cbB], [pbuf[pbk]])
                A(act(xnT_own[:, :, c0:c0 + rows], bankb(pbk, 1024).rearrange("p (k n) -> p k n", k=8)[:, :, 0:rows], AF.Copy),
                  [pbuf[pbk]], [xnB_own[t]])
        S.barrier()

        if cfg.peer:
            precast(128)
            xn2T = xnT_own
            TPE = TO + (NS if (cfg.sample and 'smerge' not in cfg.skip) else 0)
            C_IO16 = 1280
            C_IO128 = 1408
            apos[0] = aT_off
            I1T = sb("I1T", [128, TOS], F32)
            I2T = sb("I2T", [128, TOS], F32)
            gT = sb("gT", [128, TOS], F32)
            selB = [Buf("sel%d" % i) for i in range(NO + 1)]
            with Mark() as p4a:
                qT = sb("qT", [128, 16, TOS], BF16)
                qTB = Buf("qT")
                with Mark() as p4a1:
                    wqb = sb("wqb", [128, 8, 2048], BF16)
                    wqB = Buf("wq")
                    for q4 in range(4):
                        S.dma("gpsimd", wqb[:, :, q4 * 512:(q4 + 1) * 512], wq[:, q4 * 512:(q4 + 1) * 512].rearrange("(k p) c -> p k c", p=128),
                              writes=[wqB])
                    blks = [(b0, min(512, TPE - b0)) for b0 in range(0, TPE, 512)]
                    nb = 0
                    for hc in range(16):
                        for (b0, bn) in blks:
                            pk = nb % 4
                            nb += 1
                            T([mm(bank(pk, bn), wqb[:, k, hc * 128:(hc + 1) * 128], xn2T[:, k, b0:b0 + bn], start=(k == 0), stop=(k == 7))
                               for k in range(8)], xnB_own + [wqB], [pbuf[pk]])
                            A(act(qT[:, hc, b0:b0 + bn], bank(pk, bn), AF.Copy), [pbuf[pk]], [qTB])
                S.barrier()
                skb = sb("skb", [128, 256], BF16)
                skB = Buf("sk")
                S.dma("gpsimd", skb[:], skT, writes=[skB])
                ssb = sb("ssb", [128, 16, 128], F32)
                wrk = sb("wrk", [128, 16, 128], F32)
                top = sb("top", [128, 16, 16], F32)
                idx = sb("idx", [128, 16, 16], U32)
                idxf = sb("idxf", [128, 16, 16], F32)
                cand = sb("cand", [128, 8, 256], F32)
                cwk = sb("cwk", [128, 8, 256], F32)
                ctop = sb("ctop", [128, 8, 16], F32)
                pos = sb("pos", [128, 8, 16], U32)
                pa_ = sb("pa_", [128, 8, 16], U32)
                pb_ = sb("pb_", [128, 8, 16], U32)
                paf = sb("paf", [128, 8, 16], F32)
                pbf = sb("pbf", [128, 8, 16], F32)
                eq = sb("eq", [128, 8, 16, 16], F32)
                sel = sb("sel", [128, 3, 128], F32)
                zz = sb("zz", [128, 16], F32)
                ssbB, wrkB, topB, idxB, candB, ctopB, posB, eqB, selsB, zzB = [Buf(n_) for n_ in
                    ("ssb", "wrk", "top", "idx", "cand", "ctop", "pos", "eq", "sels", "zz")]
                io16 = cst_sb[:, C_IO16:C_IO16 + 16]
                for (t, rows, c0) in mtiles:
                    r_ = slice(0, rows)
                    for hc in range(16):
                        T([mm(bank(hc // 4, 128, (hc % 4) * 128)[r_, :], qT[:, hc, c0:c0 + rows], skb[:, (hc % 2) * 128:(hc % 2 + 1) * 128])],
                          [qTB, skB], [pbuf[hc // 4]])
                    for q4 in range(4):
                        A(act(ssb[r_, q4 * 4:(q4 + 1) * 4, :], bank(q4)[r_, :].rearrange("p (a b) -> p a b", a=4), AF.Copy), [pbuf[q4]], [ssbB])
                    tB_ = [Buf("top%d" % hc) for hc in range(16)]
                    wB_ = [Buf("wrk%d" % hc) for hc in range(16)]
                    iB_ = [Buf("idx%d" % hc) for hc in range(16)]
                    for hc in range(16):
                        V(lambda e, hc=hc, r_=r_: e.max(out=top[r_, hc, 0:8], in_=ssb[r_, hc, :]), [ssbB], [tB_[hc]])
                    for hc in range(16):
                        V(lambda e, hc=hc, r_=r_: e.match_replace(out=wrk[r_, hc, :], in_to_replace=top[r_, hc, 0:8], in_values=ssb[r_, hc, :],
                                                          imm_value=-1e30), [ssbB, tB_[hc]], [wB_[hc]])
                    for hc in range(16):
                        V(lambda e, hc=hc, r_=r_: e.max(out=top[r_, hc, 8:16], in_=wrk[r_, hc, :]), [wB_[hc]], [tB_[hc]])
                    for hc in range(16):
                        V(lambda e, hc=hc, r_=r_: e.max_index(out=idx[r_, hc, 0:8], in_max=top[r_, hc, 0:8], in_values=ssb[r_, hc, :]),
                          [ssbB, tB_[hc]], [iB_[hc]])
                    for hc in range(16):
                        V(lambda e, hc=hc, r_=r_: e.max_index(out=idx[r_, hc, 8:16], in_max=top[r_, hc, 8:16], in_values=ssb[r_, hc, :]),
                          [ssbB, tB_[hc]], [iB_[hc]])
                    V(cp(zz[r_, 0:1], zz[r_, 0:1]), tB_ + iB_ + wB_ + [zzB], [topB, idxB, wrkB, zzB])
                    V(cp(idxf[r_], idx[r_]), [idxB], [idxB])
                    top4 = top[r_].rearrange("p (h c) k -> p h c k", c=2)
                    V(tt(cand[r_].rearrange("p h (a b) -> p h a b", a=16),
                         top4[:, :, 0, :].unsqueeze(3).to_broadcast([rows, 8, 16, 16]),
                         top4[:, :, 1, :].unsqueeze(2).to_broadcast([rows, 8, 16, 16]), ALU.add), [topB], [candB])
                    cB_ = [Buf("ctop%d" % h) for h in range(8)]
                    cwB_ = [Buf("cwk%d" % h) for h in range(8)]
                    pB_ = [Buf("pos%d" % h) for h in range(8)]
                    for h in range(8):
                        V(lambda e, h=h, r_=r_: e.max(out=ctop[r_, h, 0:8], in_=cand[r_, h, :]), [candB, ctopB], [cB_[h]])
                    for h in range(8):
                        V(lambda e, h=h, r_=r_: e.match_replace(out=cwk[r_, h, :], in_to_replace=ctop[r_, h, 0:8], in_values=cand[r_, h, :],
                                                        imm_value=-1e30), [candB, cB_[h], wrkB], [cwB_[h]])
                    for h in range(8):
                        V(lambda e, h=h, r_=r_: e.max(out=ctop[r_, h, 8:16], in_=cwk[r_, h, :]), [cwB_[h]], [cB_[h]])
                    for h in range(8):
                        V(lambda e, h=h, r_=r_: e.max_index(out=pos[r_, h, 0:8], in_max=ctop[r_, h, 0:8], in_values=cand[r_, h, :]),
                          [candB, cB_[h], posB], [pB_[h]])
                    for h in range(8):
                        V(lambda e, h=h, r_=r_: e.max_index(out=pos[r_, h, 8:16], in_max=ctop[r_, h, 8:16], in_values=cand[r_, h, :]),
                          [candB, cB_[h]], [pB_[h]])
                    V(cp(zz[r_, 0:1], zz[r_, 0:1]), cB_ + cwB_ + pB_ + [zzB], [ctopB, wrkB, posB, zzB])
                    V(lambda e, r_=r_: e.tensor_single_scalar(out=pa_[r_], in_=pos[r_], scalar=4, op=ALU.logical_shift_right), [posB], [posB])
                    V(lambda e, r_=r_: e.tensor_single_scalar(out=pb_[r_], in_=pos[r_], scalar=15, op=ALU.bitwise_and), [posB], [posB])
                    V(cp(paf[r_], pa_[r_]), [posB], [posB])
                    V(cp(pbf[r_], pb_[r_]), [posB], [posB])
                    idx4 = idxf[r_].rearrange("p (h c) k -> p h c k", c=2)
                    for which, (pf, ci) in enumerate(((paf, 0), (pbf, 1))):
                        V(tt(eq[r_], io16[r_].unsqueeze(1).unsqueeze(1).to_broadcast([rows, 8, 16, 16]),
                             pf[r_].unsqueeze(3).to_broadcast([rows, 8, 16, 16]), ALU.is_equal), [posB, cstB], [eqB])
                        V(tt(eq[r_], eq[r_], idx4[:, :, ci, :].unsqueeze(2).to_broadcast([rows, 8, 16, 16]), ALU.mult), [eqB, idxB], [eqB])
                        V(red(sel[r_, which, :].rearrange("p (h k) -> p h k", h=8), eq[r_], ALU.add), [eqB], [selsB])
                    V(cp(zz[r_, 0:8], ctop[r_, :, 0]), [ctopB], [zzB])
                    V(tt(ctop[r_], ctop[r_], zz[r_, 0:8].unsqueeze(2).to_broadcast([rows, 8, 16]), ALU.subtract), [ctopB, zzB], [ctopB])
                    A(act(ctop[r_], ctop[r_], AF.Exp), [ctopB], [ctopB])
                    V(red(zz[r_, 0:8], ctop[r_], ALU.add), [ctopB], [zzB])
                    V(recip(zz[r_, 8:16], zz[r_, 0:8]), [zzB], [zzB])
                    V(tt(sel[r_, 2, :].rearrange("p (h k) -> p h k", h=8), ctop[r_], zz[r_, 8:16].unsqueeze(2).to_broadcast([rows, 8, 16]), ALU.mult),
                      [ctopB, zzB], [selsB])
                    T([tr(bank(4, rows, j * 128), sel[r_, j, :], ident_f[r_, r_]) for j in range(3)], [selsB, cstB], [pbuf[4]])
                    A(act(I1T[:, c0:c0 + rows], bank(4, rows, 0), AF.Copy), [pbuf[4]], [selB[t]])
                    A(act(I2T[:, c0:c0 + rows], bank(4, rows, 128), AF.Copy), [pbuf[4]], [selB[t]])
                    A(act(gT[:, c0:c0 + rows], bank(4, rows, 256), AF.Copy), [pbuf[4]], [selB[t]])
            S.barrier()
            with Mark() as p4c:
                GRP = 256
                GMAX = GRP + NS
                Gst = sb("Gst", [128, 128, GMAX], BF16)
                GstB = Buf("Gst")
                NBT = 8
                NRT = 3
                L4 = [sb("L4_%d" % i, [128, NBT, 128], BF16) for i in range(NRT)]
                R4 = [sb("R4_%d" % i, [128, NBT, 128], BF16) for i in range(NRT)]
                L4B = [Buf("L4_%d" % i) for i in range(NRT)]
                R4B = [Buf("R4_%d" % i) for i in range(NRT)]
                io128 = cst_sb[:, C_IO128:C_IO128 + 128]
                NSL = 4
                utc = [sb("utc%d" % i, [128, 8, 128], BF16) for i in range(NSL)]
                vc = [sb("vc%d" % i, [128, 1024], BF16) for i in range(NSL)]
                utB, vcB = [[Buf("%s%d" % (n_, i)) for i in range(NSL)] for n_ in ("ut", "vc")]
                NCH = 4
                sq_ = [sb("sq_%d" % i, [128, GMAX], F32) for i in range(NCH)]
                in_ = [sb("in_%d" % i, [128, GMAX], F32) for i in range(NCH)]
                sg_ = [sb("sg_%d" % i, [128, GMAX], F32) for i in range(NCH)]
                w1_ = [sb("w1_%d" % i, [128, GMAX], F32) for i in range(NCH)]
                wt_ = [sb("wt_%d" % i, [128, GMAX], BF16) for i in range(NCH)]
                sqB_, inB_, sgB_, w1B_, wtB_ = [[Buf("%s%d" % (n_, i)) for i in range(NCH)] for n_ in ("sq_", "in_", "sg_", "w1_", "wt_")]
                xl_ = [sb("xl_%d" % i, [128, 1024], F32) for i in range(2)]
                yo_ = [sb("yo_%d" % i, [128, 1024], F32) for i in range(2)]
                xlB, yoB = [[Buf(n_ + "0"), Buf(n_ + "1")] for n_ in ("xl", "yo")]
                nld = 0
                nfin = 0
                ng_ = 0
                groups = []
                for g0 in range(0, TO, GRP):
                    tl_ = [mtiles[g0 // 128 + j] for j in range(min(GRP, TO - g0) // 128)]
                    groups.append([g0, tl_])
                if len(mtiles) > NO:
                    groups[-1][1] = groups[-1][1] + [mtiles[NO]]
                for g0, tl_ in groups:
                    gn = sum(r_ for (_, r_, _) in tl_)
                    gt_ = [t for (t, _, _) in tl_]
                    for (t, rows, c0) in tl_:
                        for n0 in range(0, rows, NBT):
                            s = ng_ % NRT
                            pk = 4 + 2 * (ng_ % 2)
                            ng_ += 1
                            nn = min(NBT, rows - n0)
                            o_ = c0 - g0 + n0
                            iob = io128.unsqueeze(1).to_broadcast([128, nn, 128])
                            V(tt(L4[s][:, 0:nn, :], iob, I1T[:, c0 + n0:c0 + n0 + nn].unsqueeze(2).to_broadcast([128, nn, 128]), ALU.is_equal),
                              [selB[t], cstB], [L4B[s]])
                            for j in range(nn):
                                V(ts(R4[s][:, j, :], io128, I2T[:, c0 + n0 + j:c0 + n0 + j + 1], gT[:, c0 + n0 + j:c0 + n0 + j + 1],
                                     op0=ALU.is_equal, op1=ALU.mult), [selB[t], cstB], [R4B[s]])
                            T([mm(bank(pk + j // 4, 128, (j % 4) * 128), R4[s][:, j, :], L4[s][:, j, :]) for j in range(nn)],
                              [R4B[s], L4B[s]], [pbuf[pk], pbuf[pk + 1]])
                            A(act(Gst[:, :, o_:o_ + nn], ps[:, pk * 512:pk * 512 + nn * 128].rearrange("p (n i) -> p i n", n=nn), AF.Copy),
                              [pbuf[pk], pbuf[pk + 1]], [GstB])
                    base = nld

                    def ldU(c, base=base):
                        sl_ = (base + c) % NSL
                        S.dma("sync", utc[sl_][:].rearrange("p k e -> p (k e)"), uTb[c], reads=[precB], writes=[utB[sl_]])

                    def ldV(c, base=base):
                        sl_ = (base + c) % NSL
                        S.dma("sync", vc[sl_][:], vbd[c * 128:(c + 1) * 128, :], reads=[precB], writes=[vcB[sl_]])

                    atb = [6, 7] if len(tl_) > 2 else [4, 5, 6, 7]
                    LA = len(atb) - 1

                    def at(c, g0=g0, gn=gn, base=base, gt_=gt_, atb=atb):
                        sl_ = (base + c) % NSL
                        pk = atb[c % len(atb)]
                        T([mm(bank(pk, gn), utc[sl_][:, k, :], xn2T[:, k, g0:g0 + gn], start=(k == 0), stop=(k == 7)) for k in range(8)],
                          [utB[sl_]] + [xnB_own[t] for t in gt_], [pbuf[pk]])

                    def chain(c, gn=gn, atb=atb):
                        s = c % NCH
                        pk = atb[c % len(atb)]
                        A(act(sq_[s][:, 0:gn], bank(pk, gn), AF.Square), [pbuf[pk]], [sqB_[s]])
                        V(ts(in_[s][:, 0:gn], sq_[s][:, 0:gn], 0.044715, 1.0, op0=ALU.mult, op1=ALU.add), [sqB_[s]], [inB_[s]])
                        V(tt(in_[s][:, 0:gn], in_[s][:, 0:gn], bank(pk, gn), ALU.mult), [inB_[s], pbuf[pk]], [inB_[s]])
                        A(act(sg_[s][:, 0:gn], in_[s][:, 0:gn], AF.Sigmoid, scale=1.5957691216057308), [inB_[s]], [sgB_[s]])
                        V(tt(w1_[s][:, 0:gn], sg_[s][:, 0:gn], bank(pk, gn), ALU.mult), [sgB_[s], pbuf[pk]], [w1B_[s]])
                        G(tt(wt_[s][:, 0:gn], w1_[s][:, 0:gn], Gst[:, c, 0:gn], ALU.mult), [w1B_[s], GstB], [wtB_[s]])

                    def outmm(c, base=base, tl_=tl_):
                        sl_ = (base + c) % NSL
                        s = c % NCH
                        fns = []
                        for j, (t, rows, c0) in enumerate(tl_):
                            for hf in range(2):
                                fns.append(mm(bank(2 * j + hf)[0:rows, :], wt_[s][:, j * 128:j * 128 + rows], vc[sl_][:, hf * 512:(hf + 1) * 512],
                                              start=(c == 0), stop=(c == 127)))
                        T(fns, [wtB_[s], vcB[sl_]], [pbuf[b_] for b_ in range(2 * len(tl_))])

                    nU = 0
                    nV = 0
                    while nU < NSL:
                        ldU(nU)
                        nU += 1
                    while nV < NSL:
                        ldV(nV)
                        nV += 1
                    for c in range(LA):
                        at(c)
                    for c in range(128):
                        while nU < 128 and nU - NSL <= c + LA - 1:
                            ldU(nU)
                            nU += 1
                        if c + LA < 128:
                            at(c + LA)
                        chain(c)
                        outmm(c)
                        while nV < 128 and nV - NSL <= c:
                            ldV(nV)
                            nV += 1
                    assert nU == 128 and nV == 128
                    nld += 128
                    for j, (t, rows, c0) in enumerate(tl_):
                        s = nfin % 2
                        nfin += 1
                        S.dma("sync", xl_[s][0:rows, :], x1d[c0:c0 + rows, :], reads=[x1B[t]], writes=[xlB[s]])
                        for hf in range(2):
                            V(tt(yo_[s][0:rows, hf * 512:(hf + 1) * 512], bank(2 * j + hf)[0:rows, :], xl_[s][0:rows, hf * 512:(hf + 1) * 512], ALU.add),
                              [pbuf[2 * j + hf], xlB[s]], [yoB[s]])
                        dst = y_smp[0:rows, :] if t == NO else y_own[c0:c0 + rows, :]
                        S.dma("sync", dst, yo_[s][0:rows, :], reads=[yoB[s]])
            S.barrier()

        S.finish()
        with nc.Block() as block:
            S.emit(block)
    return nc


def alibi_slopes():
    return [2.0 ** (-8.0 * (h + 1) / 8) for h in range(8)]


def make_consts(pv, NP):
    cst = np.zeros((128, 2048), np.float32)
    cst[:, 0:128] = np.eye(128, dtype=np.float32)
    kk = np.arange(128)[:, None]
    qq = np.arange(128)[None, :]
    cst[:, 128:256] = np.where(qq >= kk, 0.0, NEG)
    sl = alibi_slopes()
    p = np.arange(128)
    for h in range(8):
        for dt in range(-3, 13):
            cst[:, 256 + h * 16 + dt + 3] = sl[h] * (p - 128.0 * dt)
        for dt in range(1, 29):
            cst[:, 384 + h * 28 + dt - 1] = sl[h] * (p - 128.0 * dt) + (0.0 if pv else NEG)
    cst[:, 1024:1152] = (qq >= kk).astype(np.float32)
    r_ = np.arange(128)
    cst[:, 1152:1280] = ((r_[:, None] % 4 == r_[None, :] % 4) & (r_[:, None] // 4 < r_[None, :] // 4)).astype(np.float32)
    cst[:, 1280:1296] = np.arange(16, dtype=np.float32)[None, :]
    cst[:, 1408:1536] = np.arange(128, dtype=np.float32)[None, :]
    qrow = np.zeros((2, 8 * 512), np.float32)
    r = np.arange(512)
    for h in range(8):
        qrow[0, h * 512:(h + 1) * 512] = -sl[h] * 128.0 * (r // 128)
        qrow[1, h * 512:(h + 1) * 512] = -sl[h] * (r % 128)
    return cst, qrow


def make_dcst():
    d = np.zeros((128, 1024), np.float32)
    sl = alibi_slopes()
    p = np.arange(128)
    for pg in range(64):
        for h in range(8):
            d[:, pg * 8 + h] = -sl[h] * (8192.0 - (pg * 128.0 + p))
    d[:, 64 * 8:65 * 8] = NEG
    d[0, 64 * 8:65 * 8] = 0.0
    d[:, 520] = p
    for si in range(4):
        d[0:2, 528 + si * 4 + si] = 1.0
        d[:, 560 + si * 4 + si] = 1.0
    return d


def make_gcol(inp, pv, NP, NO):
    bi = np.asarray(inp["b_i"], np.float32).reshape(-1)
    bf = np.asarray(inp["b_f"], np.float32).reshape(-1)
    g = np.zeros((128, 8), np.float32)
    g[:, 0] = np.tile(bi, 32)
    g[:, 1] = np.tile(bf, 32)
    rows_pre = np.arange(128) < NP * 4
    g[:, 2] = np.where(rows_pre, 1.0 if pv else 0.0, 1.0)
    g[:, 3] = np.where(rows_pre, 0.0 if pv else NEG, 0.0)
    g[:, 4] = -g[:, 2]
    return g


def make_prow(inp):
    g = lambda k: np.asarray(inp[k], np.float32).reshape(-1)
    qg, kg = g("q_norm_g"), g("k_norm_g")
    return np.concatenate([
        g("norm1_g"), g("norm2_g"), qg, qg, kg, kg, g("diff_subln_g"), g("mlstm_norm_g"),
        g("lam_q1"), g("lam_k1"), g("lam_q2"), g("lam_k2")]).reshape(1, -1).astype(np.float32)


def make_peer_inputs(inp):
    U = np.asarray(inp["peer_u"], np.float32).reshape(16384, 1024)
    uTh = np.ascontiguousarray(U.reshape(128, 128, 8, 128).transpose(0, 3, 2, 1)).reshape(128, 128, 1024)
    sk = np.asarray(inp["peer_subkeys"], np.float32).reshape(2, 128, 128)
    skT = np.ascontiguousarray(sk.transpose(2, 0, 1)).reshape(128, 256)
    return {"wq": np.asarray(inp["peer_wq"], np.float32).reshape(1024, 2048), "skT": skT, "uTh": uTh,
            "pv": np.asarray(inp["peer_v"], np.float32).reshape(16384, 1024)}


_NC_CACHE = {}


def kernel(**inputs):
    inp = {k: np.asarray(v) for k, v in inputs.items()}
    cfg = Cfg(NP=16, NO=16, NS=4, npool=int(inp["cache_k"].shape[1]))
    cfg.sample = True
    key = "full"
    if key not in _NC_CACHE:
        _NC_CACHE[key] = build_program(cfg)
    nc = _NC_CACHE[key]
    xp = inp["x_prompt"].astype(np.float32, copy=False)
    xs = inp["x_sample"].astype(np.float32, copy=False)
    prow = make_prow(inp)
    peer_in = make_peer_inputs(inp)
    shared = {"w_in": inp["w_in"][0], "prow": prow, "w_a": inp["w_branch_a"][0], "w_b": inp["w_branch_b"][0],
              "w_o": inp["w_out"][0], **peer_in}
    npool = cfg.npool
    ck2 = inp["cache_k"].reshape(npool * 128, 1024)
    cv2 = inp["cache_v"].reshape(npool * 128, 1024)
    dcst_ = make_dcst()
    brow_ = np.concatenate([inp["b_i"].reshape(-1), inp["b_f"].reshape(-1)]).reshape(1, 8).astype(np.float32)
    ptab_all = inp["page_table"].astype(np.int32, copy=False)
    sC_all = inp["state_C"][0].reshape(32 * 4, 128, 256)
    sn_all = inp["state_n"][0].reshape(32, 512)
    sm_all = inp["state_m"][0].reshape(32, 4)
    zeros_pre = np.zeros((2048, 1024), np.float32)
    maps = []
    for c in range(8):
        b, j = c // 2, c % 2
        cst, qrow = make_consts(pv=(j == 1), NP=16)
        m = dict(shared)
        m.update({"x_pre": zeros_pre if j == 0 else xp[b, 0:2048], "x_own": xp[b, j * 2048:(j + 1) * 2048],
                  "x_smp": xs[4 * c:4 * c + 4, 0], "cst": cst, "qrow": qrow, "gcol": make_gcol(inp, j == 1, 16, 16),
                  "cache_k": ck2, "cache_v": cv2, "ptab": ptab_all[4 * c:4 * c + 4], "dcst": dcst_, "brow": brow_,
                  "sC": sC_all[16 * c:16 * c + 16], "sn": sn_all[4 * c:4 * c + 4], "smm": sm_all[4 * c:4 * c + 4]})
        maps.append(m)
    res = run_bass_kernel_spmd(nc, maps, core_ids=list(range(8)))
    R_ = res.results
    y_prompt = np.zeros((4, 4096, 1024), np.float32)
    k_prompt = np.zeros((1, 4, 4096, 8, 128), np.float32)
    v_prompt = np.zeros((1, 4, 4096, 8, 128), np.float32)
    C_prompt = np.zeros((1, 4, 4, 128, 256), np.float32)
    n_prompt = np.zeros((1, 4, 4, 128), np.float32)
    m_prompt = np.zeros((1, 4, 4), np.float32)
    y_sample = np.zeros((32, 1, 1024), np.float32)
    k_sample = np.zeros((1, 32, 1, 8, 128), np.float32)
    v_sample = np.zeros((1, 32, 1, 8, 128), np.float32)
    C_sample = np.zeros((1, 32, 4, 128, 256), np.float32)
    n_sample = np.zeros((1, 32, 4, 128), np.float32)
    m_sample = np.zeros((1, 32, 4), np.float32)
    for c in range(8):
        b, j = c // 2, c % 2
        r = R_[c]
        sl = slice(j * 2048, (j + 1) * 2048)
        y_prompt[b, sl] = r["y_own"]
        k_prompt[0, b, sl] = r["k_own"].reshape(2048, 8, 128)
        v_prompt[0, b, sl] = r["v_own"].reshape(2048, 8, 128)
        y_sample[4 * c:4 * c + 4, 0] = r["y_smp"]
        k_sample[0, 4 * c:4 * c + 4, 0] = r["k_smp"].reshape(4, 8, 128)
        v_sample[0, 4 * c:4 * c + 4, 0] = r["v_smp"].reshape(4, 8, 128)
        C_sample[0, 4 * c:4 * c + 4] = r["C_smp"].reshape(4, 4, 128, 256)
        n_sample[0, 4 * c:4 * c + 4] = r["n_smp"].reshape(4, 4, 128)
        m_sample[0, 4 * c:4 * c + 4] = r["m_smp"]
        if j == 1:
            C_prompt[0, b] = r["C_out"]
            n_prompt[0, b] = r["n_out"]
            m_prompt[0, b] = r["m_out"].reshape(4)
    return (y_prompt, y_sample, k_prompt, v_prompt, C_prompt, n_prompt, m_prompt,
            k_sample, v_sample, C_sample, n_sample, m_sample)
```
